# Optimizing a Trainium2 kernel written in Bass

```python
import jax, jax.numpy as jnp
from jax import lax
import numpy as np

D_MODEL = 1024
BATCH = 8
SEQ = 4096
DEPTH = 4

CHUNK = 64
EPS = 1e-6
GLA_HEADS = 4
GLA_DK = (D_MODEL // 2) // GLA_HEADS
GLA_DV = D_MODEL // GLA_HEADS
GLA_KW = GLA_HEADS * GLA_DK
GLA_VW = GLA_HEADS * GLA_DV
GLA_RANK = 16
GLA_TAU = 16.0
LRU_WIDTH = D_MODEL
LRU_BLOCKS = 16
LRU_BW = LRU_WIDTH // LRU_BLOCKS
LRU_CONV = 4
LRU_C = 8.0
FFN_DFF = 3 * D_MODEL
FFN_CONV = 3

IN_SPLITS = (GLA_KW, GLA_KW, GLA_VW, GLA_VW, GLA_RANK, LRU_WIDTH, LRU_WIDTH, D_MODEL, D_MODEL)
IN_WIDTH = int(sum(IN_SPLITS))
IN_POINTS = tuple(int(p) for p in np.cumsum(IN_SPLITS)[:-1])

kernel_name = "hybrid_gla_rglru_convffn_trunk"


def rmsnorm(x, g):
    xf = x.astype(jnp.float32)
    y = xf * lax.rsqrt(jnp.mean(xf * xf, axis=-1, keepdims=True) + EPS)
    return (y * g.astype(jnp.float32)).astype(x.dtype)


def causal_dwconv(x, w, b):
    width = w.shape[0]
    T = x.shape[1]
    xp = jnp.pad(x, ((0, 0), (width - 1, 0), (0, 0)))
    y = b
    for j in range(width):
        y = y + xp[:, j:j + T] * w[j]
    return y


def gla_chunked(q, k, v, log_alpha):
    B, T, H, DK = q.shape
    DV = v.shape[-1]
    nc = T // CHUNK

    def to_chunks(t):
        return jnp.moveaxis(t.astype(jnp.float32).reshape(B, nc, CHUNK, H, t.shape[-1]), 1, 0)

    qc, kc, vc, ac = to_chunks(q), to_chunks(k), to_chunks(v), to_chunks(log_alpha)

    def step(S, inp):
        q_, k_, v_, a_ = inp
        cum = jnp.cumsum(a_, axis=1)
        tot = cum[:, -1]
        k_dec = k_ * jnp.exp(tot[:, None] - cum)
        S = jnp.exp(tot)[..., None] * S + jnp.einsum('bchk,bchv->bhkv', k_dec, v_)
        o = jnp.einsum('bchk,bhkv->bchv', q_, S)
        return S, o

    S0 = jnp.zeros((B, H, DK, DV), jnp.float32)
    _, o = lax.scan(step, S0, (qc, kc, vc, ac))
    return jnp.moveaxis(o, 0, 1).reshape(B, T, H, DV)


def rg_lru(x, w_a, b_a, w_x, b_x, lam):
    B, T, W = x.shape
    xf = x.astype(jnp.float32)
    xb = xf.reshape(B, T, LRU_BLOCKS, LRU_BW)
    r = jax.nn.sigmoid(jnp.einsum('btni,nij->btnj', xb, w_a.astype(jnp.float32)).reshape(B, T, W) + b_a)
    i = jax.nn.sigmoid(jnp.einsum('btni,nij->btnj', xb, w_x.astype(jnp.float32)).reshape(B, T, W) + b_x)
    log_a = LRU_C * r * jax.nn.log_sigmoid(lam.astype(jnp.float32))
    a = jnp.exp(log_a)
    u = jnp.sqrt(-jnp.expm1(2.0 * log_a)) * (i * xf)

    def combine(e1, e2):
        a1, b1 = e1
        a2, b2 = e2
        return a1 * a2, a2 * b1 + b2

    _, h = lax.associative_scan(combine, (a, u), axis=1)
    return h.astype(x.dtype)


def heads(t, n):
    B, T, C = t.shape
    return t.reshape(B, T, n, C // n)


def setup_inputs(seed: int = 0) -> dict:
    key = jax.random.key(seed)
    ks = jax.random.split(key, 24)
    f32 = jnp.float32

    def nrm(k, shape, fan_in):
        return jax.random.normal(k, shape, f32) * (fan_in ** -0.5)

    def gain(k, shape):
        return 1.0 + 0.02 * jax.random.normal(k, shape, f32)

    def bias(k, shape):
        return 0.01 * jax.random.normal(k, shape, f32)

    a0 = jax.random.uniform(ks[14], (DEPTH, LRU_WIDTH), f32, minval=0.9, maxval=0.999)
    s = a0 ** (1.0 / LRU_C)
    lam = jnp.log(s) - jnp.log1p(-s)

    return {
        "x": jax.random.normal(ks[0], (BATCH, SEQ, D_MODEL), f32),
        "norm_mix": gain(ks[1], (DEPTH, D_MODEL)),
        "w_in": nrm(ks[2], (DEPTH, D_MODEL, IN_WIDTH), D_MODEL),
        "w_alpha": nrm(ks[3], (DEPTH, GLA_RANK, GLA_KW), GLA_RANK),
        "b_alpha": bias(ks[4], (DEPTH, GLA_KW)),
        "gla_norm": gain(ks[5], (DEPTH, GLA_DV)),
        "w_out_gla": nrm(ks[6], (DEPTH, GLA_VW, D_MODEL), GLA_VW),
        "lru_conv_w": nrm(ks[7], (DEPTH, LRU_CONV, LRU_WIDTH), LRU_CONV),
        "lru_conv_b": bias(ks[8], (DEPTH, LRU_WIDTH)),
        "lru_w_a": nrm(ks[9], (DEPTH, LRU_BLOCKS, LRU_BW, LRU_BW), LRU_BW),
        "lru_b_a": bias(ks[10], (DEPTH, LRU_WIDTH)),
        "lru_w_x": nrm(ks[11], (DEPTH, LRU_BLOCKS, LRU_BW, LRU_BW), LRU_BW),
        "lru_b_x": bias(ks[12], (DEPTH, LRU_WIDTH)),
        "lru_lambda": lam,
        "w_out_lru": nrm(ks[13], (DEPTH, LRU_WIDTH, D_MODEL), LRU_WIDTH),
        "w_o": nrm(ks[15], (DEPTH, D_MODEL, D_MODEL), D_MODEL),
        "norm_ffn": gain(ks[16], (DEPTH, D_MODEL)),
        "w_up": nrm(ks[17], (DEPTH, D_MODEL, 2 * FFN_DFF), D_MODEL),
        "ffn_conv_w": nrm(ks[18], (DEPTH, FFN_CONV, 2 * FFN_DFF), FFN_CONV),
        "ffn_conv_b": bias(ks[19], (DEPTH, 2 * FFN_DFF)),
        "w_down": nrm(ks[20], (DEPTH, FFN_DFF, D_MODEL), FFN_DFF),
        "norm_final": gain(ks[21], (D_MODEL,)),
    }


def reference(x, norm_mix, w_in, w_alpha, b_alpha, gla_norm, w_out_gla,
              lru_conv_w, lru_conv_b, lru_w_a, lru_b_a, lru_w_x, lru_b_x, lru_lambda,
              w_out_lru, w_o, norm_ffn, w_up, ffn_conv_w, ffn_conv_b, w_down, norm_final):
    B, T, _ = x.shape
    for l in range(DEPTH):
        h = rmsnorm(x, norm_mix[l])
        proj = h @ w_in[l]
        q, k, v, g_out, a_code, xr, gr, gate_a, gate_b = jnp.split(proj, IN_POINTS, axis=-1)

        log_alpha = jax.nn.log_sigmoid(a_code @ w_alpha[l] + b_alpha[l]) / GLA_TAU
        o = gla_chunked(heads(q * (GLA_DK ** -0.5), GLA_HEADS), heads(k, GLA_HEADS),
                        heads(v, GLA_HEADS), heads(log_alpha, GLA_HEADS))
        o = o * lax.rsqrt(jnp.mean(o * o, axis=-1, keepdims=True) + EPS) * gla_norm[l].astype(jnp.float32)
        o = o.reshape(B, T, GLA_VW).astype(x.dtype)
        y_a = (o * jax.nn.silu(g_out)) @ w_out_gla[l]

        xc = causal_dwconv(xr, lru_conv_w[l], lru_conv_b[l])
        hr = rg_lru(xc, lru_w_a[l], lru_b_a[l], lru_w_x[l], lru_b_x[l], lru_lambda[l])
        y_b = (hr * jax.nn.gelu(gr)) @ w_out_lru[l]

        merged = jax.nn.sigmoid(gate_a) * y_a + jax.nn.sigmoid(gate_b) * y_b
        x = x + merged @ w_o[l]

        h = rmsnorm(x, norm_ffn[l])
        u = causal_dwconv(h @ w_up[l], ffn_conv_w[l], ffn_conv_b[l])
        val, gate = jnp.split(u, 2, axis=-1)
        x = x + (jax.nn.gelu(gate) * val) @ w_down[l]
    return rmsnorm(x, norm_final)
```

```python
import contextlib
import os
import numpy as np
import concourse.bass as bass
import concourse.mybir as mybir
from concourse.bass_utils import run_bass_kernel_spmd

F32 = mybir.dt.float32
BF16 = mybir.dt.bfloat16
AF = mybir.ActivationFunctionType
ALU = mybir.AluOpType

D = 1024
KC = 8
TT = 512
NH = 4
EPS = 1e-6
NG = 39
GROUPS = ["k", "v0", "v1", "q", "g0", "g1", "xr0", "blk", "og0", "ga0", "gr0", "xr1", "og1", "ga1", "gr1",
          "ol0", "gb0", "ol1", "gb1", "wo0", "wo1"] + ["up%d" % j for j in range(12)] + ["dn%d" % j for j in range(6)]
assert len(GROUPS) == NG
NSLOT = 6


class Buf:
    __slots__ = ("name", "lw", "rd", "aliases", "rng", "excl")

    def __init__(self, name, rng=None, excl=False):
        self.excl = excl
        self.name = name
        self.lw = None
        self.rd = []
        self.aliases = []
        self.rng = rng


class Op:
    __slots__ = ("id", "eng", "fn", "deps", "dma", "needs_inc", "count", "sem", "val", "prev_same_sem", "cost", "tbl", "phase", "xfer", "rdeps", "vc")

    def __init__(self, id, eng, fn, deps, dma, cost=None, tbl=0):
        self.cost = cost
        self.tbl = tbl
        self.id = id
        self.eng = eng
        self.fn = fn
        self.deps = deps
        self.dma = dma
        self.needs_inc = False
        self.count = 0
        self.sem = None
        self.val = 0
        self.prev_same_sem = None


class _Rec:
    def __init__(self):
        self.func = None
        self.n = None
        self.meth = None

    def __getattr__(self, name):
        def f(*a, **k):
            self.meth = name
            if "func" in k:
                self.func = k["func"]
            o = k.get("out")
            if o is not None:
                try:
                    n = 1
                    for st_, cnt in list(o.ap)[1:]:
                        n *= cnt
                    self.n = n
                except Exception:
                    pass
            return None
        return f


def _tbl_of(func):
    if func in (AF.Exp, AF.Ln):
        return 1
    if func == AF.Sigmoid:
        return 2
    if func == AF.Gelu_apprx_tanh:
        return 3
    if func == AF.Silu:
        return 4
    return 0


class Sched:
    ENGS = ("pe", "act", "dve", "pool", "sp")
    NDMA_SEM = 12

    def __init__(self, nc):
        self.nc = nc
        self.ops = []
        self.by_eng = {e: [] for e in self.ENGS}

    DEFCOST = {"pe": 2.2, "act": 0.65, "dve": 0.78, "pool": 0.8, "sp": 4.0}

    def add(self, eng, fn, reads=(), writes=(), dma=False, cost=None, tbl=0, xfer=0.0):
        if eng in ("act", "dve", "pool") and not dma:
            rec = _Rec()
            fn(rec)
            if eng == "act":
                tbl = _tbl_of(rec.func)
            if cost is None and rec.n is not None:
                if eng == "act":
                    cost = 0.28 + rec.n / 1200.0
                elif eng == "dve":
                    cost = 0.25 + rec.n / 960.0 * (2.0 if rec.meth == "tensor_tensor_scan" else 1.0)
                else:
                    cost = 0.2 + rec.n / 480.0
        if cost is None:
            cost = self.DEFCOST[eng]
        deps = set()
        for b in reads:
            if b.lw is not None:
                deps.add(b.lw)
            if b.excl:
                deps.update(o for o in b.rd if self.ops[o].eng != eng)
            for a in b.aliases:
                if a.lw is not None:
                    deps.add(a.lw)
        rdeps = set(deps)
        for b in writes:
            if b.lw is not None:
                deps.add(b.lw)
            deps.update(b.rd)
            for a in b.aliases:
                if a.lw is not None:
                    deps.add(a.lw)
                deps.update(a.rd)
        op = Op(len(self.ops), eng, fn, deps, dma, cost, tbl)
        op.rdeps = rdeps
        op.phase = getattr(self, "cur_phase", "")
        op.xfer = xfer
        if dma:
            op.cost = 0.3 if eng == "sp" else 1.0
        self.ops.append(op)
        self.by_eng[eng].append(op)
        for b in writes:
            b.lw = op.id
            b.rd = []
            for a in b.aliases:
                a.lw = op.id
                a.rd = []
        for b in reads:
            if b.lw != op.id:
                b.rd.append(op.id)
        return op

    def schedule(self, window):
        ops = self.ops
        n = len(ops)
        fin = [None] * n
        pend = {e: [op.id for op in self.by_eng[e]] for e in self.ENGS}
        free = {e: 0.0 for e in self.ENGS}
        new = {e: [] for e in self.ENGS}
        win = {"pe": window, "act": window, "dve": window, "pool": 1, "sp": 1}
        cur_tbl = [0]
        LAT = 0.25
        dma_free = [0.0]
        remaining = n
        while remaining:
            best = None
            for e in self.ENGS:
                lst = pend[e]
                if not lst:
                    continue
                seen = set()
                cand = None
                cnt = 0
                for oid in lst:
                    if cnt >= win[e]:
                        break
                    cnt += 1
                    op = ops[oid]
                    ok = True
                    rt = 0.0
                    for d in op.deps:
                        f = fin[d]
                        if f is None:
                            ok = False
                            break
                        if ops[d].eng != e:
                            f += LAT
                        if f > rt:
                            rt = f
                    allowed = True
                    if e == "act" and op.tbl != 0:
                        if seen and seen != {op.tbl}:
                            allowed = False
                        seen.add(op.tbl)
                    if ok and allowed:
                        stt = max(free[e], rt)
                        if e == "act" and op.tbl != 0 and op.tbl != cur_tbl[0]:
                            stt += 2.7
                        if cand is None or stt < cand[0] - 1e-9:
                            cand = (stt, oid)
                        if stt <= free[e] + 1e-9:
                            break
                if cand is not None and (best is None or cand[0] < best[0]):
                    best = (cand[0], cand[1], e)
            assert best is not None, "schedule deadlock"
            stt, oid, e = best
            op = ops[oid]
            if op.dma:
                free[e] = stt + op.cost
                t0_ = max(stt + op.cost, dma_free[0])
                dma_free[0] = t0_ + op.xfer
                fin[oid] = dma_free[0] + 2.0
            else:
                fin[oid] = stt + op.cost
                free[e] = fin[oid]
            if e == "act" and op.tbl != 0:
                cur_tbl[0] = op.tbl
            pend[e].remove(oid)
            new[e].append(op)
            remaining -= 1
        self.by_eng = new
        self.sim_time = max(free.values())
        self.sim_busy = {e: sum(op.cost for op in new[e]) for e in self.ENGS}
        self.sim_fin = fin
        import collections
        gaps = collections.Counter()
        busy = collections.Counter()
        for e in ("pe",):
            prev = 0.0
            for op in new[e]:
                stt = fin[op.id] - op.cost
                if stt > prev + 1e-9:
                    gaps[op.phase] += stt - prev
                    if stt > self.sim_time / 2:
                        gaps["late:" + op.phase] += stt - prev
                busy[op.phase] += op.cost
                prev = fin[op.id]
        self.sim_gaps = gaps
        self.sim_pebusy = busy

    def emit(self, window=0):
        nc = self.nc
        ops = self.ops
        if window > 1:
            self.schedule(window)
        for op in ops:
            for d in op.deps:
                Dd = ops[d]
                if Dd.dma:
                    continue
                if Dd.eng == "pe" and op.eng == "pe" and not op.dma:
                    continue
                Dd.needs_inc = True
        for e in self.ENGS:
            c = 0
            for op in self.by_eng[e]:
                if op.dma:
                    continue
                if op.needs_inc:
                    c += 1
                op.count = c
        CE = ("pe", "act", "dve", "pool")
        for op in ops:
            vc = {e: 0 for e in CE}
            for d in op.deps:
                dv = ops[d].vc
                for e in CE:
                    if dv[e] > vc[e]:
                        vc[e] = dv[e]
            if not op.dma and op.needs_inc:
                vc[op.eng] = max(vc[op.eng], op.count)
            op.vc = vc
        st = contextlib.ExitStack()
        with st:
            esem = {e: st.enter_context(nc.semaphore("sem_" + e)) for e in ("pe", "act", "dve", "pool")}
            for q in ("sp", "pool", "act"):
                nd = sum(1 for op in self.by_eng[q] if op.dma)
                if nd == 0:
                    continue
                k = min(self.NDMA_SEM, nd)
                sems = [st.enter_context(nc.semaphore("dsem_%s_%d" % (q, i))) for i in range(k)]
                n = 0
                lastop = {}
                for op in self.by_eng[q]:
                    if not op.dma:
                        continue
                    op.sem = sems[n % k]
                    op.val = 16 * (n // k + 1)
                    op.prev_same_sem = lastop.get(n % k)
                    lastop[n % k] = op
                    n += 1
            block = st.enter_context(nc.Block())
            handles = {"pe": block.tensor, "act": block.scalar, "dve": block.vector, "pool": block.gpsimd,
                       "sp": block.sync}

            ATTACH = int(os.environ.get("MK_ATTACH", "1"))

            def make_body(e):
                def body(eng):
                    known = {k: 0 for k in CE}
                    dwaited = {}
                    nstand = [0, 0]
                    for op in self.by_eng[e]:
                        dneed = {}
                        eneed = {}
                        for d in op.deps:
                            Dd = ops[d]
                            if Dd.dma:
                                key = id(Dd.sem)
                                if dneed.get(key, (None, 0))[1] < Dd.val:
                                    dneed[key] = (Dd.sem, Dd.val)
                            else:
                                if Dd.eng == "pe" and e == "pe" and not op.dma:
                                    continue
                                ent = eneed.setdefault(Dd.eng, [0, None, 0])
                                if Dd.count > ent[0]:
                                    ent[0] = Dd.count
                                    ent[1] = Dd
                                if d in op.rdeps and Dd.count > ent[2]:
                                    ent[2] = Dd.count
                        if op.dma and op.prev_same_sem is not None:
                            Pp = op.prev_same_sem
                            key = id(Pp.sem)
                            if dneed.get(key, (None, 0))[1] < Pp.val:
                                dneed[key] = (Pp.sem, Pp.val)
                        for key, (sem, val) in dneed.items():
                            if dwaited.get(key, 0) < val:
                                eng.wait_ge(sem, val)
                                dwaited[key] = val
                        items = [(k, v) for k, v in eneed.items() if v[0] > known[k]]
                        pruned = []
                        for k, v in items:
                            implied = False
                            for k2, v2 in items:
                                if k2 != k and v2[1].vc[k] >= v[0]:
                                    implied = True
                                    break
                            if not implied:
                                pruned.append((k, v))
                        attach = None
                        stand = []
                        if ATTACH and (not op.dma):
                            if e in ("act", "dve", "pool"):
                                if pruned:
                                    attach = pruned.pop()
                                stand = pruned
                            elif e == "pe":
                                for k, v in pruned:
                                    if v[2] > known[k]:
                                        stand.append((k, v))
                                    elif attach is None:
                                        attach = (k, v)
                                    else:
                                        stand.append((k, v))
                            else:
                                stand = pruned
                        else:
                            stand = pruned
                        for k, v in stand:
                            eng.wait_ge(esem[k], v[0])
                            nstand[0] += 1
                        ins = op.fn(eng)
                        first = last = ins
                        if isinstance(ins, tuple):
                            first, last = ins
                        if attach is not None:
                            k, v = attach
                            first._wait_ge(esem[k], v[0])
                            nstand[1] += 1
                            stand = stand + [attach]
                        for k, v in stand:
                            dv = v[1].vc
                            for kk in CE:
                                if dv[kk] > known[kk]:
                                    known[kk] = dv[kk]
                            if v[0] > known[k]:
                                known[k] = v[0]
                        if op.dma:
                            last.then_inc(op.sem, 16)
                        elif op.needs_inc:
                            last.then_inc(esem[e], 1)
                    print("[mk] waits", e, "standalone", nstand[0], "attached", nstand[1])
                return body

            for e in self.ENGS:
                if self.by_eng[e]:
                    handles[e](make_body(e))


def vec_layout(L):
    off = {}
    n = 0
    for name, sz in [("g1", L * 8), ("g2", L * 8), ("gf", 8), ("gn", L * 2), ("lcw", L * 4 * 8), ("lcb", L * 8),
                     ("ba", L * 8), ("bx", L * 8), ("lam", L * 8), ("fcw", L * 3 * 48), ("fcb", L * 48)]:
        off[name] = n
        n += sz
    return off, n


def build(T, L):
    NT = T // TT
    nc = bass.Bass("TRN2", target_bir_lowering=False)
    VO, NV = vec_layout(L)
    xT = nc.dram_tensor("xT", [D, T], F32, kind="ExternalInput").ap()
    wbig = nc.dram_tensor("wbig", [L, NG, 128, 4096], F32, kind="ExternalInput").ap()
    wa_d = nc.dram_tensor("wa", [L, 128, 128], F32, kind="ExternalInput").ap()
    wal_d = nc.dram_tensor("wal", [L, 17, 512], F32, kind="ExternalInput").ap()
    vec_d = nc.dram_tensor("vec", [128, NV], F32, kind="ExternalInput").ap()
    cst_d = nc.dram_tensor("cst", [128, 130], F32, kind="ExternalInput").ap()
    yT = nc.dram_tensor("yT", [D, T], F32, kind="ExternalOutput").ap()
    wsc = nc.dram_tensor("wsc", [L, NG, 128, 4096], BF16).ap()
    xTv = xT.rearrange("(kc p) t -> p kc t", p=128)
    yTv = yT.rearrange("(kc p) t -> p kc t", p=128)

    S = Sched(nc)
    MMC = float(os.environ.get("MK_MMC", "0.225"))
    XF = 1048576 / 340e3
    cur = [16512]
    ranged = []

    def alloc(name, shape, dt, at=None):
        nbytes = int(np.prod(shape[1:])) * (4 if dt == F32 else 2)
        if at is None:
            o = cur[0]
            cur[0] += (nbytes + 31) // 32 * 32
        else:
            o = at
        t = nc.alloc_sbuf_tensor_at(name, list(shape), dt, offset=o)
        return t, (o, o + nbytes)

    vec, _ = alloc("vec", [128, NV], F32)
    c8, _ = alloc("c8", [128, L * 8], F32)
    c16, _ = alloc("c16", [128, L * 8], F32)
    etot, _ = alloc("etot", [128, 32], F32)
    hstate, _ = alloc("hstate", [128, L * 8], F32)
    Tl, _ = alloc("Tl", [128, L, 8, 3], F32)
    fixl, _ = alloc("fixl", [128, L, 8, 3], F32)
    Tf, _ = alloc("Tf", [128, L, 48, 2], F32)
    fixf, _ = alloc("fixf", [128, L, 48, 2], F32)
    ptmp, _ = alloc("ptmp", [128, 2, 48], F32)
    cst, _ = alloc("cst", [128, 130], F32)
    ones_bf, _ = alloc("ones_bf", [128, 128], BF16)
    wal, _ = alloc("wal", [128, L, 512], BF16)
    waS, _ = alloc("waS", [128, L, 128], BF16)
    a_aug, _ = alloc("a_aug", [128, 512], BF16)
    xs0, _ = alloc("xs0", [128, KC, TT], F32)
    xs1, _ = alloc("xs1", [128, KC, TT], F32)
    XS = [xs0, xs1]
    hb0, _ = alloc("hb0", [128, KC, TT], BF16)
    hb1, _ = alloc("hb1", [128, KC, TT], BF16)
    HBS = [hb0, hb1]
    ring, _ = alloc("ring", [128, NSLOT, 4096], BF16)
    Sst, _ = alloc("Sst", [128, L, NH, 256], F32)
    sqb, _ = alloc("sqb", [128, KC, TT], BF16)
    lnv, _ = alloc("lnv", [128, TT], F32)
    u1, rg_u1 = alloc("u1", [128, KC, TT], F32)
    r1 = cur[0]
    R1SZ = 49152
    cur[0] += R1SZ
    assert cur[0] <= 229344, cur[0]

    def ralloc(name, shape, dt, off):
        t, rng = alloc(name, shape, dt, at=r1 + off)
        assert rng[1] <= r1 + R1SZ, (name, rng)
        return t, rng

    ebuf, rg_ebuf = ralloc("ebuf", [128, TT], F32, 0)
    spbuf, rg_sp = ralloc("spbuf", [128, 2, TT], F32, 2048)
    dbuf, rg_dbuf = ralloc("dbuf", [128, TT], F32, 6144)
    kdec, rg_kdec = ralloc("kdec", [128, 4, 512], BF16, 8192)
    vbf, rg_vbf = ralloc("vbf", [128, 4, 1024], BF16, 12288)
    qTb, rg_qT = ralloc("qTb", [128, NH, TT], BF16, 20480)
    Sbf, rg_Sbf = ralloc("Sbf", [128, 2, NH, 256], BF16, 24576)
    osg, rg_osg = ralloc("osg", [128, KC, TT], BF16, 40960)
    tmpf, rg_tmpf = ralloc("tmpf", [128, TT], F32, 38912)
    acc, rg_acc = ralloc("acc", [128, 2, TT], F32, 0)
    xcb, rg_xcb = ralloc("xcb", [128, 2, TT], BF16, 4096)
    rbuf, rg_rbuf = ralloc("rbuf", [128, 4, TT], F32, 6144)
    t1b, rg_t1 = ralloc("t1b", [128, 4, TT], F32, 14336)
    abuf, rg_abuf = ralloc("abuf", [128, 4, TT], F32, 22528)
    hg, rg_hg = ralloc("hg", [128, KC, TT], BF16, 30720)
    sgb, rg_sgb = ralloc("sgb", [128, 2, TT], F32, 6144)
    m2b, rg_m2 = ralloc("m2b", [128, 2, TT], F32, 10240)
    mrg, rg_mrg = ralloc("mrg", [128, KC, TT], BF16, 14336)
    facc, rg_facc = ralloc("facc", [128, 3, 2, TT], F32, 28672)
    gv_lo, rg_gvlo = alloc("gv_lo", [128, 16, TT], BF16, at=rg_u1[0])
    gv_hi, rg_gvhi = ralloc("gv_hi", [128, 8, TT], BF16, 40960)

    def gvs(i):
        return gv_lo[:, i, :] if i < 16 else gv_hi[:, i - 16, :]

    ps = nc.alloc_psum_tensor("ps", [128, 8, 512], F32)

    def mk(name, rng):
        b = Buf(name, rng)
        ranged.append(b)
        return b

    def sub(rng, i, n):
        a, b = rng
        step = (b - a) // n
        return (a + i * step, a + (i + 1) * step)

    B_ps = [Buf("ps%d" % i, excl=True) for i in range(8)]
    B_kvh = [Buf("kv%d" % h) for h in range(4)]
    BXS = [[Buf("xs%d_%d" % (t_, i)) for i in range(KC)] for t_ in range(2)]
    BHB = [[Buf("hb%d_%d" % (t_, i)) for i in range(KC)] for t_ in range(2)]
    B_sqb = [Buf("sqb%d" % i) for i in range(KC)]
    B_lnv = Buf("lnv")
    B_u1 = [mk("u1_%d" % i, sub(rg_u1, i, 8)) for i in range(KC)]
    B_ring = [Buf("ring%d" % i) for i in range(NSLOT)]
    B_wsc = [[Buf("wsc%d_%d" % (l, g)) for g in range(NG)] for l in range(L)]
    B_S = [[Buf("S%d_%d" % (l, h)) for h in range(NH)] for l in range(L)]
    B_etot = Buf("etot")
    B_aaug = Buf("aaug")
    B_hst = [[Buf("hst%d_%d" % (l, c)) for c in range(8)] for l in range(L)]
    B_Tl = [Buf("Tl%d" % l) for l in range(L)]
    B_fixl = [Buf("fixl%d" % l) for l in range(L)]
    B_Tf = [Buf("Tf%d" % l) for l in range(L)]
    B_fixf = [Buf("fixf%d" % l) for l in range(L)]
    B_ptmp = Buf("ptmp")
    B_ebuf = mk("ebuf", rg_ebuf)
    B_sp = [mk("sp%d" % i, sub(rg_sp, i, 2)) for i in range(2)]
    B_dbuf = mk("dbuf", rg_dbuf)
    B_kdec = [mk("kdec%d" % i, sub(rg_kdec, i, 4)) for i in range(4)]
    B_vbf = [mk("vbf%d" % i, sub(rg_vbf, i, 4)) for i in range(4)]
    B_qT = [mk("qT%d" % i, sub(rg_qT, i, 4)) for i in range(4)]
    B_Sbf = [[mk("Sbf%d_%d" % (p_, h), sub(sub(rg_Sbf, p_, 2), h, 4)) for h in range(4)] for p_ in range(2)]
    B_osg = [mk("osg%d" % i, sub(rg_osg, i, 8)) for i in range(8)]
    B_tmpf = mk("tmpf", rg_tmpf)
    B_acc = [mk("acc%d" % i, sub(rg_acc, i, 2)) for i in range(2)]
    B_xcb = [mk("xcb%d" % i, sub(rg_xcb, i, 2)) for i in range(2)]
    B_rbuf = [mk("rbuf%d" % i, sub(rg_rbuf, i, 4)) for i in range(4)]
    B_t1 = [mk("t1_%d" % i, sub(rg_t1, i, 4)) for i in range(4)]
    B_abuf = [mk("abuf%d" % i, sub(rg_abuf, i, 4)) for i in range(4)]
    B_hg = [mk("hg%d" % i, sub(rg_hg, i, 8)) for i in range(8)]
    B_sgb = [mk("sgb%d" % i, sub(rg_sgb, i, 2)) for i in range(2)]
    B_m2 = [mk("m2_%d" % i, sub(rg_m2, i, 2)) for i in range(2)]
    B_mrg = [mk("mrg%d" % i, sub(rg_mrg, i, 8)) for i in range(8)]
    B_facc = [mk("facc%d" % i, sub(rg_facc, i, 3)) for i in range(3)]
    B_gv = [mk("gv%d" % i, sub(rg_gvlo, i, 16) if i < 16 else sub(rg_gvhi, i - 16, 8)) for i in range(24)]
    for i, a in enumerate(ranged):
        for b in ranged[i + 1:]:
            if a.rng[0] < b.rng[1] and b.rng[0] < a.rng[1]:
                a.aliases.append(b)
                b.aliases.append(a)


    class Banks:
        def __init__(self):
            self.pool = list(range(8))
            self.i = 0

        def get(self):
            b = self.pool[self.i % len(self.pool)]
            self.i += 1
            return b

        def set(self, lst):
            self.pool = list(lst)
            self.i = 0

    BK = Banks()

    def vcol(name, idx):
        o = VO[name] + idx
        return vec[:, o:o + 1]

    B_setup = []

    def sadd(eng, fn, dma=False, extra_w=(), reads=()):
        b = Buf("setup%d" % len(B_setup))
        B_setup.append(b)
        S.add(eng, fn, reads=reads, writes=[b] + list(extra_w), dma=dma)
        return b

    b_vec = sadd("sp", lambda e: e.dma_start(out=vec[:, :], in_=vec_d[:, :]), dma=True)
    sadd("sp", lambda e: e.dma_start(out=cst[:, :], in_=cst_d[:, :]), dma=True)
    sadd("pool", lambda e: e.memset(ones_bf[:, :], 1.0))
    sadd("pool", lambda e: e.memset(a_aug[:, :], 1.0), extra_w=[B_aaug])
    sadd("pool", lambda e: e.memset(Sst[:, :, :, :], 0.0), extra_w=[b for l in range(L) for b in B_S[l]])
    sadd("pool", lambda e: e.memset(hstate[:, :], 0.0), extra_w=[b for l in range(L) for b in B_hst[l]])
    sadd("pool", lambda e: e.memset(Tl[:, :, :, :], 0.0), extra_w=B_Tl)
    sadd("pool", lambda e: e.memset(Tf[:, :, :, :], 0.0), extra_w=B_Tf)
    for l in range(L):
        sadd("pool", lambda e, l=l: e.dma_start(out=wal[0:17, l, :], in_=wal_d[l, :, :]), dma=True)
        sadd("pool", lambda e, l=l: e.dma_start(out=waS[:, l, :], in_=wa_d[l, :, :]), dma=True)
    lamv = vec[:, VO["lam"]:VO["lam"] + L * 8]
    b_c8a = sadd("act", lambda e: e.activation(out=c8[:, :], in_=lamv, func=AF.Exp, scale=-1.0), reads=[b_vec])
    b_c8b = sadd("act", lambda e: e.activation(out=c8[:, :], in_=c8[:, :], func=AF.Ln, bias=1.0), reads=[b_c8a])
    b_c16 = sadd("act", lambda e: e.mul(out=c16[:, :], in_=c8[:, :], mul=-16.0), reads=[b_c8b])
    sadd("act", lambda e: e.mul(out=c8[:, :], in_=c8[:, :], mul=-8.0), reads=[b_c8b, b_c16])

    def pool_tt(out, in0, in1, op, reads, writes):
        S.add("pool", lambda e: e.tensor_tensor(out=out, in0=in0, in1=in1, op=op), reads=reads, writes=writes)

    def compute_fix_f(l, extra_reads=()):
        w0 = vec[:, VO["fcw"] + (l * 3 + 0) * 48: VO["fcw"] + (l * 3 + 0) * 48 + 48]
        w1 = vec[:, VO["fcw"] + (l * 3 + 1) * 48: VO["fcw"] + (l * 3 + 1) * 48 + 48]
        bb = vec[:, VO["fcb"] + l * 48: VO["fcb"] + l * 48 + 48]
        T0 = Tf[:, l, :, 0]
        T1 = Tf[:, l, :, 1]
        rd = [B_Tf[l]] + list(extra_reads)
        pool_tt(ptmp[:, 0, :], w0, T0, ALU.mult, rd, [B_ptmp])
        pool_tt(ptmp[:, 1, :], w1, T1, ALU.mult, rd + [B_ptmp], [B_ptmp])
        pool_tt(ptmp[:, 0, :], ptmp[:, 0, :], ptmp[:, 1, :], ALU.add, [B_ptmp], [B_ptmp])
        pool_tt(fixf[:, l, :, 0], ptmp[:, 0, :], bb, ALU.add, [B_ptmp], [B_fixf[l]])
        pool_tt(ptmp[:, 1, :], w0, T1, ALU.mult, rd + [B_ptmp], [B_ptmp])
        pool_tt(fixf[:, l, :, 1], ptmp[:, 1, :], bb, ALU.add, [B_ptmp, B_fixf[l]], [B_fixf[l]])

    def compute_fix_l(l, extra_reads=()):
        def w(j):
            o = VO["lcw"] + (l * 4 + j) * 8
            return vec[:, o:o + 8]
        bb = vec[:, VO["lcb"] + l * 8: VO["lcb"] + l * 8 + 8]
        T0, T1, T2 = Tl[:, l, :, 0], Tl[:, l, :, 1], Tl[:, l, :, 2]
        rd = [B_Tl[l]] + list(extra_reads)
        pa, pb = ptmp[:, 0, 0:8], ptmp[:, 1, 0:8]
        pool_tt(pa, w(0), T0, ALU.mult, rd + [B_ptmp], [B_ptmp])
        pool_tt(pb, w(1), T1, ALU.mult, rd + [B_ptmp], [B_ptmp])
        pool_tt(pa, pa, pb, ALU.add, [B_ptmp], [B_ptmp])
        pool_tt(pb, w(2), T2, ALU.mult, rd + [B_ptmp], [B_ptmp])
        pool_tt(pa, pa, pb, ALU.add, [B_ptmp], [B_ptmp])
        pool_tt(fixl[:, l, :, 0], pa, bb, ALU.add, [B_ptmp, B_fixl[l]], [B_fixl[l]])
        pool_tt(pa, w(0), T1, ALU.mult, rd + [B_ptmp], [B_ptmp])
        pool_tt(pb, w(1), T2, ALU.mult, rd + [B_ptmp], [B_ptmp])
        pool_tt(pa, pa, pb, ALU.add, [B_ptmp], [B_ptmp])
        pool_tt(fixl[:, l, :, 1], pa, bb, ALU.add, [B_ptmp, B_fixl[l]], [B_fixl[l]])
        pool_tt(pa, w(0), T2, ALU.mult, rd + [B_ptmp], [B_ptmp])
        pool_tt(fixl[:, l, :, 2], pa, bb, ALU.add, [B_ptmp, B_fixl[l]], [B_fixl[l]])

    for e in ("pe", "act", "dve", "pool", "sp"):
        S.add(e, lambda eng: eng.nop(), reads=list(B_setup))
    for l in range(L):
        compute_fix_f(l)
        compute_fix_l(l)

    def add_cast(l, g):
        def fn(e):
            return e.dma_start(out=wsc[l, g].rearrange("p (a b) -> p a b", b=2048),
                               in_=wbig[l, g].rearrange("p (a b) -> p a b", b=2048))
        S.add("pool", fn, writes=[B_wsc[l][g]], dma=True, xfer=3 * XF)

    for g in range(NG):
        add_cast(0, g)
    cur_tile = [0]
    CAST_TILE = 1 if (int(os.environ.get("MK_PAIR", "2")) >= 2 and NT >= 2) else 0

    wcount = [0]

    def wnext(l, name):
        g = GROUPS.index(name)
        n = wcount[0]
        assert GROUPS[n % NG] == name, (name, GROUPS[n % NG])
        wcount[0] += 1
        s = n % NSLOT
        S.add("sp", lambda e: e.dma_start(out=ring[:, s, :], in_=wsc[l, g]), reads=[B_wsc[l][g]], writes=[B_ring[s]],
              dma=True, xfer=XF)
        if cur_tile[0] == CAST_TILE and l + 1 < L:
            add_cast(l + 1, g)
        return ring[:, s, :].rearrange("p (k c) -> p k c", c=512), B_ring[s], s

    def mm_group(out_ap, pairs, reads, writes, cost=None):
        n = len(pairs)
        if cost is None:
            cost = MMC * n

        def fn(e):
            ins = first = None
            for i, (lt, rh) in enumerate(pairs):
                ins = e.matmul(out_ap, lt, rh, start=(i == 0), stop=(i == n - 1))
                if first is None:
                    first = ins
            return (first, ins)
        S.add("pe", fn, reads=reads, writes=writes, cost=cost)

    def mm_group_ks(out_ap, pairs, reads_each, common_reads, writes):
        n = len(pairs)
        for i, (lt, rh) in enumerate(pairs):
            S.add("pe", lambda e, i=i, lt=lt, rh=rh: e.matmul(out_ap, lt, rh, start=(i == 0), stop=(i == n - 1)),
                  reads=[reads_each[i]] + list(common_reads), writes=writes, cost=MMC)

    def rmsnorm_to_hb(gname, gidx, xs, B_xs):
        for kc in range(KC):
            S.add("act", lambda e, kc=kc: e.activation(out=sqb[:, kc, :], in_=xs[:, kc, :], func=AF.Square),
                  reads=[B_xs[kc]], writes=[B_sqb[kc]])
        bk = BK.get()
        if KSPLIT:
            mm_group_ks(ps[:, bk, :], [(ones_bf[:, :], sqb[:, kc, :]) for kc in range(KC)], list(B_sqb), [], [B_ps[bk]])
        else:
            mm_group(ps[:, bk, :], [(ones_bf[:, :], sqb[:, kc, :]) for kc in range(KC)], list(B_sqb), [B_ps[bk]])
        S.add("act", lambda e: e.activation(out=lnv[:, :], in_=ps[:, bk, :], func=AF.Ln, scale=1.0 / D, bias=EPS),
              reads=[B_ps[bk]], writes=[B_lnv])
        S.add("act", lambda e: e.activation(out=ps[:, bk, :], in_=lnv[:, :], func=AF.Exp, scale=-0.5),
              reads=[B_lnv], writes=[B_ps[bk]])
        return bk

    STOP = int(os.environ.get("MK_STOP", "99"))
    GVENG = os.environ.get("MK_GVENG", "pool")
    XCBENG = os.environ.get("MK_XCBENG", "pool")
    A2ENG = os.environ.get("MK_A2ENG", "pool")
    KSPLIT = int(os.environ.get("MK_KSPLIT", "1"))

    def layer(l, xs, B_xs, hb, B_hb):
        layer_body(l, xs, B_xs, hb, B_hb)
        wcount[0] = (wcount[0] + NG - 1) // NG * NG

    def layer_body(l, xs, B_xs, hb, B_hb):
        S.cur_phase = "norm1"
        BK.set(range(8))
        bk = rmsnorm_to_hb("g1", l, xs, B_xs)
        for kc in range(KC):
            S.add("dve", lambda e, kc=kc, bk=bk: e.scalar_tensor_tensor(
                out=hb[:, kc, :], in0=xs[:, kc, :], scalar=vcol("g1", l * 8 + kc), in1=ps[:, bk, :],
                op0=ALU.mult, op1=ALU.mult), reads=[B_xs[kc], B_ps[bk]], writes=[B_hb[kc]])
        if STOP <= 1:
            return
        S.cur_phase = "prologue"
        BK.set(range(7))
        TOTB = 7
        b0 = BK.get()
        if KSPLIT:
            mm_group_ks(ps[0:16, b0, :], [(waS[:, l, kc * 16:(kc + 1) * 16], hb[:, kc, :]) for kc in range(KC)],
                        list(B_hb), [], [B_ps[b0]])
        else:
            mm_group(ps[0:16, b0, :], [(waS[:, l, kc * 16:(kc + 1) * 16], hb[:, kc, :]) for kc in range(KC)],
                     list(B_hb), [B_ps[b0]])
        S.add("act", lambda e: e.copy(out=a_aug[0:16, :], in_=ps[0:16, b0, :]), reads=[B_ps[b0]], writes=[B_aaug])
        wk, Bwk, _ = wnext(l, "k")
        wv0, Bwv0, _ = wnext(l, "v0")
        wv1, Bwv1, _ = wnext(l, "v1")
        U = cst[:, 0:128]
        Cind = cst[:, 128:130]
        for b in range(4):
            tb = slice(b * 128, (b + 1) * 128)
            sp_i = b % 2
            bz = BK.get()
            mm_group(ps[:, bz, :], [(a_aug[0:17, tb], wal[0:17, l, :])], [B_aaug], [B_ps[bz]])
            S.add("act", lambda e, bz=bz: e.activation(out=ebuf[:, :], in_=ps[:, bz, :], func=AF.Exp, scale=-1.0),
                  reads=[B_ps[bz]], writes=[B_ebuf])
            S.add("act", lambda e, sp_i=sp_i: e.activation(out=spbuf[:, sp_i, :], in_=ebuf[:, :], func=AF.Ln, bias=1.0),
                  reads=[B_ebuf], writes=[B_sp[sp_i]])
            br = BK.get()
            mm_group(ps[:, br, :], [(U, spbuf[:, sp_i, :])], [B_sp[sp_i]], [B_ps[br]], cost=1.1)
            for h in range(NH):
                col = h * 8 + b * 2
                mm_group(ps[:, TOTB, col:col + 2], [(spbuf[:, sp_i, h * 128:(h + 1) * 128], Cind)], [B_sp[sp_i]],
                         [B_ps[TOTB]], cost=0.25)
            S.add("act", lambda e, br=br: e.activation(out=dbuf[:, :], in_=ps[:, br, :], func=AF.Exp, scale=-1.0 / 16),
                  reads=[B_ps[br]], writes=[B_dbuf])
            bkk = BK.get()
            if KSPLIT and b == 0:
                mm_group_ks(ps[:, bkk, :], [(hb[:, kc, tb], wk[:, kc, :]) for kc in range(KC)], list(B_hb), [Bwk],
                            [B_ps[bkk]])
            else:
                mm_group(ps[:, bkk, :], [(hb[:, kc, tb], wk[:, kc, :]) for kc in range(KC)], list(B_hb) + [Bwk],
                         [B_ps[bkk]])
            S.add("dve", lambda e, bkk=bkk, b=b: e.tensor_tensor(out=kdec[:, b, :], in0=ps[:, bkk, :], in1=dbuf[:, :],
                                                               op=ALU.mult),
                  reads=[B_ps[bkk], B_dbuf], writes=[B_kdec[b]])
            for half, (wv, Bwv) in enumerate(((wv0, Bwv0), (wv1, Bwv1))):
                bv = BK.get()
                if KSPLIT and b == 0:
                    mm_group_ks(ps[:, bv, :], [(hb[:, kc, tb], wv[:, kc, :]) for kc in range(KC)], list(B_hb), [Bwv],
                                [B_ps[bv]])
                else:
                    mm_group(ps[:, bv, :], [(hb[:, kc, tb], wv[:, kc, :]) for kc in range(KC)], list(B_hb) + [Bwv],
                             [B_ps[bv]])
                S.add("act", lambda e, bv=bv, b=b, half=half: e.copy(out=vbf[:, b, half * 512:(half + 1) * 512],
                                                                    in_=ps[:, bv, :]),
                      reads=[B_ps[bv]], writes=[B_vbf[b]])
        S.add("act", lambda e: e.activation(out=etot[:, :], in_=ps[:, TOTB, 0:32], func=AF.Exp, scale=-1.0 / 16),
              reads=[B_ps[TOTB]], writes=[B_etot])
        wq, Bwq, _ = wnext(l, "q")
        for h in range(NH):
            bq = BK.get()
            mm_group(ps[:, bq, :], [(wq[:, kc, h * 128:(h + 1) * 128], hb[:, kc, :]) for kc in range(KC)],
                     list(B_hb) + [Bwq], [B_ps[bq]])
            S.add("act", lambda e, bq=bq, h=h: e.mul(out=qTb[:, h, :], in_=ps[:, bq, :], mul=float(128 ** -0.5)),
                  reads=[B_ps[bq]], writes=[B_qT[h]])
        if STOP <= 2:
            return
        S.cur_phase = "recur"
        BK.set([4, 5, 6, 7])
        for half in range(2):
            for cl in range(4):
                c = half * 4 + cl
                b, cc = c // 2, c % 2
                par = c % 2
                prt = slice(cc * 64, (cc + 1) * 64)
                for h in range(NH):
                    kvb, kvc = 4 + h, 0
                    mm_group(ps[:, kvb, kvc:kvc + 256],
                             [(kdec[prt, b, h * 128:(h + 1) * 128], vbf[prt, b, h * 256:(h + 1) * 256])],
                             [B_kdec[b], B_vbf[b]], [B_ps[kvb]], cost=0.2)
                for h in range(NH):
                    kvb, kvc = 4 + h, 0
                    S.add("dve", lambda e, h=h, c=c, kvb=kvb, kvc=kvc: e.scalar_tensor_tensor(
                        out=Sst[:, l, h, :], in0=Sst[:, l, h, :], scalar=etot[:, h * 8 + c:h * 8 + c + 1],
                        in1=ps[:, kvb, kvc:kvc + 256], op0=ALU.mult, op1=ALU.add),
                        reads=[B_S[l][h], B_etot, B_ps[kvb]], writes=[B_S[l][h]])
                    S.add("act", lambda e, h=h, par=par: e.copy(out=Sbf[:, par, h, :], in_=Sst[:, l, h, :]),
                          reads=[B_S[l][h]], writes=[B_Sbf[par][h]])
                for h in range(NH):
                    def fn(e, h=h, cl=cl, c=c, par=par):
                        ins = first = None
                        for j in range(2):
                            ins = e.matmul(ps[:, h, j * 256 + cl * 64: j * 256 + (cl + 1) * 64],
                                           Sbf[:, par, h, j * 128:(j + 1) * 128], qTb[:, h, c * 64:(c + 1) * 64],
                                           start=True, stop=True)
                            if first is None:
                                first = ins
                        return (first, ins)
                    S.add("pe", fn, reads=[B_Sbf[par][h], B_qT[h]], writes=[B_ps[h]], cost=0.2)
            tsl = slice(half * 256, (half + 1) * 256)
            for h in range(NH):
                for j in range(2):
                    S.add("act", lambda e, h=h, j=j, tsl=tsl: e.activation(out=sqb[:, 2 * h + j, tsl],
                                                                in_=ps[:, h, j * 256:(j + 1) * 256], func=AF.Square),
                          reads=[B_ps[h]], writes=[B_sqb[2 * h + j]])
                    S.add("dve", lambda e, h=h, j=j, tsl=tsl: e.tensor_scalar(
                        out=u1[:, 2 * h + j, tsl], in0=ps[:, h, j * 256:(j + 1) * 256],
                        scalar1=vcol("gn", l * 2 + j), scalar2=None, op0=ALU.mult),
                        reads=[B_ps[h]], writes=[B_u1[2 * h + j]])
            for h in range(NH):
                bo = BK.get()
                mm_group(ps[:, bo, 0:256], [(ones_bf[:, :], sqb[:, 2 * h + j, tsl]) for j in range(2)],
                         [B_sqb[2 * h], B_sqb[2 * h + 1]], [B_ps[bo]])
                S.add("act", lambda e, bo=bo: e.activation(out=lnv[:, 0:256], in_=ps[:, bo, 0:256], func=AF.Ln,
                                                          scale=1.0 / 256, bias=EPS),
                      reads=[B_ps[bo]], writes=[B_lnv])
                S.add("act", lambda e, bo=bo: e.activation(out=ps[:, bo, 0:256], in_=lnv[:, 0:256], func=AF.Exp,
                                                          scale=-0.5),
                      reads=[B_lnv], writes=[B_ps[bo]])
                for j in range(2):
                    S.add("dve", lambda e, h=h, j=j, bo=bo, tsl=tsl: e.tensor_tensor(
                        out=u1[:, 2 * h + j, tsl], in0=ps[:, bo, 0:256], in1=u1[:, 2 * h + j, tsl], op=ALU.mult),
                        reads=[B_ps[bo], B_u1[2 * h + j]], writes=[B_u1[2 * h + j]])
        if STOP <= 3:
            return
        S.cur_phase = "gout"
        BK.set(range(8))
        for gi in range(2):
            wg, Bwg, _ = wnext(l, "g%d" % gi)
            for mi in range(4):
                m = gi * 4 + mi
                bg = BK.get()
                mm_group(ps[:, bg, :], [(wg[:, kc, mi * 128:(mi + 1) * 128], hb[:, kc, :]) for kc in range(KC)],
                         list(B_hb) + [Bwg], [B_ps[bg]])
                S.add("act", lambda e, bg=bg: e.activation(out=ps[:, bg, :], in_=ps[:, bg, :], func=AF.Silu),
                      reads=[B_ps[bg]], writes=[B_ps[bg]])
                S.add("dve", lambda e, bg=bg, m=m: e.tensor_tensor(out=osg[:, m, :], in0=ps[:, bg, :], in1=u1[:, m, :],
                                                                 op=ALU.mult),
                      reads=[B_ps[bg], B_u1[m]], writes=[B_osg[m]])
        if STOP <= 4:
            return
        def ya_group(gi):
            S.cur_phase = "ya"
            wog, Bwog, _ = wnext(l, "og%d" % gi)
            wga, Bwga, _ = wnext(l, "ga%d" % gi)
            for mi in range(4):
                m = gi * 4 + mi
                bga = BK.get()
                mm_group(ps[:, bga, :], [(wga[:, kc, mi * 128:(mi + 1) * 128], hb[:, kc, :]) for kc in range(KC)],
                         list(B_hb) + [Bwga], [B_ps[bga]])
                S.add("act", lambda e, bga=bga: e.activation(out=tmpf[:, :], in_=ps[:, bga, :], func=AF.Sigmoid),
                      reads=[B_ps[bga]], writes=[B_tmpf])
                bya = BK.get()
                mm_group(ps[:, bya, :], [(wog[:, kc, mi * 128:(mi + 1) * 128], osg[:, kc, :]) for kc in range(KC)],
                         list(B_osg) + [Bwog], [B_ps[bya]])
                S.add("dve", lambda e, bya=bya, m=m: e.tensor_tensor(out=u1[:, m, :], in0=ps[:, bya, :], in1=tmpf[:, :],
                                                                   op=ALU.mult),
                      reads=[B_ps[bya], B_tmpf], writes=[B_u1[m]])
        if STOP <= 5:
            return
        S.cur_phase = "lru"
        wblk = None
        for hbi in range(2):
            wxr, Bwxr, _ = wnext(l, "xr%d" % hbi)
            if hbi == 0:
                wblk_raw, Bwblk, sblk = wnext(l, "blk")
                wblk = ring[:, sblk, 0:2048].rearrange("p (k c) -> p k c", c=128)
            for cl in range(4):
                c = hbi * 4 + cl
                ai = cl % 2
                bx = BK.get()
                mm_group(ps[:, bx, :], [(wxr[:, kc, cl * 128:(cl + 1) * 128], hb[:, kc, :]) for kc in range(KC)],
                         list(B_hb) + [Bwxr], [B_ps[bx]])
                wl = lambda j, c=c: vcol("lcw", (l * 4 + j) * 8 + c)
                S.add("act", lambda e, bx=bx, ai=ai, c=c, wl=wl: e.activation(
                    out=acc[:, ai, 3:512], in_=ps[:, bx, 0:509], func=AF.Identity, scale=wl(0),
                    bias=vcol("lcb", l * 8 + c)), reads=[B_ps[bx]], writes=[B_acc[ai]])
                S.add("act", lambda e, ai=ai, c=c: e.copy(out=acc[:, ai, 0:3], in_=fixl[:, l, c, :]),
                      reads=[B_fixl[l], B_acc[ai]], writes=[B_acc[ai]])
                S.add("act", lambda e, bx=bx, c=c: e.copy(out=Tl[:, l, c, :], in_=ps[:, bx, 509:512]),
                      reads=[B_ps[bx]], writes=[B_Tl[l]])
                S.add("dve", lambda e, bx=bx, ai=ai, wl=wl: e.scalar_tensor_tensor(
                    out=acc[:, ai, 2:512], in0=ps[:, bx, 0:510], scalar=wl(1), in1=acc[:, ai, 2:512],
                    op0=ALU.mult, op1=ALU.add), reads=[B_ps[bx], B_acc[ai]], writes=[B_acc[ai]])
                S.add("dve", lambda e, bx=bx, ai=ai, wl=wl: e.scalar_tensor_tensor(
                    out=acc[:, ai, 1:512], in0=ps[:, bx, 0:511], scalar=wl(2), in1=acc[:, ai, 1:512],
                    op0=ALU.mult, op1=ALU.add), reads=[B_ps[bx], B_acc[ai]], writes=[B_acc[ai]])
                S.add("dve", lambda e, bx=bx, ai=ai, wl=wl: e.scalar_tensor_tensor(
                    out=acc[:, ai, :], in0=ps[:, bx, :], scalar=wl(3), in1=acc[:, ai, :],
                    op0=ALU.mult, op1=ALU.add), reads=[B_ps[bx], B_acc[ai]], writes=[B_acc[ai]])
                S.add(XCBENG, lambda e, ai=ai: (e.copy(out=xcb[:, ai, :], in_=acc[:, ai, :]) if XCBENG == "act" else
                                               e.tensor_copy(out=xcb[:, ai, :], in_=acc[:, ai, :])),
                      reads=[B_acc[ai]], writes=[B_xcb[ai]])
                bzr = BK.get()
                mm_group(ps[:, bzr, :], [(wblk[:, c, :], xcb[:, ai, :])], [B_xcb[ai], Bwblk], [B_ps[bzr]])
                bzi = BK.get()
                mm_group(ps[:, bzi, :], [(wblk[:, 8 + c, :], xcb[:, ai, :])], [B_xcb[ai], Bwblk], [B_ps[bzi]])
                S.add("act", lambda e, bzr=bzr, cl=cl, c=c: e.activation(
                    out=rbuf[:, cl, :], in_=ps[:, bzr, :], func=AF.Sigmoid, bias=vcol("ba", l * 8 + c)),
                    reads=[B_ps[bzr]], writes=[B_rbuf[cl]])
                S.add("act", lambda e, bzi=bzi, c=c: e.activation(
                    out=ps[:, bzi, :], in_=ps[:, bzi, :], func=AF.Sigmoid, bias=vcol("bx", l * 8 + c)),
                    reads=[B_ps[bzi]], writes=[B_ps[bzi]])
                S.add("dve", lambda e, bzi=bzi, cl=cl, ai=ai: e.tensor_tensor(
                    out=t1b[:, cl, :], in0=ps[:, bzi, :], in1=acc[:, ai, :], op=ALU.mult),
                    reads=[B_ps[bzi], B_acc[ai]], writes=[B_t1[cl]])
            if hbi == 1:
                compute_fix_l(l)
            ya_group(hbi)
            S.cur_phase = "lru"
            for cl in range(4):
                c = hbi * 4 + cl
                S.add("act", lambda e, cl=cl, c=c: e.activation(out=abuf[:, cl, :], in_=rbuf[:, cl, :], func=AF.Exp,
                                                              scale=c8[:, l * 8 + c:l * 8 + c + 1]),
                      reads=[B_rbuf[cl]], writes=[B_abuf[cl]])
                if A2ENG == "act":
                    S.add("act", lambda e, cl=cl, c=c: e.activation(out=rbuf[:, cl, :], in_=rbuf[:, cl, :], func=AF.Exp,
                                                                  scale=c16[:, l * 8 + c:l * 8 + c + 1]),
                          reads=[B_rbuf[cl], B_abuf[cl]], writes=[B_rbuf[cl]])
                else:
                    S.add(A2ENG, lambda e, cl=cl: e.tensor_tensor(out=rbuf[:, cl, :], in0=abuf[:, cl, :],
                                                                  in1=abuf[:, cl, :], op=ALU.mult),
                          reads=[B_rbuf[cl], B_abuf[cl]], writes=[B_rbuf[cl]])
                S.add("act", lambda e, cl=cl: e.activation(out=rbuf[:, cl, :], in_=rbuf[:, cl, :], func=AF.Ln,
                                                         scale=-1.0, bias=1.0),
                      reads=[B_rbuf[cl]], writes=[B_rbuf[cl]])
                bs = BK.get()
                S.add("act", lambda e, cl=cl, bs=bs: e.activation(out=ps[:, bs, :], in_=rbuf[:, cl, :], func=AF.Exp,
                                                                scale=0.5),
                      reads=[B_rbuf[cl]], writes=[B_ps[bs]])
                S.add("dve", lambda e, cl=cl, bs=bs: e.tensor_tensor(out=t1b[:, cl, :], in0=ps[:, bs, :],
                                                                   in1=t1b[:, cl, :], op=ALU.mult),
                      reads=[B_ps[bs], B_t1[cl]], writes=[B_t1[cl]])
                S.add("dve", lambda e, cl=cl, c=c: e.tensor_tensor_scan(
                    out=rbuf[:, cl, :], data0=abuf[:, cl, :], data1=t1b[:, cl, :],
                    initial=hstate[:, l * 8 + c:l * 8 + c + 1], op0=ALU.mult, op1=ALU.add),
                    reads=[B_abuf[cl], B_t1[cl], B_hst[l][c], B_rbuf[cl]], writes=[B_rbuf[cl]])
                S.add("dve", lambda e, cl=cl, c=c: e.tensor_copy(out=hstate[:, l * 8 + c:l * 8 + c + 1],
                                                               in_=rbuf[:, cl, 511:512]),
                      reads=[B_rbuf[cl]], writes=[B_hst[l][c]])
            wgr, Bwgr, _ = wnext(l, "gr%d" % hbi)
            for cl in range(4):
                c = hbi * 4 + cl
                bgr = BK.get()
                mm_group(ps[:, bgr, :], [(wgr[:, kc, cl * 128:(cl + 1) * 128], hb[:, kc, :]) for kc in range(KC)],
                         list(B_hb) + [Bwgr], [B_ps[bgr]])
                S.add("act", lambda e, bgr=bgr: e.activation(out=ps[:, bgr, :], in_=ps[:, bgr, :],
                                                            func=AF.Gelu_apprx_tanh),
                      reads=[B_ps[bgr]], writes=[B_ps[bgr]])
                S.add("dve", lambda e, bgr=bgr, cl=cl, c=c: e.tensor_tensor(out=hg[:, c, :], in0=ps[:, bgr, :],
                                                                          in1=rbuf[:, cl, :], op=ALU.mult),
                      reads=[B_ps[bgr], B_rbuf[cl]], writes=[B_hg[c]])
        if STOP <= 6:
            return
        S.cur_phase = "merge"
        for gi in range(2):
            wol, Bwol, _ = wnext(l, "ol%d" % gi)
            wgb, Bwgb, _ = wnext(l, "gb%d" % gi)
            for mi in range(4):
                m = gi * 4 + mi
                si = m % 2
                bgb = BK.get()
                mm_group(ps[:, bgb, :], [(wgb[:, kc, mi * 128:(mi + 1) * 128], hb[:, kc, :]) for kc in range(KC)],
                         list(B_hb) + [Bwgb], [B_ps[bgb]])
                S.add("act", lambda e, bgb=bgb, si=si: e.activation(out=sgb[:, si, :], in_=ps[:, bgb, :],
                                                                  func=AF.Sigmoid),
                      reads=[B_ps[bgb]], writes=[B_sgb[si]])
                byb = BK.get()
                mm_group(ps[:, byb, :], [(wol[:, kc, mi * 128:(mi + 1) * 128], hg[:, kc, :]) for kc in range(KC)],
                         list(B_hg) + [Bwol], [B_ps[byb]])
                S.add("dve", lambda e, byb=byb, si=si: e.tensor_tensor(out=m2b[:, si, :], in0=ps[:, byb, :],
                                                                     in1=sgb[:, si, :], op=ALU.mult),
                      reads=[B_ps[byb], B_sgb[si]], writes=[B_m2[si]])
                S.add("pool", lambda e, si=si, m=m: e.tensor_tensor(out=mrg[:, m, :], in0=u1[:, m, :], in1=m2b[:, si, :],
                                                                  op=ALU.add),
                      reads=[B_u1[m], B_m2[si]], writes=[B_mrg[m]])
        if STOP <= 7:
            return
        S.cur_phase = "wo"
        for gi in range(2):
            wwo, Bwwo, _ = wnext(l, "wo%d" % gi)
            for mi in range(4):
                m = gi * 4 + mi
                bo = BK.get()
                if KSPLIT and gi == 0:
                    mm_group_ks(ps[:, bo, :], [(wwo[:, kc, mi * 128:(mi + 1) * 128], mrg[:, kc, :]) for kc in range(KC)],
                                list(B_mrg), [Bwwo], [B_ps[bo]])
                else:
                    mm_group(ps[:, bo, :], [(wwo[:, kc, mi * 128:(mi + 1) * 128], mrg[:, kc, :]) for kc in range(KC)],
                             list(B_mrg) + [Bwwo], [B_ps[bo]])
                S.add("dve", lambda e, bo=bo, m=m: e.tensor_tensor(out=xs[:, m, :], in0=ps[:, bo, :], in1=xs[:, m, :],
                                                                 op=ALU.add),
                      reads=[B_ps[bo], B_xs[m]], writes=[B_xs[m]])
        if STOP <= 8:
            return
        S.cur_phase = "norm2"
        bk = rmsnorm_to_hb("g2", l, xs, B_xs)
        for kc in range(KC):
            S.add("dve", lambda e, kc=kc, bk=bk: e.scalar_tensor_tensor(
                out=hb[:, kc, :], in0=xs[:, kc, :], scalar=vcol("g2", l * 8 + kc), in1=ps[:, bk, :],
                op0=ALU.mult, op1=ALU.mult), reads=[B_xs[kc], B_ps[bk]], writes=[B_hb[kc]])
        if STOP <= 9:
            return
        S.cur_phase = "up"
        BK.set(range(8))
        pair_i = 0
        for j in range(12):
            wup, Bwup, _ = wnext(l, "up%d" % j)
            for pi in range(2):
                i = 2 * j + pi
                cv, cg = 2 * i, 2 * i + 1
                bv = (pair_i % 4) * 2
                bg = bv + 1
                fa = pair_i % 3
                pair_i += 1
                for which, bb_ in ((0, bv), (1, bg)):
                    col = (2 * pi + which) * 128
                    if KSPLIT and j == 0:
                        mm_group_ks(ps[:, bb_, :], [(wup[:, kc, col:col + 128], hb[:, kc, :]) for kc in range(KC)],
                                    list(B_hb), [Bwup], [B_ps[bb_]])
                    else:
                        mm_group(ps[:, bb_, :], [(wup[:, kc, col:col + 128], hb[:, kc, :]) for kc in range(KC)],
                                 list(B_hb) + [Bwup], [B_ps[bb_]])
                wf = lambda tap, cc_: vcol("fcw", (l * 3 + tap) * 48 + cc_)
                for which, bb_, cc_ in ((0, bv, cv), (1, bg, cg)):
                    S.add("act", lambda e, which=which, bb_=bb_, cc_=cc_, fa=fa: e.activation(
                        out=facc[:, fa, which, 2:512], in_=ps[:, bb_, 0:510], func=AF.Identity, scale=wf(0, cc_),
                        bias=vcol("fcb", l * 48 + cc_)), reads=[B_ps[bb_]], writes=[B_facc[fa]])
                S.add("act", lambda e, fa=fa, cv=cv: e.copy(out=facc[:, fa, :, 0:2], in_=fixf[:, l, cv:cv + 2, :]),
                      reads=[B_fixf[l], B_facc[fa]], writes=[B_facc[fa]])
                S.add("act", lambda e, bv=bv, cv=cv: e.copy(out=Tf[:, l, cv:cv + 2, :], in_=ps[:, bv:bv + 2, 510:512]),
                      reads=[B_ps[bv], B_ps[bg]], writes=[B_Tf[l]])
                for which, bb_, cc_ in ((0, bv, cv), (1, bg, cg)):
                    S.add("dve", lambda e, which=which, bb_=bb_, cc_=cc_, fa=fa: e.scalar_tensor_tensor(
                        out=facc[:, fa, which, 1:512], in0=ps[:, bb_, 0:511], scalar=wf(1, cc_),
                        in1=facc[:, fa, which, 1:512], op0=ALU.mult, op1=ALU.add),
                        reads=[B_ps[bb_], B_facc[fa]], writes=[B_facc[fa]])
                S.add("dve", lambda e, bv=bv, cv=cv, fa=fa: e.scalar_tensor_tensor(
                    out=facc[:, fa, 0, :], in0=ps[:, bv, :], scalar=wf(2, cv), in1=facc[:, fa, 0, :],
                    op0=ALU.mult, op1=ALU.add), reads=[B_ps[bv], B_facc[fa]], writes=[B_facc[fa]])
                S.add("dve", lambda e, bg=bg, cg=cg, fa=fa: e.scalar_tensor_tensor(
                    out=facc[:, fa, 1, :], in0=ps[:, bg, :], scalar=wf(2, cg), in1=facc[:, fa, 1, :],
                    op0=ALU.mult, op1=ALU.add), reads=[B_ps[bg], B_facc[fa]], writes=[B_facc[fa]])
                S.add("act", lambda e, fa=fa: e.activation(out=facc[:, fa, 1, :], in_=facc[:, fa, 1, :],
                                                          func=AF.Gelu_apprx_tanh),
                      reads=[B_facc[fa]], writes=[B_facc[fa]])
                S.add(GVENG, lambda e, fa=fa, i=i: e.tensor_tensor(out=gvs(i), in0=facc[:, fa, 0, :],
                                                                  in1=facc[:, fa, 1, :], op=ALU.mult),
                      reads=[B_facc[fa]], writes=[B_gv[i]])
        compute_fix_f(l)
        if STOP <= 10:
            return
        S.cur_phase = "down"
        for cb in range(2):
            for t in range(3):
                wdn, Bwdn, _ = wnext(l, "dn%d" % (cb * 3 + t))

                def fn(e, wdn=wdn, t=t, cb=cb):
                    ins = first = None
                    for mi in range(4):
                        for kc in range(KC):
                            ins = e.matmul(ps[:, cb * 4 + mi, :], wdn[:, kc, mi * 128:(mi + 1) * 128], gvs(t * 8 + kc),
                                           start=(t == 0 and kc == 0), stop=(t == 2 and kc == KC - 1))
                            if first is None:
                                first = ins
                    return (first, ins)
                S.add("pe", fn, reads=[B_gv[t * 8 + kc] for kc in range(KC)] + [Bwdn],
                      writes=[B_ps[cb * 4 + mi] for mi in range(4)], cost=7.9)
            for mi in range(4):
                m = cb * 4 + mi
                S.add("dve", lambda e, mi=mi, m=m, cb=cb: e.tensor_tensor(out=xs[:, m, :], in0=ps[:, cb * 4 + mi, :],
                                                                        in1=xs[:, m, :], op=ALU.add),
                      reads=[B_ps[cb * 4 + mi], B_xs[m]], writes=[B_xs[m]])

    B_out = Buf("out_dram")
    def load_x(i, xs, B_xs):
        t0 = i * TT
        S.add("sp", lambda e: e.dma_start(out=xs[:, :, :], in_=xTv[:, :, t0:t0 + TT]), writes=list(B_xs), dma=True,
              xfer=2 * XF)

    def finalize(i, xs, B_xs):
        t0 = i * TT
        S.cur_phase = "final"
        BK.set(range(8))
        bk = rmsnorm_to_hb("gf", 0, xs, B_xs)
        for kc in range(KC):
            S.add("dve", lambda e, kc=kc: e.scalar_tensor_tensor(
                out=u1[:, kc, :], in0=xs[:, kc, :], scalar=vcol("gf", kc), in1=ps[:, bk, :],
                op0=ALU.mult, op1=ALU.mult), reads=[B_xs[kc], B_ps[bk]], writes=[B_u1[kc]])
        S.add("sp", lambda e: e.dma_start(out=yTv[:, :, t0:t0 + TT], in_=u1[:, :, :]), reads=list(B_u1),
              writes=[B_out], dma=True, xfer=2 * XF)

    PAIR = int(os.environ.get("MK_PAIR", "2"))
    for ti in range(min(PAIR, NT)):
        load_x(ti, XS[ti], BXS[ti])
    for j0 in range(0, NT, PAIR):
        tiles = list(range(j0, min(NT, j0 + PAIR)))
        for l in range(L):
            for ti, i in enumerate(tiles):
                cur_tile[0] = i
                layer(l, XS[ti], BXS[ti], HBS[ti], BHB[ti])
                if l == L - 1:
                    finalize(i, XS[ti], BXS[ti])
                    if i + PAIR < NT:
                        load_x(i + PAIR, XS[ti], BXS[ti])
    S.add("sp", lambda e: e.nop(), reads=[B_out])
    S.emit(window=int(os.environ.get("MK_WINDOW", "96")))
    print("[mk] simulated schedule time (us):", getattr(S, "sim_time", None), "ops:", len(S.ops),
          "busy:", {k: round(v) for k, v in getattr(S, "sim_busy", {}).items()})
    if hasattr(S, "sim_gaps"):
        ntl = max(1, NT * L)
        print("[mk] PE idle-by-phase us/tile-layer:", {k: round(v / ntl, 1) for k, v in S.sim_gaps.items()})
        print("[mk] PE busy-by-phase us/tile-layer:", {k: round(v / ntl, 1) for k, v in S.sim_pebusy.items()})
    return nc


def prep_weights(inp, L):
    f = lambda a: np.asarray(a, dtype=np.float32)
    VO, NV = vec_layout(L)
    wbig = np.zeros((L, NG, 128, 4096), np.float32)
    wa = np.zeros((L, 128, 128), np.float32)
    wal = np.zeros((L, 17, 512), np.float32)
    vec = np.zeros((128, NV), np.float32)

    def grp(W, c0, ncols=512):
        blk = W[:, c0:c0 + ncols].reshape(8, 128, ncols).transpose(1, 0, 2)
        return blk.reshape(128, 8 * ncols)

    w_in = f(inp["w_in"])
    for l in range(L):
        Wl = w_in[l]
        oq, ok, ov, og, oa, oxr, ogr, oga, ogb = 0, 512, 1024, 2048, 3072, 3088, 4112, 5136, 6160
        G = {}
        G["k"] = grp(Wl, ok)
        G["v0"] = grp(Wl, ov); G["v1"] = grp(Wl, ov + 512)
        G["q"] = grp(Wl, oq)
        G["g0"] = grp(Wl, og); G["g1"] = grp(Wl, og + 512)
        G["ga0"] = grp(Wl, oga); G["ga1"] = grp(Wl, oga + 512)
        G["xr0"] = grp(Wl, oxr); G["xr1"] = grp(Wl, oxr + 512)
        G["gr0"] = grp(Wl, ogr); G["gr1"] = grp(Wl, ogr + 512)
        G["gb0"] = grp(Wl, ogb); G["gb1"] = grp(Wl, ogb + 512)
        for nm, key in (("og", "w_out_gla"), ("ol", "w_out_lru"), ("wo", "w_o")):
            W = f(inp[key])[l]
            G[nm + "0"] = grp(W, 0); G[nm + "1"] = grp(W, 512)
        blk = np.zeros((128, 16, 128), np.float32)
        for ti, key in enumerate(("lru_w_a", "lru_w_x")):
            W = f(inp[key])[l]
            for c in range(8):
                for s in range(2):
                    blk[s * 64:(s + 1) * 64, ti * 8 + c, s * 64:(s + 1) * 64] = W[2 * c + s]
        gb = np.zeros((128, 4096), np.float32)
        gb[:, 0:2048] = blk.reshape(128, 2048)
        G["blk"] = gb
        Wup = f(inp["w_up"])[l]
        for j in range(12):
            cols = np.concatenate([np.arange(c0, c0 + 128) for c0 in
                                   (2 * j * 128, 3072 + 2 * j * 128, (2 * j + 1) * 128, 3072 + (2 * j + 1) * 128)])
            G["up%d" % j] = grp(Wup[:, cols], 0)
        Wdn = f(inp["w_down"])[l]
        for cb in range(2):
            for t in range(3):
                G["dn%d" % (cb * 3 + t)] = grp(Wdn[t * 1024:(t + 1) * 1024, :], cb * 512)
        for gi, nm in enumerate(GROUPS):
            wbig[l, gi] = G[nm]
        wa[l] = Wl[:, oa:oa + 16].reshape(8, 128, 16).transpose(1, 0, 2).reshape(128, 128)
        wal[l, 0:16] = f(inp["w_alpha"])[l]
        wal[l, 16] = f(inp["b_alpha"])[l]

        def fm(v, n):
            return v.reshape(n, 128).T
        vec[:, VO["g1"] + l * 8: VO["g1"] + l * 8 + 8] = fm(f(inp["norm_mix"])[l], 8)
        vec[:, VO["g2"] + l * 8: VO["g2"] + l * 8 + 8] = fm(f(inp["norm_ffn"])[l], 8)
        vec[:, VO["gn"] + l * 2: VO["gn"] + l * 2 + 2] = fm(f(inp["gla_norm"])[l], 2)
        for j in range(4):
            o = VO["lcw"] + (l * 4 + j) * 8
            vec[:, o:o + 8] = fm(f(inp["lru_conv_w"])[l, j], 8)
        vec[:, VO["lcb"] + l * 8: VO["lcb"] + l * 8 + 8] = fm(f(inp["lru_conv_b"])[l], 8)
        vec[:, VO["ba"] + l * 8: VO["ba"] + l * 8 + 8] = fm(f(inp["lru_b_a"])[l], 8)
        vec[:, VO["bx"] + l * 8: VO["bx"] + l * 8 + 8] = fm(f(inp["lru_b_x"])[l], 8)
        vec[:, VO["lam"] + l * 8: VO["lam"] + l * 8 + 8] = fm(f(inp["lru_lambda"])[l], 8)
        perm = np.zeros(48, np.int64)
        for i in range(24):
            perm[2 * i] = i
            perm[2 * i + 1] = 24 + i
        for j in range(3):
            o = VO["fcw"] + (l * 3 + j) * 48
            vec[:, o:o + 48] = fm(f(inp["ffn_conv_w"])[l, j], 48)[:, perm]
        vec[:, VO["fcb"] + l * 48: VO["fcb"] + l * 48 + 48] = fm(f(inp["ffn_conv_b"])[l], 48)[:, perm]
    vec[:, VO["gf"]:VO["gf"] + 8] = f(inp["norm_final"]).reshape(8, 128).T
    cst = np.zeros((128, 130), np.float32)
    s = np.arange(128)
    cst[:, 0:128] = ((s[:, None] > s[None, :]) & (s[:, None] // 64 == s[None, :] // 64)).astype(np.float32)
    cst[:, 128] = (s // 64 == 0)
    cst[:, 129] = (s // 64 == 1)
    return dict(wbig=wbig, wa=wa, wal=wal, vec=vec, cst=cst)


_CACHE = {}


def run(inp, T, L, ncores, trace=False):
    key = (T, L)
    if key not in _CACHE:
        _CACHE[key] = build(T, L)
    nc = _CACHE[key]
    shared = prep_weights(inp, L)
    x = np.asarray(inp["x"], dtype=np.float32)
    in_maps = []
    for c in range(ncores):
        m = dict(shared)
        m["xT"] = np.ascontiguousarray(x[c, :T, :].T)
        in_maps.append(m)
    res = run_bass_kernel_spmd(nc, in_maps, core_ids=list(range(ncores)), trace=trace)
    out = np.stack([np.ascontiguousarray(r["yT"].T) for r in res.results], axis=0)
    return out, res


def kernel(**inputs):
    out, _ = run(inputs, 4096, 4, 8)
    return out.astype(np.float32)
```

```python
import contextlib
import os
import numpy as np
import concourse.bass as bass
import concourse.mybir as mybir
from concourse.bass_utils import run_bass_kernel_spmd

F32 = mybir.dt.float32
BF16 = mybir.dt.bfloat16
AF = mybir.ActivationFunctionType
ALU = mybir.AluOpType

D = 1024
KC = 8
TT = 512
NH = 4
EPS = 1e-6
NG = 39
GROUPS = ["k", "v0", "v1", "q", "g0", "g1", "xr0", "blk", "og0", "ga0", "gr0", "xr1", "og1", "ga1", "gr1",
          "ol0", "gb0", "ol1", "gb1", "wo0", "wo1"] + ["up%d" % j for j in range(12)] + ["dn%d" % j for j in range(6)]
assert len(GROUPS) == NG
NSLOT = 6


class Buf:
    __slots__ = ("name", "lw", "rd", "aliases", "rng", "excl")

    def __init__(self, name, rng=None, excl=False):
        self.excl = excl
        self.name = name
        self.lw = None
        self.rd = []
        self.aliases = []
        self.rng = rng


class Op:
    __slots__ = ("id", "eng", "fn", "deps", "dma", "needs_inc", "count", "sem", "val", "prev_same_sem", "cost", "tbl", "phase", "xfer", "rdeps", "vc")

    def __init__(self, id, eng, fn, deps, dma, cost=None, tbl=0):
        self.cost = cost
        self.tbl = tbl
        self.id = id
        self.eng = eng
        self.fn = fn
        self.deps = deps
        self.dma = dma
        self.needs_inc = False
        self.count = 0
        self.sem = None
        self.val = 0
        self.prev_same_sem = None


class _Rec:
    def __init__(self):
        self.func = None
        self.n = None
        self.meth = None

    def __getattr__(self, name):
        def f(*a, **k):
            self.meth = name
            if "func" in k:
                self.func = k["func"]
            o = k.get("out")
            if o is not None:
                try:
                    n = 1
                    for st_, cnt in list(o.ap)[1:]:
                        n *= cnt
                    self.n = n
                except Exception:
                    pass
            return None
        return f


def _tbl_of(func):
    if func in (AF.Exp, AF.Ln):
        return 1
    if func == AF.Sigmoid:
        return 2
    if func == AF.Gelu_apprx_tanh:
        return 3
    if func == AF.Silu:
        return 4
    return 0


class Sched:
    ENGS = ("pe", "act", "dve", "pool", "sp")
    NDMA_SEM = 12

    def __init__(self, nc):
        self.nc = nc
        self.ops = []
        self.by_eng = {e: [] for e in self.ENGS}

    DEFCOST = {"pe": 2.2, "act": 0.65, "dve": 0.78, "pool": 0.8, "sp": 4.0}

    def add(self, eng, fn, reads=(), writes=(), dma=False, cost=None, tbl=0, xfer=0.0):
        if eng in ("act", "dve", "pool") and not dma:
            rec = _Rec()
            fn(rec)
            if eng == "act":
                tbl = _tbl_of(rec.func)
            if cost is None and rec.n is not None:
                if eng == "act":
                    cost = 0.28 + rec.n / 1200.0
                elif eng == "dve":
                    cost = 0.25 + rec.n / 960.0 * (2.0 if rec.meth == "tensor_tensor_scan" else 1.0)
                else:
                    cost = 0.2 + rec.n / 480.0
        if cost is None:
            cost = self.DEFCOST[eng]
        deps = set()
        for b in reads:
            if b.lw is not None:
                deps.add(b.lw)
            if b.excl:
                deps.update(o for o in b.rd if self.ops[o].eng != eng)
            for a in b.aliases:
                if a.lw is not None:
                    deps.add(a.lw)
        rdeps = set(deps)
        for b in writes:
            if b.lw is not None:
                deps.add(b.lw)
            deps.update(b.rd)
            for a in b.aliases:
                if a.lw is not None:
                    deps.add(a.lw)
                deps.update(a.rd)
        op = Op(len(self.ops), eng, fn, deps, dma, cost, tbl)
        op.rdeps = rdeps
        op.phase = getattr(self, "cur_phase", "")
        op.xfer = xfer
        if dma:
            op.cost = 0.3 if eng == "sp" else 1.0
        self.ops.append(op)
        self.by_eng[eng].append(op)
        for b in writes:
            b.lw = op.id
            b.rd = []
            for a in b.aliases:
                a.lw = op.id
                a.rd = []
        for b in reads:
            if b.lw != op.id:
                b.rd.append(op.id)
        return op

    def schedule(self, window):
        ops = self.ops
        n = len(ops)
        fin = [None] * n
        pend = {e: [op.id for op in self.by_eng[e]] for e in self.ENGS}
        free = {e: 0.0 for e in self.ENGS}
        new = {e: [] for e in self.ENGS}
        win = {"pe": window, "act": window, "dve": window, "pool": 1, "sp": 1}
        cur_tbl = [0]
        LAT = 0.25
        dma_free = [0.0]
        remaining = n
        while remaining:
            best = None
            for e in self.ENGS:
                lst = pend[e]
                if not lst:
                    continue
                seen = set()
                cand = None
                cnt = 0
                for oid in lst:
                    if cnt >= win[e]:
                        break
                    cnt += 1
                    op = ops[oid]
                    ok = True
                    rt = 0.0
                    for d in op.deps:
                        f = fin[d]
                        if f is None:
                            ok = False
                            break
                        if ops[d].eng != e:
                            f += LAT
                        if f > rt:
                            rt = f
                    allowed = True
                    if e == "act" and op.tbl != 0:
                        if seen and seen != {op.tbl}:
                            allowed = False
                        seen.add(op.tbl)
                    if ok and allowed:
                        stt = max(free[e], rt)
                        if e == "act" and op.tbl != 0 and op.tbl != cur_tbl[0]:
                            stt += 2.7
                        if cand is None or stt < cand[0] - 1e-9:
                            cand = (stt, oid)
                        if stt <= free[e] + 1e-9:
                            break
                if cand is not None and (best is None or cand[0] < best[0]):
                    best = (cand[0], cand[1], e)
            assert best is not None, "schedule deadlock"
            stt, oid, e = best
            op = ops[oid]
            if op.dma:
                free[e] = stt + op.cost
                t0_ = max(stt + op.cost, dma_free[0])
                dma_free[0] = t0_ + op.xfer
                fin[oid] = dma_free[0] + 2.0
            else:
                fin[oid] = stt + op.cost
                free[e] = fin[oid]
            if e == "act" and op.tbl != 0:
                cur_tbl[0] = op.tbl
            pend[e].remove(oid)
            new[e].append(op)
            remaining -= 1
        self.by_eng = new
        self.sim_time = max(free.values())
        self.sim_busy = {e: sum(op.cost for op in new[e]) for e in self.ENGS}
        self.sim_fin = fin
        import collections
        gaps = collections.Counter()
        busy = collections.Counter()
        for e in ("pe",):
            prev = 0.0
            for op in new[e]:
                stt = fin[op.id] - op.cost
                if stt > prev + 1e-9:
                    gaps[op.phase] += stt - prev
                    if stt > self.sim_time / 2:
                        gaps["late:" + op.phase] += stt - prev
                busy[op.phase] += op.cost
                prev = fin[op.id]
        self.sim_gaps = gaps
        self.sim_pebusy = busy

    def emit(self, window=0):
        nc = self.nc
        ops = self.ops
        if window > 1:
            self.schedule(window)
        for op in ops:
            for d in op.deps:
                Dd = ops[d]
                if Dd.dma:
                    continue
                if Dd.eng == "pe" and op.eng == "pe" and not op.dma:
                    continue
                Dd.needs_inc = True
        for e in self.ENGS:
            c = 0
            for op in self.by_eng[e]:
                if op.dma:
                    continue
                if op.needs_inc:
                    c += 1
                op.count = c
        CE = ("pe", "act", "dve", "pool")
        for op in ops:
            vc = {e: 0 for e in CE}
            for d in op.deps:
                dv = ops[d].vc
                for e in CE:
                    if dv[e] > vc[e]:
                        vc[e] = dv[e]
            if not op.dma and op.needs_inc:
                vc[op.eng] = max(vc[op.eng], op.count)
            op.vc = vc
        st = contextlib.ExitStack()
        with st:
            esem = {e: st.enter_context(nc.semaphore("sem_" + e)) for e in ("pe", "act", "dve", "pool")}
            for q in ("sp", "pool", "act"):
                nd = sum(1 for op in self.by_eng[q] if op.dma)
                if nd == 0:
                    continue
                k = min(self.NDMA_SEM, nd)
                sems = [st.enter_context(nc.semaphore("dsem_%s_%d" % (q, i))) for i in range(k)]
                n = 0
                lastop = {}
                for op in self.by_eng[q]:
                    if not op.dma:
                        continue
                    op.sem = sems[n % k]
                    op.val = 16 * (n // k + 1)
                    op.prev_same_sem = lastop.get(n % k)
                    lastop[n % k] = op
                    n += 1
            block = st.enter_context(nc.Block())
            handles = {"pe": block.tensor, "act": block.scalar, "dve": block.vector, "pool": block.gpsimd,
                       "sp": block.sync}

            ATTACH = int(os.environ.get("MK_ATTACH", "1"))

            def make_body(e):
                def body(eng):
                    known = {k: 0 for k in CE}
                    dwaited = {}
                    nstand = [0, 0]
                    for op in self.by_eng[e]:
                        dneed = {}
                        eneed = {}
                        for d in op.deps:
                            Dd = ops[d]
                            if Dd.dma:
                                key = id(Dd.sem)
                                if dneed.get(key, (None, 0))[1] < Dd.val:
                                    dneed[key] = (Dd.sem, Dd.val)
                            else:
                                if Dd.eng == "pe" and e == "pe" and not op.dma:
                                    continue
                                ent = eneed.setdefault(Dd.eng, [0, None, 0])
                                if Dd.count > ent[0]:
                                    ent[0] = Dd.count
                                    ent[1] = Dd
                                if d in op.rdeps and Dd.count > ent[2]:
                                    ent[2] = Dd.count
                        if op.dma and op.prev_same_sem is not None:
                            Pp = op.prev_same_sem
                            key = id(Pp.sem)
                            if dneed.get(key, (None, 0))[1] < Pp.val:
                                dneed[key] = (Pp.sem, Pp.val)
                        for key, (sem, val) in dneed.items():
                            if dwaited.get(key, 0) < val:
                                eng.wait_ge(sem, val)
                                dwaited[key] = val
                        items = [(k, v) for k, v in eneed.items() if v[0] > known[k]]
                        pruned = []
                        for k, v in items:
                            implied = False
                            for k2, v2 in items:
                                if k2 != k and v2[1].vc[k] >= v[0]:
                                    implied = True
                                    break
                            if not implied:
                                pruned.append((k, v))
                        attach = None
                        stand = []
                        if ATTACH and (not op.dma):
                            if e in ("act", "dve", "pool"):
                                if pruned:
                                    attach = pruned.pop()
                                stand = pruned
                            elif e == "pe":
                                for k, v in pruned:
                                    if v[2] > known[k]:
                                        stand.append((k, v))
                                    elif attach is None:
                                        attach = (k, v)
                                    else:
                                        stand.append((k, v))
                            else:
                                stand = pruned
                        else:
                            stand = pruned
                        for k, v in stand:
                            eng.wait_ge(esem[k], v[0])
                            nstand[0] += 1
                        ins = op.fn(eng)
                        first = last = ins
                        if isinstance(ins, tuple):
                            first, last = ins
                        if attach is not None:
                            k, v = attach
                            first._wait_ge(esem[k], v[0])
                            nstand[1] += 1
                            stand = stand + [attach]
                        for k, v in stand:
                            dv = v[1].vc
                            for kk in CE:
                                if dv[kk] > known[kk]:
                                    known[kk] = dv[kk]
                            if v[0] > known[k]:
                                known[k] = v[0]
                        if op.dma:
                            last.then_inc(op.sem, 16)
                        elif op.needs_inc:
                            last.then_inc(esem[e], 1)
                    print("[mk] waits", e, "standalone", nstand[0], "attached", nstand[1])
                return body

            for e in self.ENGS:
                if self.by_eng[e]:
                    handles[e](make_body(e))


def vec_layout(L):
    off = {}
    n = 0
    for name, sz in [("g1", L * 8), ("g2", L * 8), ("gf", 8), ("gn", L * 2), ("lcw", L * 4 * 8), ("lcb", L * 8),
                     ("ba", L * 8), ("bx", L * 8), ("lam", L * 8), ("fcw", L * 3 * 48), ("fcb", L * 48)]:
        off[name] = n
        n += sz
    return off, n


def build(T, L):
    NT = T // TT
    nc = bass.Bass("TRN2", target_bir_lowering=False)
    VO, NV = vec_layout(L)
    xT = nc.dram_tensor("xT", [D, T], F32, kind="ExternalInput").ap()
    wbig = nc.dram_tensor("wbig", [L, NG, 128, 4096], F32, kind="ExternalInput").ap()
    wa_d = nc.dram_tensor("wa", [L, 128, 128], F32, kind="ExternalInput").ap()
    wal_d = nc.dram_tensor("wal", [L, 17, 512], F32, kind="ExternalInput").ap()
    vec_d = nc.dram_tensor("vec", [128, NV], F32, kind="ExternalInput").ap()
    cst_d = nc.dram_tensor("cst", [128, 130], F32, kind="ExternalInput").ap()
    yT = nc.dram_tensor("yT", [D, T], F32, kind="ExternalOutput").ap()
    wsc = nc.dram_tensor("wsc", [L, NG, 128, 4096], BF16).ap()
    xTv = xT.rearrange("(kc p) t -> p kc t", p=128)
    yTv = yT.rearrange("(kc p) t -> p kc t", p=128)

    S = Sched(nc)
    MMC = float(os.environ.get("MK_MMC", "0.225"))
    XF = 1048576 / 340e3
    cur = [16512]
    ranged = []

    def alloc(name, shape, dt, at=None):
        nbytes = int(np.prod(shape[1:])) * (4 if dt == F32 else 2)
        if at is None:
            o = cur[0]
            cur[0] += (nbytes + 31) // 32 * 32
        else:
            o = at
        t = nc.alloc_sbuf_tensor_at(name, list(shape), dt, offset=o)
        return t, (o, o + nbytes)

    vec, _ = alloc("vec", [128, NV], F32)
    c8, _ = alloc("c8", [128, L * 8], F32)
    c16, _ = alloc("c16", [128, L * 8], F32)
    etot, _ = alloc("etot", [128, 32], F32)
    hstate, _ = alloc("hstate", [128, L * 8], F32)
    Tl, _ = alloc("Tl", [128, L, 8, 3], F32)
    fixl, _ = alloc("fixl", [128, L, 8, 3], F32)
    Tf, _ = alloc("Tf", [128, L, 48, 2], F32)
    fixf, _ = alloc("fixf", [128, L, 48, 2], F32)
    ptmp, _ = alloc("ptmp", [128, 2, 48], F32)
    cst, _ = alloc("cst", [128, 130], F32)
    ones_bf, _ = alloc("ones_bf", [128, 128], BF16)
    wal, _ = alloc("wal", [128, L, 512], BF16)
    waS, _ = alloc("waS", [128, L, 128], BF16)
    a_aug, _ = alloc("a_aug", [128, 512], BF16)
    xs0, _ = alloc("xs0", [128, KC, TT], F32)
    xs1, _ = alloc("xs1", [128, KC, TT], F32)
    XS = [xs0, xs1]
    hb0, _ = alloc("hb0", [128, KC, TT], BF16)
    hb1, _ = alloc("hb1", [128, KC, TT], BF16)
    HBS = [hb0, hb1]
    ring, _ = alloc("ring", [128, NSLOT, 4096], BF16)
    Sst, _ = alloc("Sst", [128, L, NH, 256], F32)
    sqb, _ = alloc("sqb", [128, KC, TT], BF16)
    lnv, _ = alloc("lnv", [128, TT], F32)
    u1, rg_u1 = alloc("u1", [128, KC, TT], F32)
    r1 = cur[0]
    R1SZ = 49152
    cur[0] += R1SZ
    assert cur[0] <= 229344, cur[0]

    def ralloc(name, shape, dt, off):
        t, rng = alloc(name, shape, dt, at=r1 + off)
        assert rng[1] <= r1 + R1SZ, (name, rng)
        return t, rng

    ebuf, rg_ebuf = ralloc("ebuf", [128, TT], F32, 0)
    spbuf, rg_sp = ralloc("spbuf", [128, 2, TT], F32, 2048)
    dbuf, rg_dbuf = ralloc("dbuf", [128, TT], F32, 6144)
    kdec, rg_kdec = ralloc("kdec", [128, 4, 512], BF16, 8192)
    vbf, rg_vbf = ralloc("vbf", [128, 4, 1024], BF16, 12288)
    qTb, rg_qT = ralloc("qTb", [128, NH, TT], BF16, 20480)
    Sbf, rg_Sbf = ralloc("Sbf", [128, 2, NH, 256], BF16, 24576)
    osg, rg_osg = ralloc("osg", [128, KC, TT], BF16, 40960)
    tmpf, rg_tmpf = ralloc("tmpf", [128, TT], F32, 38912)
    acc, rg_acc = ralloc("acc", [128, 2, TT], F32, 0)
    xcb, rg_xcb = ralloc("xcb", [128, 2, TT], BF16, 4096)
    rbuf, rg_rbuf = ralloc("rbuf", [128, 4, TT], F32, 6144)
    t1b, rg_t1 = ralloc("t1b", [128, 4, TT], F32, 14336)
    abuf, rg_abuf = ralloc("abuf", [128, 4, TT], F32, 22528)
    hg, rg_hg = ralloc("hg", [128, KC, TT], BF16, 30720)
    sgb, rg_sgb = ralloc("sgb", [128, 2, TT], F32, 6144)
    m2b, rg_m2 = ralloc("m2b", [128, 2, TT], F32, 10240)
    mrg, rg_mrg = ralloc("mrg", [128, KC, TT], BF16, 14336)
    facc, rg_facc = ralloc("facc", [128, 3, 2, TT], F32, 28672)
    gv_lo, rg_gvlo = alloc("gv_lo", [128, 16, TT], BF16, at=rg_u1[0])
    gv_hi, rg_gvhi = ralloc("gv_hi", [128, 8, TT], BF16, 40960)

    def gvs(i):
        return gv_lo[:, i, :] if i < 16 else gv_hi[:, i - 16, :]

    ps = nc.alloc_psum_tensor("ps", [128, 8, 512], F32)

    def mk(name, rng):
        b = Buf(name, rng)
        ranged.append(b)
        return b

    def sub(rng, i, n):
        a, b = rng
        step = (b - a) // n
        return (a + i * step, a + (i + 1) * step)

    B_ps = [Buf("ps%d" % i, excl=True) for i in range(8)]
    B_kvh = [Buf("kv%d" % h) for h in range(4)]
    BXS = [[Buf("xs%d_%d" % (t_, i)) for i in range(KC)] for t_ in range(2)]
    BHB = [[Buf("hb%d_%d" % (t_, i)) for i in range(KC)] for t_ in range(2)]
    B_sqb = [Buf("sqb%d" % i) for i in range(KC)]
    B_lnv = Buf("lnv")
    B_u1 = [mk("u1_%d" % i, sub(rg_u1, i, 8)) for i in range(KC)]
    B_ring = [Buf("ring%d" % i) for i in range(NSLOT)]
    B_wsc = [[Buf("wsc%d_%d" % (l, g)) for g in range(NG)] for l in range(L)]
    B_S = [[Buf("S%d_%d" % (l, h)) for h in range(NH)] for l in range(L)]
    B_etot = Buf("etot")
    B_aaug = Buf("aaug")
    B_hst = [[Buf("hst%d_%d" % (l, c)) for c in range(8)] for l in range(L)]
    B_Tl = [Buf("Tl%d" % l) for l in range(L)]
    B_fixl = [Buf("fixl%d" % l) for l in range(L)]
    B_Tf = [Buf("Tf%d" % l) for l in range(L)]
    B_fixf = [Buf("fixf%d" % l) for l in range(L)]
    B_ptmp = Buf("ptmp")
    B_ebuf = mk("ebuf", rg_ebuf)
    B_sp = [mk("sp%d" % i, sub(rg_sp, i, 2)) for i in range(2)]
    B_dbuf = mk("dbuf", rg_dbuf)
    B_kdec = [mk("kdec%d" % i, sub(rg_kdec, i, 4)) for i in range(4)]
    B_vbf = [mk("vbf%d" % i, sub(rg_vbf, i, 4)) for i in range(4)]
    B_qT = [mk("qT%d" % i, sub(rg_qT, i, 4)) for i in range(4)]
    B_Sbf = [[mk("Sbf%d_%d" % (p_, h), sub(sub(rg_Sbf, p_, 2), h, 4)) for h in range(4)] for p_ in range(2)]
    B_osg = [mk("osg%d" % i, sub(rg_osg, i, 8)) for i in range(8)]
    B_tmpf = mk("tmpf", rg_tmpf)
    B_acc = [mk("acc%d" % i, sub(rg_acc, i, 2)) for i in range(2)]
    B_xcb = [mk("xcb%d" % i, sub(rg_xcb, i, 2)) for i in range(2)]
    B_rbuf = [mk("rbuf%d" % i, sub(rg_rbuf, i, 4)) for i in range(4)]
    B_t1 = [mk("t1_%d" % i, sub(rg_t1, i, 4)) for i in range(4)]
    B_abuf = [mk("abuf%d" % i, sub(rg_abuf, i, 4)) for i in range(4)]
    B_hg = [mk("hg%d" % i, sub(rg_hg, i, 8)) for i in range(8)]
    B_sgb = [mk("sgb%d" % i, sub(rg_sgb, i, 2)) for i in range(2)]
    B_m2 = [mk("m2_%d" % i, sub(rg_m2, i, 2)) for i in range(2)]
    B_mrg = [mk("mrg%d" % i, sub(rg_mrg, i, 8)) for i in range(8)]
    B_facc = [mk("facc%d" % i, sub(rg_facc, i, 3)) for i in range(3)]
    B_gv = [mk("gv%d" % i, sub(rg_gvlo, i, 16) if i < 16 else sub(rg_gvhi, i - 16, 8)) for i in range(24)]
    for i, a in enumerate(ranged):
        for b in ranged[i + 1:]:
            if a.rng[0] < b.rng[1] and b.rng[0] < a.rng[1]:
                a.aliases.append(b)
                b.aliases.append(a)


    class Banks:
        def __init__(self):
            self.pool = list(range(8))
            self.i = 0

        def get(self):
            b = self.pool[self.i % len(self.pool)]
            self.i += 1
            return b

        def set(self, lst):
            self.pool = list(lst)
            self.i = 0

    BK = Banks()

    def vcol(name, idx):
        o = VO[name] + idx
        return vec[:, o:o + 1]

    B_setup = []

    def sadd(eng, fn, dma=False, extra_w=(), reads=()):
        b = Buf("setup%d" % len(B_setup))
        B_setup.append(b)
        S.add(eng, fn, reads=reads, writes=[b] + list(extra_w), dma=dma)
        return b

    b_vec = sadd("sp", lambda e: e.dma_start(out=vec[:, :], in_=vec_d[:, :]), dma=True)
    sadd("sp", lambda e: e.dma_start(out=cst[:, :], in_=cst_d[:, :]), dma=True)
    sadd("pool", lambda e: e.memset(ones_bf[:, :], 1.0))
    sadd("pool", lambda e: e.memset(a_aug[:, :], 1.0), extra_w=[B_aaug])
    sadd("pool", lambda e: e.memset(Sst[:, :, :, :], 0.0), extra_w=[b for l in range(L) for b in B_S[l]])
    sadd("pool", lambda e: e.memset(hstate[:, :], 0.0), extra_w=[b for l in range(L) for b in B_hst[l]])
    sadd("pool", lambda e: e.memset(Tl[:, :, :, :], 0.0), extra_w=B_Tl)
    sadd("pool", lambda e: e.memset(Tf[:, :, :, :], 0.0), extra_w=B_Tf)
    for l in range(L):
        sadd("pool", lambda e, l=l: e.dma_start(out=wal[0:17, l, :], in_=wal_d[l, :, :]), dma=True)
        sadd("pool", lambda e, l=l: e.dma_start(out=waS[:, l, :], in_=wa_d[l, :, :]), dma=True)
    lamv = vec[:, VO["lam"]:VO["lam"] + L * 8]
    b_c8a = sadd("act", lambda e: e.activation(out=c8[:, :], in_=lamv, func=AF.Exp, scale=-1.0), reads=[b_vec])
    b_c8b = sadd("act", lambda e: e.activation(out=c8[:, :], in_=c8[:, :], func=AF.Ln, bias=1.0), reads=[b_c8a])
    b_c16 = sadd("act", lambda e: e.mul(out=c16[:, :], in_=c8[:, :], mul=-16.0), reads=[b_c8b])
    sadd("act", lambda e: e.mul(out=c8[:, :], in_=c8[:, :], mul=-8.0), reads=[b_c8b, b_c16])

    def pool_tt(out, in0, in1, op, reads, writes):
        S.add("pool", lambda e: e.tensor_tensor(out=out, in0=in0, in1=in1, op=op), reads=reads, writes=writes)

    def compute_fix_f(l, extra_reads=()):
        w0 = vec[:, VO["fcw"] + (l * 3 + 0) * 48: VO["fcw"] + (l * 3 + 0) * 48 + 48]
        w1 = vec[:, VO["fcw"] + (l * 3 + 1) * 48: VO["fcw"] + (l * 3 + 1) * 48 + 48]
        bb = vec[:, VO["fcb"] + l * 48: VO["fcb"] + l * 48 + 48]
        T0 = Tf[:, l, :, 0]
        T1 = Tf[:, l, :, 1]
        rd = [B_Tf[l]] + list(extra_reads)
        pool_tt(ptmp[:, 0, :], w0, T0, ALU.mult, rd, [B_ptmp])
        pool_tt(ptmp[:, 1, :], w1, T1, ALU.mult, rd + [B_ptmp], [B_ptmp])
        pool_tt(ptmp[:, 0, :], ptmp[:, 0, :], ptmp[:, 1, :], ALU.add, [B_ptmp], [B_ptmp])
        pool_tt(fixf[:, l, :, 0], ptmp[:, 0, :], bb, ALU.add, [B_ptmp], [B_fixf[l]])
        pool_tt(ptmp[:, 1, :], w0, T1, ALU.mult, rd + [B_ptmp], [B_ptmp])
        pool_tt(fixf[:, l, :, 1], ptmp[:, 1, :], bb, ALU.add, [B_ptmp, B_fixf[l]], [B_fixf[l]])

    def compute_fix_l(l, extra_reads=()):
        def w(j):
            o = VO["lcw"] + (l * 4 + j) * 8
            return vec[:, o:o + 8]
        bb = vec[:, VO["lcb"] + l * 8: VO["lcb"] + l * 8 + 8]
        T0, T1, T2 = Tl[:, l, :, 0], Tl[:, l, :, 1], Tl[:, l, :, 2]
        rd = [B_Tl[l]] + list(extra_reads)
        pa, pb = ptmp[:, 0, 0:8], ptmp[:, 1, 0:8]
        pool_tt(pa, w(0), T0, ALU.mult, rd + [B_ptmp], [B_ptmp])
        pool_tt(pb, w(1), T1, ALU.mult, rd + [B_ptmp], [B_ptmp])
        pool_tt(pa, pa, pb, ALU.add, [B_ptmp], [B_ptmp])
        pool_tt(pb, w(2), T2, ALU.mult, rd + [B_ptmp], [B_ptmp])
        pool_tt(pa, pa, pb, ALU.add, [B_ptmp], [B_ptmp])
        pool_tt(fixl[:, l, :, 0], pa, bb, ALU.add, [B_ptmp, B_fixl[l]], [B_fixl[l]])
        pool_tt(pa, w(0), T1, ALU.mult, rd + [B_ptmp], [B_ptmp])
        pool_tt(pb, w(1), T2, ALU.mult, rd + [B_ptmp], [B_ptmp])
        pool_tt(pa, pa, pb, ALU.add, [B_ptmp], [B_ptmp])
        pool_tt(fixl[:, l, :, 1], pa, bb, ALU.add, [B_ptmp, B_fixl[l]], [B_fixl[l]])
        pool_tt(pa, w(0), T2, ALU.mult, rd + [B_ptmp], [B_ptmp])
        pool_tt(fixl[:, l, :, 2], pa, bb, ALU.add, [B_ptmp, B_fixl[l]], [B_fixl[l]])

    for e in ("pe", "act", "dve", "pool", "sp"):
        S.add(e, lambda eng: eng.nop(), reads=list(B_setup))
    for l in range(L):
        compute_fix_f(l)
        compute_fix_l(l)

    def add_cast(l, g):
        def fn(e):
            return e.dma_start(out=wsc[l, g].rearrange("p (a b) -> p a b", b=2048),
                               in_=wbig[l, g].rearrange("p (a b) -> p a b", b=2048))
        S.add("pool", fn, writes=[B_wsc[l][g]], dma=True, xfer=3 * XF)

    for g in range(NG):
        add_cast(0, g)
    cur_tile = [0]
    CAST_TILE = 1 if (int(os.environ.get("MK_PAIR", "2")) >= 2 and NT >= 2) else 0

    wcount = [0]

    def wnext(l, name):
        g = GROUPS.index(name)
        n = wcount[0]
        assert GROUPS[n % NG] == name, (name, GROUPS[n % NG])
        wcount[0] += 1
        s = n % NSLOT
        S.add("sp", lambda e: e.dma_start(out=ring[:, s, :], in_=wsc[l, g]), reads=[B_wsc[l][g]], writes=[B_ring[s]],
              dma=True, xfer=XF)
        if cur_tile[0] == CAST_TILE and l + 1 < L:
            add_cast(l + 1, g)
        return ring[:, s, :].rearrange("p (k c) -> p k c", c=512), B_ring[s], s

    def mm_group(out_ap, pairs, reads, writes, cost=None):
        n = len(pairs)
        if cost is None:
            cost = MMC * n

        def fn(e):
            ins = first = None
            for i, (lt, rh) in enumerate(pairs):
                ins = e.matmul(out_ap, lt, rh, start=(i == 0), stop=(i == n - 1))
                if first is None:
                    first = ins
            return (first, ins)
        S.add("pe", fn, reads=reads, writes=writes, cost=cost)

    def mm_group_ks(out_ap, pairs, reads_each, common_reads, writes):
        n = len(pairs)
        for i, (lt, rh) in enumerate(pairs):
            S.add("pe", lambda e, i=i, lt=lt, rh=rh: e.matmul(out_ap, lt, rh, start=(i == 0), stop=(i == n - 1)),
                  reads=[reads_each[i]] + list(common_reads), writes=writes, cost=MMC)

    def rmsnorm_to_hb(gname, gidx, xs, B_xs):
        for kc in range(KC):
            S.add("act", lambda e, kc=kc: e.activation(out=sqb[:, kc, :], in_=xs[:, kc, :], func=AF.Square),
                  reads=[B_xs[kc]], writes=[B_sqb[kc]])
        bk = BK.get()
        if KSPLIT:
            mm_group_ks(ps[:, bk, :], [(ones_bf[:, :], sqb[:, kc, :]) for kc in range(KC)], list(B_sqb), [], [B_ps[bk]])
        else:
            mm_group(ps[:, bk, :], [(ones_bf[:, :], sqb[:, kc, :]) for kc in range(KC)], list(B_sqb), [B_ps[bk]])
        S.add("act", lambda e: e.activation(out=lnv[:, :], in_=ps[:, bk, :], func=AF.Ln, scale=1.0 / D, bias=EPS),
              reads=[B_ps[bk]], writes=[B_lnv])
        S.add("act", lambda e: e.activation(out=ps[:, bk, :], in_=lnv[:, :], func=AF.Exp, scale=-0.5),
              reads=[B_lnv], writes=[B_ps[bk]])
        return bk

    STOP = int(os.environ.get("MK_STOP", "99"))
    GVENG = os.environ.get("MK_GVENG", "pool")
    XCBENG = os.environ.get("MK_XCBENG", "pool")
    A2ENG = os.environ.get("MK_A2ENG", "pool")
    KSPLIT = int(os.environ.get("MK_KSPLIT", "1"))

    def layer(l, xs, B_xs, hb, B_hb):
        layer_body(l, xs, B_xs, hb, B_hb)
        wcount[0] = (wcount[0] + NG - 1) // NG * NG

    def layer_body(l, xs, B_xs, hb, B_hb):
        S.cur_phase = "norm1"
        BK.set(range(8))
        bk = rmsnorm_to_hb("g1", l, xs, B_xs)
        for kc in range(KC):
            S.add("dve", lambda e, kc=kc, bk=bk: e.scalar_tensor_tensor(
                out=hb[:, kc, :], in0=xs[:, kc, :], scalar=vcol("g1", l * 8 + kc), in1=ps[:, bk, :],
                op0=ALU.mult, op1=ALU.mult), reads=[B_xs[kc], B_ps[bk]], writes=[B_hb[kc]])
        if STOP <= 1:
            return
        S.cur_phase = "prologue"
        BK.set(range(7))
        TOTB = 7
        b0 = BK.get()
        if KSPLIT:
            mm_group_ks(ps[0:16, b0, :], [(waS[:, l, kc * 16:(kc + 1) * 16], hb[:, kc, :]) for kc in range(KC)],
                        list(B_hb), [], [B_ps[b0]])
        else:
            mm_group(ps[0:16, b0, :], [(waS[:, l, kc * 16:(kc + 1) * 16], hb[:, kc, :]) for kc in range(KC)],
                     list(B_hb), [B_ps[b0]])
        S.add("act", lambda e: e.copy(out=a_aug[0:16, :], in_=ps[0:16, b0, :]), reads=[B_ps[b0]], writes=[B_aaug])
        wk, Bwk, _ = wnext(l, "k")
        wv0, Bwv0, _ = wnext(l, "v0")
        wv1, Bwv1, _ = wnext(l, "v1")
        U = cst[:, 0:128]
        Cind = cst[:, 128:130]
        for b in range(4):
            tb = slice(b * 128, (b + 1) * 128)
            sp_i = b % 2
            bz = BK.get()
            mm_group(ps[:, bz, :], [(a_aug[0:17, tb], wal[0:17, l, :])], [B_aaug], [B_ps[bz]])
            S.add("act", lambda e, bz=bz: e.activation(out=ebuf[:, :], in_=ps[:, bz, :], func=AF.Exp, scale=-1.0),
                  reads=[B_ps[bz]], writes=[B_ebuf])
            S.add("act", lambda e, sp_i=sp_i: e.activation(out=spbuf[:, sp_i, :], in_=ebuf[:, :], func=AF.Ln, bias=1.0),
                  reads=[B_ebuf], writes=[B_sp[sp_i]])
            br = BK.get()
            mm_group(ps[:, br, :], [(U, spbuf[:, sp_i, :])], [B_sp[sp_i]], [B_ps[br]], cost=1.1)
            for h in range(NH):
                col = h * 8 + b * 2
                mm_group(ps[:, TOTB, col:col + 2], [(spbuf[:, sp_i, h * 128:(h + 1) * 128], Cind)], [B_sp[sp_i]],
                         [B_ps[TOTB]], cost=0.25)
            S.add("act", lambda e, br=br: e.activation(out=dbuf[:, :], in_=ps[:, br, :], func=AF.Exp, scale=-1.0 / 16),
                  reads=[B_ps[br]], writes=[B_dbuf])
            bkk = BK.get()
            if KSPLIT and b == 0:
                mm_group_ks(ps[:, bkk, :], [(hb[:, kc, tb], wk[:, kc, :]) for kc in range(KC)], list(B_hb), [Bwk],
                            [B_ps[bkk]])
            else:
                mm_group(ps[:, bkk, :], [(hb[:, kc, tb], wk[:, kc, :]) for kc in range(KC)], list(B_hb) + [Bwk],
                         [B_ps[bkk]])
            S.add("dve", lambda e, bkk=bkk, b=b: e.tensor_tensor(out=kdec[:, b, :], in0=ps[:, bkk, :], in1=dbuf[:, :],
                                                               op=ALU.mult),
                  reads=[B_ps[bkk], B_dbuf], writes=[B_kdec[b]])
            for half, (wv, Bwv) in enumerate(((wv0, Bwv0), (wv1, Bwv1))):
                bv = BK.get()
                if KSPLIT and b == 0:
                    mm_group_ks(ps[:, bv, :], [(hb[:, kc, tb], wv[:, kc, :]) for kc in range(KC)], list(B_hb), [Bwv],
                                [B_ps[bv]])
                else:
                    mm_group(ps[:, bv, :], [(hb[:, kc, tb], wv[:, kc, :]) for kc in range(KC)], list(B_hb) + [Bwv],
                             [B_ps[bv]])
                S.add("act", lambda e, bv=bv, b=b, half=half: e.copy(out=vbf[:, b, half * 512:(half + 1) * 512],
                                                                    in_=ps[:, bv, :]),
                      reads=[B_ps[bv]], writes=[B_vbf[b]])
        S.add("act", lambda e: e.activation(out=etot[:, :], in_=ps[:, TOTB, 0:32], func=AF.Exp, scale=-1.0 / 16),
              reads=[B_ps[TOTB]], writes=[B_etot])
        wq, Bwq, _ = wnext(l, "q")
        for h in range(NH):
            bq = BK.get()
            mm_group(ps[:, bq, :], [(wq[:, kc, h * 128:(h + 1) * 128], hb[:, kc, :]) for kc in range(KC)],
                     list(B_hb) + [Bwq], [B_ps[bq]])
            S.add("act", lambda e, bq=bq, h=h: e.mul(out=qTb[:, h, :], in_=ps[:, bq, :], mul=float(128 ** -0.5)),
                  reads=[B_ps[bq]], writes=[B_qT[h]])
        if STOP <= 2:
            return
        S.cur_phase = "recur"
        BK.set([4, 5, 6, 7])
        for half in range(2):
            for cl in range(4):
                c = half * 4 + cl
                b, cc = c // 2, c % 2
                par = c % 2
                prt = slice(cc * 64, (cc + 1) * 64)
                for h in range(NH):
                    kvb, kvc = 4 + h, 0
                    mm_group(ps[:, kvb, kvc:kvc + 256],
                             [(kdec[prt, b, h * 128:(h + 1) * 128], vbf[prt, b, h * 256:(h + 1) * 256])],
                             [B_kdec[b], B_vbf[b]], [B_ps[kvb]], cost=0.2)
                for h in range(NH):
                    kvb, kvc = 4 + h, 0
                    S.add("dve", lambda e, h=h, c=c, kvb=kvb, kvc=kvc: e.scalar_tensor_tensor(
                        out=Sst[:, l, h, :], in0=Sst[:, l, h, :], scalar=etot[:, h * 8 + c:h * 8 + c + 1],
                        in1=ps[:, kvb, kvc:kvc + 256], op0=ALU.mult, op1=ALU.add),
                        reads=[B_S[l][h], B_etot, B_ps[kvb]], writes=[B_S[l][h]])
                    S.add("act", lambda e, h=h, par=par: e.copy(out=Sbf[:, par, h, :], in_=Sst[:, l, h, :]),
                          reads=[B_S[l][h]], writes=[B_Sbf[par][h]])
                for h in range(NH):
                    def fn(e, h=h, cl=cl, c=c, par=par):
                        ins = first = None
                        for j in range(2):
                            ins = e.matmul(ps[:, h, j * 256 + cl * 64: j * 256 + (cl + 1) * 64],
                                           Sbf[:, par, h, j * 128:(j + 1) * 128], qTb[:, h, c * 64:(c + 1) * 64],
                                           start=True, stop=True)
                            if first is None:
                                first = ins
                        return (first, ins)
                    S.add("pe", fn, reads=[B_Sbf[par][h], B_qT[h]], writes=[B_ps[h]], cost=0.2)
            tsl = slice(half * 256, (half + 1) * 256)
            for h in range(NH):
                for j in range(2):
                    S.add("act", lambda e, h=h, j=j, tsl=tsl: e.activation(out=sqb[:, 2 * h + j, tsl],
                                                                in_=ps[:, h, j * 256:(j + 1) * 256], func=AF.Square),
                          reads=[B_ps[h]], writes=[B_sqb[2 * h + j]])
                    S.add("dve", lambda e, h=h, j=j, tsl=tsl: e.tensor_scalar(
                        out=u1[:, 2 * h + j, tsl], in0=ps[:, h, j * 256:(j + 1) * 256],
                        scalar1=vcol("gn", l * 2 + j), scalar2=None, op0=ALU.mult),
                        reads=[B_ps[h]], writes=[B_u1[2 * h + j]])
            for h in range(NH):
                bo = BK.get()
                mm_group(ps[:, bo, 0:256], [(ones_bf[:, :], sqb[:, 2 * h + j, tsl]) for j in range(2)],
                         [B_sqb[2 * h], B_sqb[2 * h + 1]], [B_ps[bo]])
                S.add("act", lambda e, bo=bo: e.activation(out=lnv[:, 0:256], in_=ps[:, bo, 0:256], func=AF.Ln,
                                                          scale=1.0 / 256, bias=EPS),
                      reads=[B_ps[bo]], writes=[B_lnv])
                S.add("act", lambda e, bo=bo: e.activation(out=ps[:, bo, 0:256], in_=lnv[:, 0:256], func=AF.Exp,
                                                          scale=-0.5),
                      reads=[B_lnv], writes=[B_ps[bo]])
                for j in range(2):
                    S.add("dve", lambda e, h=h, j=j, bo=bo, tsl=tsl: e.tensor_tensor(
                        out=u1[:, 2 * h + j, tsl], in0=ps[:, bo, 0:256], in1=u1[:, 2 * h + j, tsl], op=ALU.mult),
                        reads=[B_ps[bo], B_u1[2 * h + j]], writes=[B_u1[2 * h + j]])
        if STOP <= 3:
            return
        S.cur_phase = "gout"
        BK.set(range(8))
        for gi in range(2):
            wg, Bwg, _ = wnext(l, "g%d" % gi)
            for mi in range(4):
                m = gi * 4 + mi
                bg = BK.get()
                mm_group(ps[:, bg, :], [(wg[:, kc, mi * 128:(mi + 1) * 128], hb[:, kc, :]) for kc in range(KC)],
                         list(B_hb) + [Bwg], [B_ps[bg]])
                S.add("act", lambda e, bg=bg: e.activation(out=ps[:, bg, :], in_=ps[:, bg, :], func=AF.Silu),
                      reads=[B_ps[bg]], writes=[B_ps[bg]])
                S.add("dve", lambda e, bg=bg, m=m: e.tensor_tensor(out=osg[:, m, :], in0=ps[:, bg, :], in1=u1[:, m, :],
                                                                 op=ALU.mult),
                      reads=[B_ps[bg], B_u1[m]], writes=[B_osg[m]])
        if STOP <= 4:
            return
        def ya_group(gi):
            S.cur_phase = "ya"
            wog, Bwog, _ = wnext(l, "og%d" % gi)
            wga, Bwga, _ = wnext(l, "ga%d" % gi)
            for mi in range(4):
                m = gi * 4 + mi
                bga = BK.get()
                mm_group(ps[:, bga, :], [(wga[:, kc, mi * 128:(mi + 1) * 128], hb[:, kc, :]) for kc in range(KC)],
                         list(B_hb) + [Bwga], [B_ps[bga]])
                S.add("act", lambda e, bga=bga: e.activation(out=tmpf[:, :], in_=ps[:, bga, :], func=AF.Sigmoid),
                      reads=[B_ps[bga]], writes=[B_tmpf])
                bya = BK.get()
                if KSPLIT and gi == 0 and mi < 2:
                    mm_group_ks(ps[:, bya, :], [(wog[:, kc, mi * 128:(mi + 1) * 128], osg[:, kc, :]) for kc in range(KC)],
                                list(B_osg), [Bwog], [B_ps[bya]])
                else:
                    mm_group(ps[:, bya, :], [(wog[:, kc, mi * 128:(mi + 1) * 128], osg[:, kc, :]) for kc in range(KC)],
                             list(B_osg) + [Bwog], [B_ps[bya]])
                S.add("dve", lambda e, bya=bya, m=m: e.tensor_tensor(out=u1[:, m, :], in0=ps[:, bya, :], in1=tmpf[:, :],
                                                                   op=ALU.mult),
                      reads=[B_ps[bya], B_tmpf], writes=[B_u1[m]])
        if STOP <= 5:
            return
        S.cur_phase = "lru"
        wblk = None
        for hbi in range(2):
            wxr, Bwxr, _ = wnext(l, "xr%d" % hbi)
            if hbi == 0:
                wblk_raw, Bwblk, sblk = wnext(l, "blk")
                wblk = ring[:, sblk, 0:2048].rearrange("p (k c) -> p k c", c=128)
            for cl in range(4):
                c = hbi * 4 + cl
                ai = cl % 2
                bx = BK.get()
                mm_group(ps[:, bx, :], [(wxr[:, kc, cl * 128:(cl + 1) * 128], hb[:, kc, :]) for kc in range(KC)],
                         list(B_hb) + [Bwxr], [B_ps[bx]])
                wl = lambda j, c=c: vcol("lcw", (l * 4 + j) * 8 + c)
                S.add("act", lambda e, bx=bx, ai=ai, c=c, wl=wl: e.activation(
                    out=acc[:, ai, 3:512], in_=ps[:, bx, 0:509], func=AF.Identity, scale=wl(0),
                    bias=vcol("lcb", l * 8 + c)), reads=[B_ps[bx]], writes=[B_acc[ai]])
                S.add("act", lambda e, ai=ai, c=c: e.copy(out=acc[:, ai, 0:3], in_=fixl[:, l, c, :]),
                      reads=[B_fixl[l], B_acc[ai]], writes=[B_acc[ai]])
                S.add("act", lambda e, bx=bx, c=c: e.copy(out=Tl[:, l, c, :], in_=ps[:, bx, 509:512]),
                      reads=[B_ps[bx]], writes=[B_Tl[l]])
                S.add("dve", lambda e, bx=bx, ai=ai, wl=wl: e.scalar_tensor_tensor(
                    out=acc[:, ai, 2:512], in0=ps[:, bx, 0:510], scalar=wl(1), in1=acc[:, ai, 2:512],
                    op0=ALU.mult, op1=ALU.add), reads=[B_ps[bx], B_acc[ai]], writes=[B_acc[ai]])
                S.add("dve", lambda e, bx=bx, ai=ai, wl=wl: e.scalar_tensor_tensor(
                    out=acc[:, ai, 1:512], in0=ps[:, bx, 0:511], scalar=wl(2), in1=acc[:, ai, 1:512],
                    op0=ALU.mult, op1=ALU.add), reads=[B_ps[bx], B_acc[ai]], writes=[B_acc[ai]])
                S.add("dve", lambda e, bx=bx, ai=ai, wl=wl: e.scalar_tensor_tensor(
                    out=acc[:, ai, :], in0=ps[:, bx, :], scalar=wl(3), in1=acc[:, ai, :],
                    op0=ALU.mult, op1=ALU.add), reads=[B_ps[bx], B_acc[ai]], writes=[B_acc[ai]])
                S.add(XCBENG, lambda e, ai=ai: (e.copy(out=xcb[:, ai, :], in_=acc[:, ai, :]) if XCBENG == "act" else
                                               e.tensor_copy(out=xcb[:, ai, :], in_=acc[:, ai, :])),
                      reads=[B_acc[ai]], writes=[B_xcb[ai]])
                bzr = BK.get()
                mm_group(ps[:, bzr, :], [(wblk[:, c, :], xcb[:, ai, :])], [B_xcb[ai], Bwblk], [B_ps[bzr]])
                bzi = BK.get()
                mm_group(ps[:, bzi, :], [(wblk[:, 8 + c, :], xcb[:, ai, :])], [B_xcb[ai], Bwblk], [B_ps[bzi]])
                S.add("act", lambda e, bzr=bzr, cl=cl, c=c: e.activation(
                    out=rbuf[:, cl, :], in_=ps[:, bzr, :], func=AF.Sigmoid, bias=vcol("ba", l * 8 + c)),
                    reads=[B_ps[bzr]], writes=[B_rbuf[cl]])
                S.add("act", lambda e, bzi=bzi, c=c: e.activation(
                    out=ps[:, bzi, :], in_=ps[:, bzi, :], func=AF.Sigmoid, bias=vcol("bx", l * 8 + c)),
                    reads=[B_ps[bzi]], writes=[B_ps[bzi]])
                S.add("dve", lambda e, bzi=bzi, cl=cl, ai=ai: e.tensor_tensor(
                    out=t1b[:, cl, :], in0=ps[:, bzi, :], in1=acc[:, ai, :], op=ALU.mult),
                    reads=[B_ps[bzi], B_acc[ai]], writes=[B_t1[cl]])
            if hbi == 1:
                compute_fix_l(l)
            ya_group(hbi)
            S.cur_phase = "lru"
            for cl in range(4):
                c = hbi * 4 + cl
                S.add("act", lambda e, cl=cl, c=c: e.activation(out=abuf[:, cl, :], in_=rbuf[:, cl, :], func=AF.Exp,
                                                              scale=c8[:, l * 8 + c:l * 8 + c + 1]),
                      reads=[B_rbuf[cl]], writes=[B_abuf[cl]])
                if A2ENG == "act":
                    S.add("act", lambda e, cl=cl, c=c: e.activation(out=rbuf[:, cl, :], in_=rbuf[:, cl, :], func=AF.Exp,
                                                                  scale=c16[:, l * 8 + c:l * 8 + c + 1]),
                          reads=[B_rbuf[cl], B_abuf[cl]], writes=[B_rbuf[cl]])
                else:
                    S.add(A2ENG, lambda e, cl=cl: e.tensor_tensor(out=rbuf[:, cl, :], in0=abuf[:, cl, :],
                                                                  in1=abuf[:, cl, :], op=ALU.mult),
                          reads=[B_rbuf[cl], B_abuf[cl]], writes=[B_rbuf[cl]])
                S.add("act", lambda e, cl=cl: e.activation(out=rbuf[:, cl, :], in_=rbuf[:, cl, :], func=AF.Ln,
                                                         scale=-1.0, bias=1.0),
                      reads=[B_rbuf[cl]], writes=[B_rbuf[cl]])
                bs = BK.get()
                S.add("act", lambda e, cl=cl, bs=bs: e.activation(out=ps[:, bs, :], in_=rbuf[:, cl, :], func=AF.Exp,
                                                                scale=0.5),
                      reads=[B_rbuf[cl]], writes=[B_ps[bs]])
                S.add("dve", lambda e, cl=cl, bs=bs: e.tensor_tensor(out=t1b[:, cl, :], in0=ps[:, bs, :],
                                                                   in1=t1b[:, cl, :], op=ALU.mult),
                      reads=[B_ps[bs], B_t1[cl]], writes=[B_t1[cl]])
                S.add("dve", lambda e, cl=cl, c=c: e.tensor_tensor_scan(
                    out=rbuf[:, cl, :], data0=abuf[:, cl, :], data1=t1b[:, cl, :],
                    initial=hstate[:, l * 8 + c:l * 8 + c + 1], op0=ALU.mult, op1=ALU.add),
                    reads=[B_abuf[cl], B_t1[cl], B_hst[l][c], B_rbuf[cl]], writes=[B_rbuf[cl]])
                S.add("dve", lambda e, cl=cl, c=c: e.tensor_copy(out=hstate[:, l * 8 + c:l * 8 + c + 1],
                                                               in_=rbuf[:, cl, 511:512]),
                      reads=[B_rbuf[cl]], writes=[B_hst[l][c]])
            wgr, Bwgr, _ = wnext(l, "gr%d" % hbi)
            for cl in range(4):
                c = hbi * 4 + cl
                bgr = BK.get()
                mm_group(ps[:, bgr, :], [(wgr[:, kc, cl * 128:(cl + 1) * 128], hb[:, kc, :]) for kc in range(KC)],
                         list(B_hb) + [Bwgr], [B_ps[bgr]])
                S.add("act", lambda e, bgr=bgr: e.activation(out=ps[:, bgr, :], in_=ps[:, bgr, :],
                                                            func=AF.Gelu_apprx_tanh),
                      reads=[B_ps[bgr]], writes=[B_ps[bgr]])
                S.add("dve", lambda e, bgr=bgr, cl=cl, c=c: e.tensor_tensor(out=hg[:, c, :], in0=ps[:, bgr, :],
                                                                          in1=rbuf[:, cl, :], op=ALU.mult),
                      reads=[B_ps[bgr], B_rbuf[cl]], writes=[B_hg[c]])
        if STOP <= 6:
            return
        S.cur_phase = "merge"
        for gi in range(2):
            wol, Bwol, _ = wnext(l, "ol%d" % gi)
            wgb, Bwgb, _ = wnext(l, "gb%d" % gi)
            for mi in range(4):
                m = gi * 4 + mi
                si = m % 2
                bgb = BK.get()
                mm_group(ps[:, bgb, :], [(wgb[:, kc, mi * 128:(mi + 1) * 128], hb[:, kc, :]) for kc in range(KC)],
                         list(B_hb) + [Bwgb], [B_ps[bgb]])
                S.add("act", lambda e, bgb=bgb, si=si: e.activation(out=sgb[:, si, :], in_=ps[:, bgb, :],
                                                                  func=AF.Sigmoid),
                      reads=[B_ps[bgb]], writes=[B_sgb[si]])
                byb = BK.get()
                if KSPLIT and gi == 0:
                    mm_group_ks(ps[:, byb, :], [(wol[:, kc, mi * 128:(mi + 1) * 128], hg[:, kc, :]) for kc in range(KC)],
                                list(B_hg), [Bwol], [B_ps[byb]])
                else:
                    mm_group(ps[:, byb, :], [(wol[:, kc, mi * 128:(mi + 1) * 128], hg[:, kc, :]) for kc in range(KC)],
                             list(B_hg) + [Bwol], [B_ps[byb]])
                S.add("dve", lambda e, byb=byb, si=si: e.tensor_tensor(out=m2b[:, si, :], in0=ps[:, byb, :],
                                                                     in1=sgb[:, si, :], op=ALU.mult),
                      reads=[B_ps[byb], B_sgb[si]], writes=[B_m2[si]])
                S.add("pool", lambda e, si=si, m=m: e.tensor_tensor(out=mrg[:, m, :], in0=u1[:, m, :], in1=m2b[:, si, :],
                                                                  op=ALU.add),
                      reads=[B_u1[m], B_m2[si]], writes=[B_mrg[m]])
        if STOP <= 7:
            return
        S.cur_phase = "wo"
        for gi in range(2):
            wwo, Bwwo, _ = wnext(l, "wo%d" % gi)
            for mi in range(4):
                m = gi * 4 + mi
                bo = BK.get()
                if KSPLIT and gi == 0:
                    mm_group_ks(ps[:, bo, :], [(wwo[:, kc, mi * 128:(mi + 1) * 128], mrg[:, kc, :]) for kc in range(KC)],
                                list(B_mrg), [Bwwo], [B_ps[bo]])
                else:
                    mm_group(ps[:, bo, :], [(wwo[:, kc, mi * 128:(mi + 1) * 128], mrg[:, kc, :]) for kc in range(KC)],
                             list(B_mrg) + [Bwwo], [B_ps[bo]])
                S.add("dve", lambda e, bo=bo, m=m: e.tensor_tensor(out=xs[:, m, :], in0=ps[:, bo, :], in1=xs[:, m, :],
                                                                 op=ALU.add),
                      reads=[B_ps[bo], B_xs[m]], writes=[B_xs[m]])
        if STOP <= 8:
            return
        S.cur_phase = "norm2"
        bk = rmsnorm_to_hb("g2", l, xs, B_xs)
        for kc in range(KC):
            S.add("dve", lambda e, kc=kc, bk=bk: e.scalar_tensor_tensor(
                out=hb[:, kc, :], in0=xs[:, kc, :], scalar=vcol("g2", l * 8 + kc), in1=ps[:, bk, :],
                op0=ALU.mult, op1=ALU.mult), reads=[B_xs[kc], B_ps[bk]], writes=[B_hb[kc]])
        if STOP <= 9:
            return
        S.cur_phase = "up"
        BK.set(range(8))
        pair_i = 0
        for j in range(12):
            wup, Bwup, _ = wnext(l, "up%d" % j)
            for pi in range(2):
                i = 2 * j + pi
                cv, cg = 2 * i, 2 * i + 1
                bv = (pair_i % 4) * 2
                bg = bv + 1
                fa = pair_i % 3
                pair_i += 1
                for which, bb_ in ((0, bv), (1, bg)):
                    col = (2 * pi + which) * 128
                    if KSPLIT and j == 0:
                        mm_group_ks(ps[:, bb_, :], [(wup[:, kc, col:col + 128], hb[:, kc, :]) for kc in range(KC)],
                                    list(B_hb), [Bwup], [B_ps[bb_]])
                    else:
                        mm_group(ps[:, bb_, :], [(wup[:, kc, col:col + 128], hb[:, kc, :]) for kc in range(KC)],
                                 list(B_hb) + [Bwup], [B_ps[bb_]])
                wf = lambda tap, cc_: vcol("fcw", (l * 3 + tap) * 48 + cc_)
                for which, bb_, cc_ in ((0, bv, cv), (1, bg, cg)):
                    S.add("act", lambda e, which=which, bb_=bb_, cc_=cc_, fa=fa: e.activation(
                        out=facc[:, fa, which, 2:512], in_=ps[:, bb_, 0:510], func=AF.Identity, scale=wf(0, cc_),
                        bias=vcol("fcb", l * 48 + cc_)), reads=[B_ps[bb_]], writes=[B_facc[fa]])
                S.add("act", lambda e, fa=fa, cv=cv: e.copy(out=facc[:, fa, :, 0:2], in_=fixf[:, l, cv:cv + 2, :]),
                      reads=[B_fixf[l], B_facc[fa]], writes=[B_facc[fa]])
                S.add("act", lambda e, bv=bv, cv=cv: e.copy(out=Tf[:, l, cv:cv + 2, :], in_=ps[:, bv:bv + 2, 510:512]),
                      reads=[B_ps[bv], B_ps[bg]], writes=[B_Tf[l]])
                for which, bb_, cc_ in ((0, bv, cv), (1, bg, cg)):
                    S.add("dve", lambda e, which=which, bb_=bb_, cc_=cc_, fa=fa: e.scalar_tensor_tensor(
                        out=facc[:, fa, which, 1:512], in0=ps[:, bb_, 0:511], scalar=wf(1, cc_),
                        in1=facc[:, fa, which, 1:512], op0=ALU.mult, op1=ALU.add),
                        reads=[B_ps[bb_], B_facc[fa]], writes=[B_facc[fa]])
                S.add("dve", lambda e, bv=bv, cv=cv, fa=fa: e.scalar_tensor_tensor(
                    out=facc[:, fa, 0, :], in0=ps[:, bv, :], scalar=wf(2, cv), in1=facc[:, fa, 0, :],
                    op0=ALU.mult, op1=ALU.add), reads=[B_ps[bv], B_facc[fa]], writes=[B_facc[fa]])
                S.add("dve", lambda e, bg=bg, cg=cg, fa=fa: e.scalar_tensor_tensor(
                    out=facc[:, fa, 1, :], in0=ps[:, bg, :], scalar=wf(2, cg), in1=facc[:, fa, 1, :],
                    op0=ALU.mult, op1=ALU.add), reads=[B_ps[bg], B_facc[fa]], writes=[B_facc[fa]])
                S.add("act", lambda e, fa=fa: e.activation(out=facc[:, fa, 1, :], in_=facc[:, fa, 1, :],
                                                          func=AF.Gelu_apprx_tanh),
                      reads=[B_facc[fa]], writes=[B_facc[fa]])
                S.add(GVENG, lambda e, fa=fa, i=i: e.tensor_tensor(out=gvs(i), in0=facc[:, fa, 0, :],
                                                                  in1=facc[:, fa, 1, :], op=ALU.mult),
                      reads=[B_facc[fa]], writes=[B_gv[i]])
        compute_fix_f(l)
        if STOP <= 10:
            return
        S.cur_phase = "down"
        for cb in range(2):
            for t in range(3):
                wdn, Bwdn, _ = wnext(l, "dn%d" % (cb * 3 + t))

                def fn(e, wdn=wdn, t=t, cb=cb):
                    ins = first = None
                    for mi in range(4):
                        for kc in range(KC):
                            ins = e.matmul(ps[:, cb * 4 + mi, :], wdn[:, kc, mi * 128:(mi + 1) * 128], gvs(t * 8 + kc),
                                           start=(t == 0 and kc == 0), stop=(t == 2 and kc == KC - 1))
                            if first is None:
                                first = ins
                    return (first, ins)
                S.add("pe", fn, reads=[B_gv[t * 8 + kc] for kc in range(KC)] + [Bwdn],
                      writes=[B_ps[cb * 4 + mi] for mi in range(4)], cost=7.9)
            for mi in range(4):
                m = cb * 4 + mi
                S.add("dve", lambda e, mi=mi, m=m, cb=cb: e.tensor_tensor(out=xs[:, m, :], in0=ps[:, cb * 4 + mi, :],
                                                                        in1=xs[:, m, :], op=ALU.add),
                      reads=[B_ps[cb * 4 + mi], B_xs[m]], writes=[B_xs[m]])

    B_out = Buf("out_dram")
    def load_x(i, xs, B_xs):
        t0 = i * TT
        S.add("sp", lambda e: e.dma_start(out=xs[:, :, :], in_=xTv[:, :, t0:t0 + TT]), writes=list(B_xs), dma=True,
              xfer=2 * XF)

    def finalize(i, xs, B_xs):
        t0 = i * TT
        S.cur_phase = "final"
        BK.set(range(8))
        bk = rmsnorm_to_hb("gf", 0, xs, B_xs)
        for kc in range(KC):
            S.add("dve", lambda e, kc=kc: e.scalar_tensor_tensor(
                out=u1[:, kc, :], in0=xs[:, kc, :], scalar=vcol("gf", kc), in1=ps[:, bk, :],
                op0=ALU.mult, op1=ALU.mult), reads=[B_xs[kc], B_ps[bk]], writes=[B_u1[kc]])
        S.add("sp", lambda e: e.dma_start(out=yTv[:, :, t0:t0 + TT], in_=u1[:, :, :]), reads=list(B_u1),
              writes=[B_out], dma=True, xfer=2 * XF)

    PAIR = int(os.environ.get("MK_PAIR", "2"))
    for ti in range(min(PAIR, NT)):
        load_x(ti, XS[ti], BXS[ti])
    for j0 in range(0, NT, PAIR):
        tiles = list(range(j0, min(NT, j0 + PAIR)))
        for l in range(L):
            for ti, i in enumerate(tiles):
                cur_tile[0] = i
                layer(l, XS[ti], BXS[ti], HBS[ti], BHB[ti])
                if l == L - 1:
                    finalize(i, XS[ti], BXS[ti])
                    if i + PAIR < NT:
                        load_x(i + PAIR, XS[ti], BXS[ti])
    S.add("sp", lambda e: e.nop(), reads=[B_out])
    S.emit(window=int(os.environ.get("MK_WINDOW", "96")))
    print("[mk] simulated schedule time (us):", getattr(S, "sim_time", None), "ops:", len(S.ops),
          "busy:", {k: round(v) for k, v in getattr(S, "sim_busy", {}).items()})
    if hasattr(S, "sim_gaps"):
        ntl = max(1, NT * L)
        print("[mk] PE idle-by-phase us/tile-layer:", {k: round(v / ntl, 1) for k, v in S.sim_gaps.items()})
        print("[mk] PE busy-by-phase us/tile-layer:", {k: round(v / ntl, 1) for k, v in S.sim_pebusy.items()})
    return nc


def prep_weights(inp, L):
    f = lambda a: np.asarray(a, dtype=np.float32)
    VO, NV = vec_layout(L)
    wbig = np.zeros((L, NG, 128, 4096), np.float32)
    wa = np.zeros((L, 128, 128), np.float32)
    wal = np.zeros((L, 17, 512), np.float32)
    vec = np.zeros((128, NV), np.float32)

    def grp(W, c0, ncols=512):
        blk = W[:, c0:c0 + ncols].reshape(8, 128, ncols).transpose(1, 0, 2)
        return blk.reshape(128, 8 * ncols)

    w_in = f(inp["w_in"])
    for l in range(L):
        Wl = w_in[l]
        oq, ok, ov, og, oa, oxr, ogr, oga, ogb = 0, 512, 1024, 2048, 3072, 3088, 4112, 5136, 6160
        G = {}
        G["k"] = grp(Wl, ok)
        G["v0"] = grp(Wl, ov); G["v1"] = grp(Wl, ov + 512)
        G["q"] = grp(Wl, oq)
        G["g0"] = grp(Wl, og); G["g1"] = grp(Wl, og + 512)
        G["ga0"] = grp(Wl, oga); G["ga1"] = grp(Wl, oga + 512)
        G["xr0"] = grp(Wl, oxr); G["xr1"] = grp(Wl, oxr + 512)
        G["gr0"] = grp(Wl, ogr); G["gr1"] = grp(Wl, ogr + 512)
        G["gb0"] = grp(Wl, ogb); G["gb1"] = grp(Wl, ogb + 512)
        for nm, key in (("og", "w_out_gla"), ("ol", "w_out_lru"), ("wo", "w_o")):
            W = f(inp[key])[l]
            G[nm + "0"] = grp(W, 0); G[nm + "1"] = grp(W, 512)
        blk = np.zeros((128, 16, 128), np.float32)
        for ti, key in enumerate(("lru_w_a", "lru_w_x")):
            W = f(inp[key])[l]
            for c in range(8):
                for s in range(2):
                    blk[s * 64:(s + 1) * 64, ti * 8 + c, s * 64:(s + 1) * 64] = W[2 * c + s]
        gb = np.zeros((128, 4096), np.float32)
        gb[:, 0:2048] = blk.reshape(128, 2048)
        G["blk"] = gb
        Wup = f(inp["w_up"])[l]
        for j in range(12):
            cols = np.concatenate([np.arange(c0, c0 + 128) for c0 in
                                   (2 * j * 128, 3072 + 2 * j * 128, (2 * j + 1) * 128, 3072 + (2 * j + 1) * 128)])
            G["up%d" % j] = grp(Wup[:, cols], 0)
        Wdn = f(inp["w_down"])[l]
        for cb in range(2):
            for t in range(3):
                G["dn%d" % (cb * 3 + t)] = grp(Wdn[t * 1024:(t + 1) * 1024, :], cb * 512)
        for gi, nm in enumerate(GROUPS):
            wbig[l, gi] = G[nm]
        wa[l] = Wl[:, oa:oa + 16].reshape(8, 128, 16).transpose(1, 0, 2).reshape(128, 128)
        wal[l, 0:16] = f(inp["w_alpha"])[l]
        wal[l, 16] = f(inp["b_alpha"])[l]

        def fm(v, n):
            return v.reshape(n, 128).T
        vec[:, VO["g1"] + l * 8: VO["g1"] + l * 8 + 8] = fm(f(inp["norm_mix"])[l], 8)
        vec[:, VO["g2"] + l * 8: VO["g2"] + l * 8 + 8] = fm(f(inp["norm_ffn"])[l], 8)
        vec[:, VO["gn"] + l * 2: VO["gn"] + l * 2 + 2] = fm(f(inp["gla_norm"])[l], 2)
        for j in range(4):
            o = VO["lcw"] + (l * 4 + j) * 8
            vec[:, o:o + 8] = fm(f(inp["lru_conv_w"])[l, j], 8)
        vec[:, VO["lcb"] + l * 8: VO["lcb"] + l * 8 + 8] = fm(f(inp["lru_conv_b"])[l], 8)
        vec[:, VO["ba"] + l * 8: VO["ba"] + l * 8 + 8] = fm(f(inp["lru_b_a"])[l], 8)
        vec[:, VO["bx"] + l * 8: VO["bx"] + l * 8 + 8] = fm(f(inp["lru_b_x"])[l], 8)
        vec[:, VO["lam"] + l * 8: VO["lam"] + l * 8 + 8] = fm(f(inp["lru_lambda"])[l], 8)
        perm = np.zeros(48, np.int64)
        for i in range(24):
            perm[2 * i] = i
            perm[2 * i + 1] = 24 + i
        for j in range(3):
            o = VO["fcw"] + (l * 3 + j) * 48
            vec[:, o:o + 48] = fm(f(inp["ffn_conv_w"])[l, j], 48)[:, perm]
        vec[:, VO["fcb"] + l * 48: VO["fcb"] + l * 48 + 48] = fm(f(inp["ffn_conv_b"])[l], 48)[:, perm]
    vec[:, VO["gf"]:VO["gf"] + 8] = f(inp["norm_final"]).reshape(8, 128).T
    cst = np.zeros((128, 130), np.float32)
    s = np.arange(128)
    cst[:, 0:128] = ((s[:, None] > s[None, :]) & (s[:, None] // 64 == s[None, :] // 64)).astype(np.float32)
    cst[:, 128] = (s // 64 == 0)
    cst[:, 129] = (s // 64 == 1)
    return dict(wbig=wbig, wa=wa, wal=wal, vec=vec, cst=cst)


_CACHE = {}


def run(inp, T, L, ncores, trace=False):
    key = (T, L)
    if key not in _CACHE:
        _CACHE[key] = build(T, L)
    nc = _CACHE[key]
    shared = prep_weights(inp, L)
    x = np.asarray(inp["x"], dtype=np.float32)
    in_maps = []
    for c in range(ncores):
        m = dict(shared)
        m["xT"] = np.ascontiguousarray(x[c, :T, :].T)
        in_maps.append(m)
    res = run_bass_kernel_spmd(nc, in_maps, core_ids=list(range(ncores)), trace=trace)
    out = np.stack([np.ascontiguousarray(r["yT"].T) for r in res.results], axis=0)
    return out, res


def kernel(**inputs):
    out, _ = run(inputs, 4096, 4, 8)
    return out.astype(np.float32)
```

```python
import contextlib
import os
import numpy as np
import concourse.bass as bass
import concourse.mybir as mybir
from concourse.bass_utils import run_bass_kernel_spmd

F32 = mybir.dt.float32
BF16 = mybir.dt.bfloat16
AF = mybir.ActivationFunctionType
ALU = mybir.AluOpType

D = 1024
KC = 8
TT = 512
NH = 4
EPS = 1e-6
NG = 39
GROUPS = ["k", "v0", "v1", "q", "g0", "g1", "xr0", "blk", "og0", "ga0", "gr0", "xr1", "og1", "ga1", "gr1",
          "ol0", "gb0", "ol1", "gb1", "wo0", "wo1"] + ["up%d" % j for j in range(12)] + ["dn%d" % j for j in range(6)]
assert len(GROUPS) == NG
NSLOT = 6


class Buf:
    __slots__ = ("name", "lw", "rd", "aliases", "rng", "excl")

    def __init__(self, name, rng=None, excl=False):
        self.excl = excl
        self.name = name
        self.lw = None
        self.rd = []
        self.aliases = []
        self.rng = rng


class Op:
    __slots__ = ("id", "eng", "fn", "deps", "dma", "needs_inc", "count", "sem", "val", "prev_same_sem", "cost", "tbl", "phase", "xfer", "rdeps", "vc")

    def __init__(self, id, eng, fn, deps, dma, cost=None, tbl=0):
        self.cost = cost
        self.tbl = tbl
        self.id = id
        self.eng = eng
        self.fn = fn
        self.deps = deps
        self.dma = dma
        self.needs_inc = False
        self.count = 0
        self.sem = None
        self.val = 0
        self.prev_same_sem = None


class _Rec:
    def __init__(self):
        self.func = None
        self.n = None
        self.meth = None

    def __getattr__(self, name):
        def f(*a, **k):
            self.meth = name
            if "func" in k:
                self.func = k["func"]
            o = k.get("out")
            if o is not None:
                try:
                    n = 1
                    for st_, cnt in list(o.ap)[1:]:
                        n *= cnt
                    self.n = n
                except Exception:
                    pass
            return None
        return f


def _tbl_of(func):
    if func in (AF.Exp, AF.Ln):
        return 1
    if func == AF.Sigmoid:
        return 2
    if func == AF.Gelu_apprx_tanh:
        return 3
    if func == AF.Silu:
        return 4
    return 0


class Sched:
    ENGS = ("pe", "act", "dve", "pool", "sp")
    NDMA_SEM = 12

    def __init__(self, nc):
        self.nc = nc
        self.ops = []
        self.by_eng = {e: [] for e in self.ENGS}

    DEFCOST = {"pe": 2.2, "act": 0.65, "dve": 0.78, "pool": 0.8, "sp": 4.0}

    def add(self, eng, fn, reads=(), writes=(), dma=False, cost=None, tbl=0, xfer=0.0):
        if eng in ("act", "dve", "pool") and not dma:
            rec = _Rec()
            fn(rec)
            if eng == "act":
                tbl = _tbl_of(rec.func)
            if cost is None and rec.n is not None:
                if eng == "act":
                    cost = 0.28 + rec.n / 1200.0
                elif eng == "dve":
                    cost = 0.25 + rec.n / 960.0 * (2.0 if rec.meth == "tensor_tensor_scan" else 1.0)
                else:
                    cost = 0.2 + rec.n / 480.0
        if cost is None:
            cost = self.DEFCOST[eng]
        deps = set()
        for b in reads:
            if b.lw is not None:
                deps.add(b.lw)
            if b.excl:
                deps.update(o for o in b.rd if self.ops[o].eng != eng)
            for a in b.aliases:
                if a.lw is not None:
                    deps.add(a.lw)
        rdeps = set(deps)
        for b in writes:
            if b.lw is not None:
                deps.add(b.lw)
            deps.update(b.rd)
            for a in b.aliases:
                if a.lw is not None:
                    deps.add(a.lw)
                deps.update(a.rd)
        op = Op(len(self.ops), eng, fn, deps, dma, cost, tbl)
        op.rdeps = rdeps
        op.phase = getattr(self, "cur_phase", "")
        op.xfer = xfer
        if dma:
            op.cost = 0.3 if eng == "sp" else 1.0
        self.ops.append(op)
        self.by_eng[eng].append(op)
        for b in writes:
            b.lw = op.id
            b.rd = []
            for a in b.aliases:
                a.lw = op.id
                a.rd = []
        for b in reads:
            if b.lw != op.id:
                b.rd.append(op.id)
        return op

    def schedule(self, window):
        ops = self.ops
        n = len(ops)
        fin = [None] * n
        pend = {e: [op.id for op in self.by_eng[e]] for e in self.ENGS}
        free = {e: 0.0 for e in self.ENGS}
        new = {e: [] for e in self.ENGS}
        win = {"pe": window, "act": window, "dve": window, "pool": 1, "sp": 1}
        cur_tbl = [0]
        CP = int(os.environ.get("MK_CP", "1"))
        bl = [0.0] * n
        if CP:
            succ = [[] for _ in range(n)]
            for op in ops:
                for d in op.deps:
                    succ[d].append(op.id)
            for i in range(n - 1, -1, -1):
                m_ = 0.0
                for s_ in succ[i]:
                    if bl[s_] > m_:
                        m_ = bl[s_]
                bl[i] = m_ + ops[i].cost + (ops[i].xfer if ops[i].dma else 0.0)
        LAT = 0.25
        dma_free = [0.0]
        remaining = n
        while remaining:
            best = None
            for e in self.ENGS:
                lst = pend[e]
                if not lst:
                    continue
                seen = set()
                cand = None
                cnt = 0
                for oid in lst:
                    if cnt >= win[e]:
                        break
                    cnt += 1
                    op = ops[oid]
                    ok = True
                    rt = 0.0
                    for d in op.deps:
                        f = fin[d]
                        if f is None:
                            ok = False
                            break
                        if ops[d].eng != e:
                            f += LAT
                        if f > rt:
                            rt = f
                    allowed = True
                    if e == "act" and op.tbl != 0:
                        if seen and seen != {op.tbl}:
                            allowed = False
                        seen.add(op.tbl)
                    if ok and allowed:
                        stt = max(free[e], rt)
                        if e == "act" and op.tbl != 0 and op.tbl != cur_tbl[0]:
                            stt += 2.7
                        if CP and e in ("pe", "act", "dve"):
                            key_ = (max(stt, free[e]), -bl[oid])
                            if cand is None or key_ < (max(cand[0], free[e]), -bl[cand[1]]):
                                cand = (stt, oid)
                        else:
                            if cand is None or stt < cand[0] - 1e-9:
                                cand = (stt, oid)
                            if stt <= free[e] + 1e-9:
                                break
                if cand is not None and (best is None or cand[0] < best[0]):
                    best = (cand[0], cand[1], e)
            assert best is not None, "schedule deadlock"
            stt, oid, e = best
            op = ops[oid]
            if op.dma:
                free[e] = stt + op.cost
                t0_ = max(stt + op.cost, dma_free[0])
                dma_free[0] = t0_ + op.xfer
                fin[oid] = dma_free[0] + 2.0
            else:
                fin[oid] = stt + op.cost
                free[e] = fin[oid]
            if e == "act" and op.tbl != 0:
                cur_tbl[0] = op.tbl
            pend[e].remove(oid)
            new[e].append(op)
            remaining -= 1
        self.by_eng = new
        self.sim_time = max(free.values())
        self.sim_busy = {e: sum(op.cost for op in new[e]) for e in self.ENGS}
        self.sim_fin = fin
        import collections
        gaps = collections.Counter()
        busy = collections.Counter()
        for e in ("pe",):
            prev = 0.0
            for op in new[e]:
                stt = fin[op.id] - op.cost
                if stt > prev + 1e-9:
                    gaps[op.phase] += stt - prev
                    if stt > self.sim_time / 2:
                        gaps["late:" + op.phase] += stt - prev
                busy[op.phase] += op.cost
                prev = fin[op.id]
        self.sim_gaps = gaps
        self.sim_pebusy = busy

    def emit(self, window=0):
        nc = self.nc
        ops = self.ops
        if window > 1:
            self.schedule(window)
        for op in ops:
            for d in op.deps:
                Dd = ops[d]
                if Dd.dma:
                    continue
                if Dd.eng == "pe" and op.eng == "pe" and not op.dma:
                    continue
                Dd.needs_inc = True
        for e in self.ENGS:
            c = 0
            for op in self.by_eng[e]:
                if op.dma:
                    continue
                if op.needs_inc:
                    c += 1
                op.count = c
        CE = ("pe", "act", "dve", "pool")
        for op in ops:
            vc = {e: 0 for e in CE}
            for d in op.deps:
                dv = ops[d].vc
                for e in CE:
                    if dv[e] > vc[e]:
                        vc[e] = dv[e]
            if not op.dma and op.needs_inc:
                vc[op.eng] = max(vc[op.eng], op.count)
            op.vc = vc
        st = contextlib.ExitStack()
        with st:
            esem = {e: st.enter_context(nc.semaphore("sem_" + e)) for e in ("pe", "act", "dve", "pool")}
            for q in ("sp", "pool", "act"):
                nd = sum(1 for op in self.by_eng[q] if op.dma)
                if nd == 0:
                    continue
                k = min(self.NDMA_SEM, nd)
                sems = [st.enter_context(nc.semaphore("dsem_%s_%d" % (q, i))) for i in range(k)]
                n = 0
                lastop = {}
                for op in self.by_eng[q]:
                    if not op.dma:
                        continue
                    op.sem = sems[n % k]
                    op.val = 16 * (n // k + 1)
                    op.prev_same_sem = lastop.get(n % k)
                    lastop[n % k] = op
                    n += 1
            block = st.enter_context(nc.Block())
            handles = {"pe": block.tensor, "act": block.scalar, "dve": block.vector, "pool": block.gpsimd,
                       "sp": block.sync}

            ATTACH = int(os.environ.get("MK_ATTACH", "1"))

            def make_body(e):
                def body(eng):
                    known = {k: 0 for k in CE}
                    dwaited = {}
                    nstand = [0, 0]
                    for op in self.by_eng[e]:
                        dneed = {}
                        eneed = {}
                        for d in op.deps:
                            Dd = ops[d]
                            if Dd.dma:
                                key = id(Dd.sem)
                                if dneed.get(key, (None, 0))[1] < Dd.val:
                                    dneed[key] = (Dd.sem, Dd.val)
                            else:
                                if Dd.eng == "pe" and e == "pe" and not op.dma:
                                    continue
                                ent = eneed.setdefault(Dd.eng, [0, None, 0])
                                if Dd.count > ent[0]:
                                    ent[0] = Dd.count
                                    ent[1] = Dd
                                if d in op.rdeps and Dd.count > ent[2]:
                                    ent[2] = Dd.count
                        if op.dma and op.prev_same_sem is not None:
                            Pp = op.prev_same_sem
                            key = id(Pp.sem)
                            if dneed.get(key, (None, 0))[1] < Pp.val:
                                dneed[key] = (Pp.sem, Pp.val)
                        for key, (sem, val) in dneed.items():
                            if dwaited.get(key, 0) < val:
                                eng.wait_ge(sem, val)
                                dwaited[key] = val
                        items = [(k, v) for k, v in eneed.items() if v[0] > known[k]]
                        pruned = []
                        for k, v in items:
                            implied = False
                            for k2, v2 in items:
                                if k2 != k and v2[1].vc[k] >= v[0]:
                                    implied = True
                                    break
                            if not implied:
                                pruned.append((k, v))
                        attach = None
                        stand = []
                        if ATTACH and (not op.dma):
                            if e in ("act", "dve", "pool"):
                                if pruned:
                                    attach = pruned.pop()
                                stand = pruned
                            elif e == "pe":
                                for k, v in pruned:
                                    if v[2] > known[k]:
                                        stand.append((k, v))
                                    elif attach is None:
                                        attach = (k, v)
                                    else:
                                        stand.append((k, v))
                            else:
                                stand = pruned
                        else:
                            stand = pruned
                        for k, v in stand:
                            eng.wait_ge(esem[k], v[0])
                            nstand[0] += 1
                        ins = op.fn(eng)
                        first = last = ins
                        if isinstance(ins, tuple):
                            first, last = ins
                        if attach is not None:
                            k, v = attach
                            first._wait_ge(esem[k], v[0])
                            nstand[1] += 1
                            stand = stand + [attach]
                        for k, v in stand:
                            dv = v[1].vc
                            for kk in CE:
                                if dv[kk] > known[kk]:
                                    known[kk] = dv[kk]
                            if v[0] > known[k]:
                                known[k] = v[0]
                        if op.dma:
                            last.then_inc(op.sem, 16)
                        elif op.needs_inc:
                            last.then_inc(esem[e], 1)
                    print("[mk] waits", e, "standalone", nstand[0], "attached", nstand[1])
                return body

            for e in self.ENGS:
                if self.by_eng[e]:
                    handles[e](make_body(e))


def vec_layout(L):
    off = {}
    n = 0
    for name, sz in [("g1", L * 8), ("g2", L * 8), ("gf", 8), ("gn", L * 2), ("lcw", L * 4 * 8), ("lcb", L * 8),
                     ("ba", L * 8), ("bx", L * 8), ("lam", L * 8), ("fcw", L * 3 * 48), ("fcb", L * 48)]:
        off[name] = n
        n += sz
    return off, n


def build(T, L):
    NT = T // TT
    nc = bass.Bass("TRN2", target_bir_lowering=False)
    VO, NV = vec_layout(L)
    xT = nc.dram_tensor("xT", [D, T], F32, kind="ExternalInput").ap()
    wbig = nc.dram_tensor("wbig", [L, NG, 128, 4096], F32, kind="ExternalInput").ap()
    wa_d = nc.dram_tensor("wa", [L, 128, 128], F32, kind="ExternalInput").ap()
    wal_d = nc.dram_tensor("wal", [L, 17, 512], F32, kind="ExternalInput").ap()
    vec_d = nc.dram_tensor("vec", [128, NV], F32, kind="ExternalInput").ap()
    cst_d = nc.dram_tensor("cst", [128, 130], F32, kind="ExternalInput").ap()
    yT = nc.dram_tensor("yT", [D, T], F32, kind="ExternalOutput").ap()
    wsc = nc.dram_tensor("wsc", [L, NG, 128, 4096], BF16).ap()
    xTv = xT.rearrange("(kc p) t -> p kc t", p=128)
    yTv = yT.rearrange("(kc p) t -> p kc t", p=128)

    S = Sched(nc)
    MMC = float(os.environ.get("MK_MMC", "0.225"))
    XF = 1048576 / 340e3
    cur = [16512]
    ranged = []

    def alloc(name, shape, dt, at=None):
        nbytes = int(np.prod(shape[1:])) * (4 if dt == F32 else 2)
        if at is None:
            o = cur[0]
            cur[0] += (nbytes + 31) // 32 * 32
        else:
            o = at
        t = nc.alloc_sbuf_tensor_at(name, list(shape), dt, offset=o)
        return t, (o, o + nbytes)

    vec, _ = alloc("vec", [128, NV], F32)
    c8, _ = alloc("c8", [128, L * 8], F32)
    c16, _ = alloc("c16", [128, L * 8], F32)
    etot, _ = alloc("etot", [128, 32], F32)
    hstate, _ = alloc("hstate", [128, L * 8], F32)
    Tl, _ = alloc("Tl", [128, L, 8, 3], F32)
    fixl, _ = alloc("fixl", [128, L, 8, 3], F32)
    Tf, _ = alloc("Tf", [128, L, 48, 2], F32)
    fixf, _ = alloc("fixf", [128, L, 48, 2], F32)
    ptmp, _ = alloc("ptmp", [128, 2, 48], F32)
    cst, _ = alloc("cst", [128, 130], F32)
    ones_bf, _ = alloc("ones_bf", [128, 128], BF16)
    wal, _ = alloc("wal", [128, L, 512], BF16)
    waS, _ = alloc("waS", [128, L, 128], BF16)
    a_aug, _ = alloc("a_aug", [128, 512], BF16)
    xs0, _ = alloc("xs0", [128, KC, TT], F32)
    xs1, _ = alloc("xs1", [128, KC, TT], F32)
    XS = [xs0, xs1]
    hb0, _ = alloc("hb0", [128, KC, TT], BF16)
    hb1, _ = alloc("hb1", [128, KC, TT], BF16)
    HBS = [hb0, hb1]
    ring, _ = alloc("ring", [128, NSLOT, 4096], BF16)
    Sst, _ = alloc("Sst", [128, L, NH, 256], F32)
    sqb, _ = alloc("sqb", [128, KC, TT], BF16)
    lnv, _ = alloc("lnv", [128, TT], F32)
    u1, rg_u1 = alloc("u1", [128, KC, TT], F32)
    r1 = cur[0]
    R1SZ = 49152
    cur[0] += R1SZ
    assert cur[0] <= 229344, cur[0]

    def ralloc(name, shape, dt, off):
        t, rng = alloc(name, shape, dt, at=r1 + off)
        assert rng[1] <= r1 + R1SZ, (name, rng)
        return t, rng

    ebuf, rg_ebuf = ralloc("ebuf", [128, TT], F32, 0)
    spbuf, rg_sp = ralloc("spbuf", [128, 2, TT], F32, 2048)
    dbuf, rg_dbuf = ralloc("dbuf", [128, TT], F32, 6144)
    kdec, rg_kdec = ralloc("kdec", [128, 4, 512], BF16, 8192)
    vbf, rg_vbf = ralloc("vbf", [128, 4, 1024], BF16, 12288)
    qTb, rg_qT = ralloc("qTb", [128, NH, TT], BF16, 20480)
    Sbf, rg_Sbf = ralloc("Sbf", [128, 2, NH, 256], BF16, 24576)
    osg, rg_osg = ralloc("osg", [128, KC, TT], BF16, 40960)
    tmpf, rg_tmpf = ralloc("tmpf", [128, TT], F32, 38912)
    acc, rg_acc = ralloc("acc", [128, 2, TT], F32, 0)
    xcb, rg_xcb = ralloc("xcb", [128, 2, TT], BF16, 4096)
    rbuf, rg_rbuf = ralloc("rbuf", [128, 4, TT], F32, 6144)
    t1b, rg_t1 = ralloc("t1b", [128, 4, TT], F32, 14336)
    abuf, rg_abuf = ralloc("abuf", [128, 4, TT], F32, 22528)
    hg, rg_hg = ralloc("hg", [128, KC, TT], BF16, 30720)
    sgb, rg_sgb = ralloc("sgb", [128, 2, TT], F32, 6144)
    m2b, rg_m2 = ralloc("m2b", [128, 2, TT], F32, 10240)
    mrg, rg_mrg = ralloc("mrg", [128, KC, TT], BF16, 14336)
    facc, rg_facc = ralloc("facc", [128, 3, 2, TT], F32, 28672)
    gv_lo, rg_gvlo = alloc("gv_lo", [128, 16, TT], BF16, at=rg_u1[0])
    gv_hi, rg_gvhi = ralloc("gv_hi", [128, 8, TT], BF16, 40960)

    def gvs(i):
        return gv_lo[:, i, :] if i < 16 else gv_hi[:, i - 16, :]

    ps = nc.alloc_psum_tensor("ps", [128, 8, 512], F32)

    def mk(name, rng):
        b = Buf(name, rng)
        ranged.append(b)
        return b

    def sub(rng, i, n):
        a, b = rng
        step = (b - a) // n
        return (a + i * step, a + (i + 1) * step)

    B_ps = [Buf("ps%d" % i, excl=True) for i in range(8)]
    B_kvh = [Buf("kv%d" % h) for h in range(4)]
    BXS = [[Buf("xs%d_%d" % (t_, i)) for i in range(KC)] for t_ in range(2)]
    BHB = [[Buf("hb%d_%d" % (t_, i)) for i in range(KC)] for t_ in range(2)]
    B_sqb = [Buf("sqb%d" % i) for i in range(KC)]
    B_lnv = Buf("lnv")
    B_u1 = [mk("u1_%d" % i, sub(rg_u1, i, 8)) for i in range(KC)]
    B_ring = [Buf("ring%d" % i) for i in range(NSLOT)]
    B_wsc = [[Buf("wsc%d_%d" % (l, g)) for g in range(NG)] for l in range(L)]
    B_S = [[Buf("S%d_%d" % (l, h)) for h in range(NH)] for l in range(L)]
    B_etot = Buf("etot")
    B_aaug = Buf("aaug")
    B_hst = [[Buf("hst%d_%d" % (l, c)) for c in range(8)] for l in range(L)]
    B_Tl = [Buf("Tl%d" % l) for l in range(L)]
    B_fixl = [Buf("fixl%d" % l) for l in range(L)]
    B_Tf = [Buf("Tf%d" % l) for l in range(L)]
    B_fixf = [Buf("fixf%d" % l) for l in range(L)]
    B_ptmp = Buf("ptmp")
    B_ebuf = mk("ebuf", rg_ebuf)
    B_sp = [mk("sp%d" % i, sub(rg_sp, i, 2)) for i in range(2)]
    B_dbuf = mk("dbuf", rg_dbuf)
    B_kdec = [mk("kdec%d" % i, sub(rg_kdec, i, 4)) for i in range(4)]
    B_vbf = [mk("vbf%d" % i, sub(rg_vbf, i, 4)) for i in range(4)]
    B_qT = [mk("qT%d" % i, sub(rg_qT, i, 4)) for i in range(4)]
    B_Sbf = [[mk("Sbf%d_%d" % (p_, h), sub(sub(rg_Sbf, p_, 2), h, 4)) for h in range(4)] for p_ in range(2)]
    B_osg = [mk("osg%d" % i, sub(rg_osg, i, 8)) for i in range(8)]
    B_tmpf = mk("tmpf", rg_tmpf)
    B_acc = [mk("acc%d" % i, sub(rg_acc, i, 2)) for i in range(2)]
    B_xcb = [mk("xcb%d" % i, sub(rg_xcb, i, 2)) for i in range(2)]
    B_rbuf = [mk("rbuf%d" % i, sub(rg_rbuf, i, 4)) for i in range(4)]
    B_t1 = [mk("t1_%d" % i, sub(rg_t1, i, 4)) for i in range(4)]
    B_abuf = [mk("abuf%d" % i, sub(rg_abuf, i, 4)) for i in range(4)]
    B_hg = [mk("hg%d" % i, sub(rg_hg, i, 8)) for i in range(8)]
    B_sgb = [mk("sgb%d" % i, sub(rg_sgb, i, 2)) for i in range(2)]
    B_m2 = [mk("m2_%d" % i, sub(rg_m2, i, 2)) for i in range(2)]
    B_mrg = [mk("mrg%d" % i, sub(rg_mrg, i, 8)) for i in range(8)]
    B_facc = [mk("facc%d" % i, sub(rg_facc, i, 3)) for i in range(3)]
    B_gv = [mk("gv%d" % i, sub(rg_gvlo, i, 16) if i < 16 else sub(rg_gvhi, i - 16, 8)) for i in range(24)]
    for i, a in enumerate(ranged):
        for b in ranged[i + 1:]:
            if a.rng[0] < b.rng[1] and b.rng[0] < a.rng[1]:
                a.aliases.append(b)
                b.aliases.append(a)


    class Banks:
        def __init__(self):
            self.pool = list(range(8))
            self.i = 0

        def get(self):
            b = self.pool[self.i % len(self.pool)]
            self.i += 1
            return b

        def set(self, lst):
            self.pool = list(lst)
            self.i = 0

    BK = Banks()

    def vcol(name, idx):
        o = VO[name] + idx
        return vec[:, o:o + 1]

    B_setup = []

    def sadd(eng, fn, dma=False, extra_w=(), reads=()):
        b = Buf("setup%d" % len(B_setup))
        B_setup.append(b)
        S.add(eng, fn, reads=reads, writes=[b] + list(extra_w), dma=dma)
        return b

    b_vec = sadd("sp", lambda e: e.dma_start(out=vec[:, :], in_=vec_d[:, :]), dma=True)
    sadd("sp", lambda e: e.dma_start(out=cst[:, :], in_=cst_d[:, :]), dma=True)
    sadd("pool", lambda e: e.memset(ones_bf[:, :], 1.0))
    sadd("pool", lambda e: e.memset(a_aug[:, :], 1.0), extra_w=[B_aaug])
    sadd("pool", lambda e: e.memset(Sst[:, :, :, :], 0.0), extra_w=[b for l in range(L) for b in B_S[l]])
    sadd("pool", lambda e: e.memset(hstate[:, :], 0.0), extra_w=[b for l in range(L) for b in B_hst[l]])
    sadd("pool", lambda e: e.memset(Tl[:, :, :, :], 0.0), extra_w=B_Tl)
    sadd("pool", lambda e: e.memset(Tf[:, :, :, :], 0.0), extra_w=B_Tf)
    for l in range(L):
        sadd("pool", lambda e, l=l: e.dma_start(out=wal[0:17, l, :], in_=wal_d[l, :, :]), dma=True)
        sadd("pool", lambda e, l=l: e.dma_start(out=waS[:, l, :], in_=wa_d[l, :, :]), dma=True)
    lamv = vec[:, VO["lam"]:VO["lam"] + L * 8]
    b_c8a = sadd("act", lambda e: e.activation(out=c8[:, :], in_=lamv, func=AF.Exp, scale=-1.0), reads=[b_vec])
    b_c8b = sadd("act", lambda e: e.activation(out=c8[:, :], in_=c8[:, :], func=AF.Ln, bias=1.0), reads=[b_c8a])
    b_c16 = sadd("act", lambda e: e.mul(out=c16[:, :], in_=c8[:, :], mul=-16.0), reads=[b_c8b])
    sadd("act", lambda e: e.mul(out=c8[:, :], in_=c8[:, :], mul=-8.0), reads=[b_c8b, b_c16])

    def pool_tt(out, in0, in1, op, reads, writes):
        S.add("pool", lambda e: e.tensor_tensor(out=out, in0=in0, in1=in1, op=op), reads=reads, writes=writes)

    def compute_fix_f(l, extra_reads=()):
        w0 = vec[:, VO["fcw"] + (l * 3 + 0) * 48: VO["fcw"] + (l * 3 + 0) * 48 + 48]
        w1 = vec[:, VO["fcw"] + (l * 3 + 1) * 48: VO["fcw"] + (l * 3 + 1) * 48 + 48]
        bb = vec[:, VO["fcb"] + l * 48: VO["fcb"] + l * 48 + 48]
        T0 = Tf[:, l, :, 0]
        T1 = Tf[:, l, :, 1]
        rd = [B_Tf[l]] + list(extra_reads)
        pool_tt(ptmp[:, 0, :], w0, T0, ALU.mult, rd, [B_ptmp])
        pool_tt(ptmp[:, 1, :], w1, T1, ALU.mult, rd + [B_ptmp], [B_ptmp])
        pool_tt(ptmp[:, 0, :], ptmp[:, 0, :], ptmp[:, 1, :], ALU.add, [B_ptmp], [B_ptmp])
        pool_tt(fixf[:, l, :, 0], ptmp[:, 0, :], bb, ALU.add, [B_ptmp], [B_fixf[l]])
        pool_tt(ptmp[:, 1, :], w0, T1, ALU.mult, rd + [B_ptmp], [B_ptmp])
        pool_tt(fixf[:, l, :, 1], ptmp[:, 1, :], bb, ALU.add, [B_ptmp, B_fixf[l]], [B_fixf[l]])

    def compute_fix_l(l, extra_reads=()):
        def w(j):
            o = VO["lcw"] + (l * 4 + j) * 8
            return vec[:, o:o + 8]
        bb = vec[:, VO["lcb"] + l * 8: VO["lcb"] + l * 8 + 8]
        T0, T1, T2 = Tl[:, l, :, 0], Tl[:, l, :, 1], Tl[:, l, :, 2]
        rd = [B_Tl[l]] + list(extra_reads)
        pa, pb = ptmp[:, 0, 0:8], ptmp[:, 1, 0:8]
        pool_tt(pa, w(0), T0, ALU.mult, rd + [B_ptmp], [B_ptmp])
        pool_tt(pb, w(1), T1, ALU.mult, rd + [B_ptmp], [B_ptmp])
        pool_tt(pa, pa, pb, ALU.add, [B_ptmp], [B_ptmp])
        pool_tt(pb, w(2), T2, ALU.mult, rd + [B_ptmp], [B_ptmp])
        pool_tt(pa, pa, pb, ALU.add, [B_ptmp], [B_ptmp])
        pool_tt(fixl[:, l, :, 0], pa, bb, ALU.add, [B_ptmp, B_fixl[l]], [B_fixl[l]])
        pool_tt(pa, w(0), T1, ALU.mult, rd + [B_ptmp], [B_ptmp])
        pool_tt(pb, w(1), T2, ALU.mult, rd + [B_ptmp], [B_ptmp])
        pool_tt(pa, pa, pb, ALU.add, [B_ptmp], [B_ptmp])
        pool_tt(fixl[:, l, :, 1], pa, bb, ALU.add, [B_ptmp, B_fixl[l]], [B_fixl[l]])
        pool_tt(pa, w(0), T2, ALU.mult, rd + [B_ptmp], [B_ptmp])
        pool_tt(fixl[:, l, :, 2], pa, bb, ALU.add, [B_ptmp, B_fixl[l]], [B_fixl[l]])

    for e in ("pe", "act", "dve", "pool", "sp"):
        S.add(e, lambda eng: eng.nop(), reads=list(B_setup))
    for l in range(L):
        compute_fix_f(l)
        compute_fix_l(l)

    def add_cast(l, g):
        def fn(e):
            return e.dma_start(out=wsc[l, g].rearrange("p (a b) -> p a b", b=2048),
                               in_=wbig[l, g].rearrange("p (a b) -> p a b", b=2048))
        S.add("pool", fn, writes=[B_wsc[l][g]], dma=True, xfer=3 * XF)

    for g in range(NG):
        add_cast(0, g)
    cur_tile = [0]
    CAST_TILE = 1 if (int(os.environ.get("MK_PAIR", "2")) >= 2 and NT >= 2) else 0

    wcount = [0]

    def wnext(l, name):
        g = GROUPS.index(name)
        n = wcount[0]
        assert GROUPS[n % NG] == name, (name, GROUPS[n % NG])
        wcount[0] += 1
        s = n % NSLOT
        S.add("sp", lambda e: e.dma_start(out=ring[:, s, :], in_=wsc[l, g]), reads=[B_wsc[l][g]], writes=[B_ring[s]],
              dma=True, xfer=XF)
        if cur_tile[0] == CAST_TILE and l + 1 < L:
            add_cast(l + 1, g)
        return ring[:, s, :].rearrange("p (k c) -> p k c", c=512), B_ring[s], s

    def mm_group(out_ap, pairs, reads, writes, cost=None):
        n = len(pairs)
        if cost is None:
            cost = MMC * n

        def fn(e):
            ins = first = None
            for i, (lt, rh) in enumerate(pairs):
                ins = e.matmul(out_ap, lt, rh, start=(i == 0), stop=(i == n - 1))
                if first is None:
                    first = ins
            return (first, ins)
        S.add("pe", fn, reads=reads, writes=writes, cost=cost)

    def mm_group_ks(out_ap, pairs, reads_each, common_reads, writes):
        n = len(pairs)
        for i, (lt, rh) in enumerate(pairs):
            S.add("pe", lambda e, i=i, lt=lt, rh=rh: e.matmul(out_ap, lt, rh, start=(i == 0), stop=(i == n - 1)),
                  reads=[reads_each[i]] + list(common_reads), writes=writes, cost=MMC)

    def rmsnorm_to_hb(gname, gidx, xs, B_xs):
        for kc in range(KC):
            S.add("act", lambda e, kc=kc: e.activation(out=sqb[:, kc, :], in_=xs[:, kc, :], func=AF.Square),
                  reads=[B_xs[kc]], writes=[B_sqb[kc]])
        bk = BK.get()
        if KSPLIT:
            mm_group_ks(ps[:, bk, :], [(ones_bf[:, :], sqb[:, kc, :]) for kc in range(KC)], list(B_sqb), [], [B_ps[bk]])
        else:
            mm_group(ps[:, bk, :], [(ones_bf[:, :], sqb[:, kc, :]) for kc in range(KC)], list(B_sqb), [B_ps[bk]])
        S.add("act", lambda e: e.activation(out=lnv[:, :], in_=ps[:, bk, :], func=AF.Ln, scale=1.0 / D, bias=EPS),
              reads=[B_ps[bk]], writes=[B_lnv])
        S.add("act", lambda e: e.activation(out=ps[:, bk, :], in_=lnv[:, :], func=AF.Exp, scale=-0.5),
              reads=[B_lnv], writes=[B_ps[bk]])
        return bk

    STOP = int(os.environ.get("MK_STOP", "99"))
    GVENG = os.environ.get("MK_GVENG", "pool")
    XCBENG = os.environ.get("MK_XCBENG", "pool")
    A2ENG = os.environ.get("MK_A2ENG", "pool")
    KSPLIT = int(os.environ.get("MK_KSPLIT", "1"))

    def layer(l, xs, B_xs, hb, B_hb):
        layer_body(l, xs, B_xs, hb, B_hb)
        wcount[0] = (wcount[0] + NG - 1) // NG * NG

    def layer_body(l, xs, B_xs, hb, B_hb):
        S.cur_phase = "norm1"
        BK.set(range(8))
        bk = rmsnorm_to_hb("g1", l, xs, B_xs)
        for kc in range(KC):
            S.add("dve", lambda e, kc=kc, bk=bk: e.scalar_tensor_tensor(
                out=hb[:, kc, :], in0=xs[:, kc, :], scalar=vcol("g1", l * 8 + kc), in1=ps[:, bk, :],
                op0=ALU.mult, op1=ALU.mult), reads=[B_xs[kc], B_ps[bk]], writes=[B_hb[kc]])
        if STOP <= 1:
            return
        S.cur_phase = "prologue"
        BK.set(range(7))
        TOTB = 7
        b0 = BK.get()
        if KSPLIT:
            mm_group_ks(ps[0:16, b0, :], [(waS[:, l, kc * 16:(kc + 1) * 16], hb[:, kc, :]) for kc in range(KC)],
                        list(B_hb), [], [B_ps[b0]])
        else:
            mm_group(ps[0:16, b0, :], [(waS[:, l, kc * 16:(kc + 1) * 16], hb[:, kc, :]) for kc in range(KC)],
                     list(B_hb), [B_ps[b0]])
        S.add("act", lambda e: e.copy(out=a_aug[0:16, :], in_=ps[0:16, b0, :]), reads=[B_ps[b0]], writes=[B_aaug])
        wk, Bwk, _ = wnext(l, "k")
        wv0, Bwv0, _ = wnext(l, "v0")
        wv1, Bwv1, _ = wnext(l, "v1")
        U = cst[:, 0:128]
        Cind = cst[:, 128:130]
        for b in range(4):
            tb = slice(b * 128, (b + 1) * 128)
            sp_i = b % 2
            bz = BK.get()
            mm_group(ps[:, bz, :], [(a_aug[0:17, tb], wal[0:17, l, :])], [B_aaug], [B_ps[bz]])
            S.add("act", lambda e, bz=bz: e.activation(out=ebuf[:, :], in_=ps[:, bz, :], func=AF.Exp, scale=-1.0),
                  reads=[B_ps[bz]], writes=[B_ebuf])
            S.add("act", lambda e, sp_i=sp_i: e.activation(out=spbuf[:, sp_i, :], in_=ebuf[:, :], func=AF.Ln, bias=1.0),
                  reads=[B_ebuf], writes=[B_sp[sp_i]])
            br = BK.get()
            mm_group(ps[:, br, :], [(U, spbuf[:, sp_i, :])], [B_sp[sp_i]], [B_ps[br]], cost=1.1)
            for h in range(NH):
                col = h * 8 + b * 2
                mm_group(ps[:, TOTB, col:col + 2], [(spbuf[:, sp_i, h * 128:(h + 1) * 128], Cind)], [B_sp[sp_i]],
                         [B_ps[TOTB]], cost=0.25)
            S.add("act", lambda e, br=br: e.activation(out=dbuf[:, :], in_=ps[:, br, :], func=AF.Exp, scale=-1.0 / 16),
                  reads=[B_ps[br]], writes=[B_dbuf])
            bkk = BK.get()
            if KSPLIT and b == 0:
                mm_group_ks(ps[:, bkk, :], [(hb[:, kc, tb], wk[:, kc, :]) for kc in range(KC)], list(B_hb), [Bwk],
                            [B_ps[bkk]])
            else:
                mm_group(ps[:, bkk, :], [(hb[:, kc, tb], wk[:, kc, :]) for kc in range(KC)], list(B_hb) + [Bwk],
                         [B_ps[bkk]])
            S.add("dve", lambda e, bkk=bkk, b=b: e.tensor_tensor(out=kdec[:, b, :], in0=ps[:, bkk, :], in1=dbuf[:, :],
                                                               op=ALU.mult),
                  reads=[B_ps[bkk], B_dbuf], writes=[B_kdec[b]])
            for half, (wv, Bwv) in enumerate(((wv0, Bwv0), (wv1, Bwv1))):
                bv = BK.get()
                if KSPLIT and b == 0:
                    mm_group_ks(ps[:, bv, :], [(hb[:, kc, tb], wv[:, kc, :]) for kc in range(KC)], list(B_hb), [Bwv],
                                [B_ps[bv]])
                else:
                    mm_group(ps[:, bv, :], [(hb[:, kc, tb], wv[:, kc, :]) for kc in range(KC)], list(B_hb) + [Bwv],
                             [B_ps[bv]])
                S.add("act", lambda e, bv=bv, b=b, half=half: e.copy(out=vbf[:, b, half * 512:(half + 1) * 512],
                                                                    in_=ps[:, bv, :]),
                      reads=[B_ps[bv]], writes=[B_vbf[b]])
        S.add("act", lambda e: e.activation(out=etot[:, :], in_=ps[:, TOTB, 0:32], func=AF.Exp, scale=-1.0 / 16),
              reads=[B_ps[TOTB]], writes=[B_etot])
        wq, Bwq, _ = wnext(l, "q")
        for h in range(NH):
            bq = BK.get()
            mm_group(ps[:, bq, :], [(wq[:, kc, h * 128:(h + 1) * 128], hb[:, kc, :]) for kc in range(KC)],
                     list(B_hb) + [Bwq], [B_ps[bq]])
            S.add("act", lambda e, bq=bq, h=h: e.mul(out=qTb[:, h, :], in_=ps[:, bq, :], mul=float(128 ** -0.5)),
                  reads=[B_ps[bq]], writes=[B_qT[h]])
        if STOP <= 2:
            return
        S.cur_phase = "recur"
        BK.set([4, 5, 6, 7])
        for half in range(2):
            for cl in range(4):
                c = half * 4 + cl
                b, cc = c // 2, c % 2
                par = c % 2
                prt = slice(cc * 64, (cc + 1) * 64)
                for h in range(NH):
                    kvb, kvc = 4 + h, 0
                    mm_group(ps[:, kvb, kvc:kvc + 256],
                             [(kdec[prt, b, h * 128:(h + 1) * 128], vbf[prt, b, h * 256:(h + 1) * 256])],
                             [B_kdec[b], B_vbf[b]], [B_ps[kvb]], cost=0.2)
                for h in range(NH):
                    kvb, kvc = 4 + h, 0
                    S.add("dve", lambda e, h=h, c=c, kvb=kvb, kvc=kvc: e.scalar_tensor_tensor(
                        out=Sst[:, l, h, :], in0=Sst[:, l, h, :], scalar=etot[:, h * 8 + c:h * 8 + c + 1],
                        in1=ps[:, kvb, kvc:kvc + 256], op0=ALU.mult, op1=ALU.add),
                        reads=[B_S[l][h], B_etot, B_ps[kvb]], writes=[B_S[l][h]])
                    S.add("act", lambda e, h=h, par=par: e.copy(out=Sbf[:, par, h, :], in_=Sst[:, l, h, :]),
                          reads=[B_S[l][h]], writes=[B_Sbf[par][h]])
                for h in range(NH):
                    def fn(e, h=h, cl=cl, c=c, par=par):
                        ins = first = None
                        for j in range(2):
                            ins = e.matmul(ps[:, h, j * 256 + cl * 64: j * 256 + (cl + 1) * 64],
                                           Sbf[:, par, h, j * 128:(j + 1) * 128], qTb[:, h, c * 64:(c + 1) * 64],
                                           start=True, stop=True)
                            if first is None:
                                first = ins
                        return (first, ins)
                    S.add("pe", fn, reads=[B_Sbf[par][h], B_qT[h]], writes=[B_ps[h]], cost=0.2)
            tsl = slice(half * 256, (half + 1) * 256)
            for h in range(NH):
                for j in range(2):
                    S.add("act", lambda e, h=h, j=j, tsl=tsl: e.activation(out=sqb[:, 2 * h + j, tsl],
                                                                in_=ps[:, h, j * 256:(j + 1) * 256], func=AF.Square),
                          reads=[B_ps[h]], writes=[B_sqb[2 * h + j]])
                    S.add("dve", lambda e, h=h, j=j, tsl=tsl: e.tensor_scalar(
                        out=u1[:, 2 * h + j, tsl], in0=ps[:, h, j * 256:(j + 1) * 256],
                        scalar1=vcol("gn", l * 2 + j), scalar2=None, op0=ALU.mult),
                        reads=[B_ps[h]], writes=[B_u1[2 * h + j]])
            for h in range(NH):
                bo = BK.get()
                mm_group(ps[:, bo, 0:256], [(ones_bf[:, :], sqb[:, 2 * h + j, tsl]) for j in range(2)],
                         [B_sqb[2 * h], B_sqb[2 * h + 1]], [B_ps[bo]])
                S.add("act", lambda e, bo=bo: e.activation(out=lnv[:, 0:256], in_=ps[:, bo, 0:256], func=AF.Ln,
                                                          scale=1.0 / 256, bias=EPS),
                      reads=[B_ps[bo]], writes=[B_lnv])
                S.add("act", lambda e, bo=bo: e.activation(out=ps[:, bo, 0:256], in_=lnv[:, 0:256], func=AF.Exp,
                                                          scale=-0.5),
                      reads=[B_lnv], writes=[B_ps[bo]])
                for j in range(2):
                    S.add("dve", lambda e, h=h, j=j, bo=bo, tsl=tsl: e.tensor_tensor(
                        out=u1[:, 2 * h + j, tsl], in0=ps[:, bo, 0:256], in1=u1[:, 2 * h + j, tsl], op=ALU.mult),
                        reads=[B_ps[bo], B_u1[2 * h + j]], writes=[B_u1[2 * h + j]])
        if STOP <= 3:
            return
        S.cur_phase = "gout"
        BK.set(range(8))
        for gi in range(2):
            wg, Bwg, _ = wnext(l, "g%d" % gi)
            for mi in range(4):
                m = gi * 4 + mi
                bg = BK.get()
                mm_group(ps[:, bg, :], [(wg[:, kc, mi * 128:(mi + 1) * 128], hb[:, kc, :]) for kc in range(KC)],
                         list(B_hb) + [Bwg], [B_ps[bg]])
                S.add("act", lambda e, bg=bg: e.activation(out=ps[:, bg, :], in_=ps[:, bg, :], func=AF.Silu),
                      reads=[B_ps[bg]], writes=[B_ps[bg]])
                S.add("dve", lambda e, bg=bg, m=m: e.tensor_tensor(out=osg[:, m, :], in0=ps[:, bg, :], in1=u1[:, m, :],
                                                                 op=ALU.mult),
                      reads=[B_ps[bg], B_u1[m]], writes=[B_osg[m]])
        if STOP <= 4:
            return
        def ya_group(gi):
            S.cur_phase = "ya"
            wog, Bwog, _ = wnext(l, "og%d" % gi)
            wga, Bwga, _ = wnext(l, "ga%d" % gi)
            for mi in range(4):
                m = gi * 4 + mi
                bga = BK.get()
                mm_group(ps[:, bga, :], [(wga[:, kc, mi * 128:(mi + 1) * 128], hb[:, kc, :]) for kc in range(KC)],
                         list(B_hb) + [Bwga], [B_ps[bga]])
                S.add("act", lambda e, bga=bga: e.activation(out=tmpf[:, :], in_=ps[:, bga, :], func=AF.Sigmoid),
                      reads=[B_ps[bga]], writes=[B_tmpf])
                bya = BK.get()
                mm_group(ps[:, bya, :], [(wog[:, kc, mi * 128:(mi + 1) * 128], osg[:, kc, :]) for kc in range(KC)],
                         list(B_osg) + [Bwog], [B_ps[bya]])
                S.add("dve", lambda e, bya=bya, m=m: e.tensor_tensor(out=u1[:, m, :], in0=ps[:, bya, :], in1=tmpf[:, :],
                                                                   op=ALU.mult),
                      reads=[B_ps[bya], B_tmpf], writes=[B_u1[m]])
        if STOP <= 5:
            return
        S.cur_phase = "lru"
        wblk = None
        for hbi in range(2):
            wxr, Bwxr, _ = wnext(l, "xr%d" % hbi)
            if hbi == 0:
                wblk_raw, Bwblk, sblk = wnext(l, "blk")
                wblk = ring[:, sblk, 0:2048].rearrange("p (k c) -> p k c", c=128)
            for cl in range(4):
                c = hbi * 4 + cl
                ai = cl % 2
                bx = BK.get()
                mm_group(ps[:, bx, :], [(wxr[:, kc, cl * 128:(cl + 1) * 128], hb[:, kc, :]) for kc in range(KC)],
                         list(B_hb) + [Bwxr], [B_ps[bx]])
                wl = lambda j, c=c: vcol("lcw", (l * 4 + j) * 8 + c)
                S.add("act", lambda e, bx=bx, ai=ai, c=c, wl=wl: e.activation(
                    out=acc[:, ai, 3:512], in_=ps[:, bx, 0:509], func=AF.Identity, scale=wl(0),
                    bias=vcol("lcb", l * 8 + c)), reads=[B_ps[bx]], writes=[B_acc[ai]])
                S.add("act", lambda e, ai=ai, c=c: e.copy(out=acc[:, ai, 0:3], in_=fixl[:, l, c, :]),
                      reads=[B_fixl[l], B_acc[ai]], writes=[B_acc[ai]])
                S.add("act", lambda e, bx=bx, c=c: e.copy(out=Tl[:, l, c, :], in_=ps[:, bx, 509:512]),
                      reads=[B_ps[bx]], writes=[B_Tl[l]])
                S.add("dve", lambda e, bx=bx, ai=ai, wl=wl: e.scalar_tensor_tensor(
                    out=acc[:, ai, 2:512], in0=ps[:, bx, 0:510], scalar=wl(1), in1=acc[:, ai, 2:512],
                    op0=ALU.mult, op1=ALU.add), reads=[B_ps[bx], B_acc[ai]], writes=[B_acc[ai]])
                S.add("dve", lambda e, bx=bx, ai=ai, wl=wl: e.scalar_tensor_tensor(
                    out=acc[:, ai, 1:512], in0=ps[:, bx, 0:511], scalar=wl(2), in1=acc[:, ai, 1:512],
                    op0=ALU.mult, op1=ALU.add), reads=[B_ps[bx], B_acc[ai]], writes=[B_acc[ai]])
                S.add("dve", lambda e, bx=bx, ai=ai, wl=wl: e.scalar_tensor_tensor(
                    out=acc[:, ai, :], in0=ps[:, bx, :], scalar=wl(3), in1=acc[:, ai, :],
                    op0=ALU.mult, op1=ALU.add), reads=[B_ps[bx], B_acc[ai]], writes=[B_acc[ai]])
                S.add(XCBENG, lambda e, ai=ai: (e.copy(out=xcb[:, ai, :], in_=acc[:, ai, :]) if XCBENG == "act" else
                                               e.tensor_copy(out=xcb[:, ai, :], in_=acc[:, ai, :])),
                      reads=[B_acc[ai]], writes=[B_xcb[ai]])
                bzr = BK.get()
                mm_group(ps[:, bzr, :], [(wblk[:, c, :], xcb[:, ai, :])], [B_xcb[ai], Bwblk], [B_ps[bzr]])
                bzi = BK.get()
                mm_group(ps[:, bzi, :], [(wblk[:, 8 + c, :], xcb[:, ai, :])], [B_xcb[ai], Bwblk], [B_ps[bzi]])
                S.add("act", lambda e, bzr=bzr, cl=cl, c=c: e.activation(
                    out=rbuf[:, cl, :], in_=ps[:, bzr, :], func=AF.Sigmoid, bias=vcol("ba", l * 8 + c)),
                    reads=[B_ps[bzr]], writes=[B_rbuf[cl]])
                S.add("act", lambda e, bzi=bzi, c=c: e.activation(
                    out=ps[:, bzi, :], in_=ps[:, bzi, :], func=AF.Sigmoid, bias=vcol("bx", l * 8 + c)),
                    reads=[B_ps[bzi]], writes=[B_ps[bzi]])
                S.add("dve", lambda e, bzi=bzi, cl=cl, ai=ai: e.tensor_tensor(
                    out=t1b[:, cl, :], in0=ps[:, bzi, :], in1=acc[:, ai, :], op=ALU.mult),
                    reads=[B_ps[bzi], B_acc[ai]], writes=[B_t1[cl]])
            if hbi == 1:
                compute_fix_l(l)
            ya_group(hbi)
            S.cur_phase = "lru"
            for cl in range(4):
                c = hbi * 4 + cl
                S.add("act", lambda e, cl=cl, c=c: e.activation(out=abuf[:, cl, :], in_=rbuf[:, cl, :], func=AF.Exp,
                                                              scale=c8[:, l * 8 + c:l * 8 + c + 1]),
                      reads=[B_rbuf[cl]], writes=[B_abuf[cl]])
                if A2ENG == "act":
                    S.add("act", lambda e, cl=cl, c=c: e.activation(out=rbuf[:, cl, :], in_=rbuf[:, cl, :], func=AF.Exp,
                                                                  scale=c16[:, l * 8 + c:l * 8 + c + 1]),
                          reads=[B_rbuf[cl], B_abuf[cl]], writes=[B_rbuf[cl]])
                else:
                    S.add(A2ENG, lambda e, cl=cl: e.tensor_tensor(out=rbuf[:, cl, :], in0=abuf[:, cl, :],
                                                                  in1=abuf[:, cl, :], op=ALU.mult),
                          reads=[B_rbuf[cl], B_abuf[cl]], writes=[B_rbuf[cl]])
                S.add("act", lambda e, cl=cl: e.activation(out=rbuf[:, cl, :], in_=rbuf[:, cl, :], func=AF.Ln,
                                                         scale=-1.0, bias=1.0),
                      reads=[B_rbuf[cl]], writes=[B_rbuf[cl]])
                bs = BK.get()
                S.add("act", lambda e, cl=cl, bs=bs: e.activation(out=ps[:, bs, :], in_=rbuf[:, cl, :], func=AF.Exp,
                                                                scale=0.5),
                      reads=[B_rbuf[cl]], writes=[B_ps[bs]])
                S.add("dve", lambda e, cl=cl, bs=bs: e.tensor_tensor(out=t1b[:, cl, :], in0=ps[:, bs, :],
                                                                   in1=t1b[:, cl, :], op=ALU.mult),
                      reads=[B_ps[bs], B_t1[cl]], writes=[B_t1[cl]])
                S.add("dve", lambda e, cl=cl, c=c: e.tensor_tensor_scan(
                    out=rbuf[:, cl, :], data0=abuf[:, cl, :], data1=t1b[:, cl, :],
                    initial=hstate[:, l * 8 + c:l * 8 + c + 1], op0=ALU.mult, op1=ALU.add),
                    reads=[B_abuf[cl], B_t1[cl], B_hst[l][c], B_rbuf[cl]], writes=[B_rbuf[cl]])
                S.add("dve", lambda e, cl=cl, c=c: e.tensor_copy(out=hstate[:, l * 8 + c:l * 8 + c + 1],
                                                               in_=rbuf[:, cl, 511:512]),
                      reads=[B_rbuf[cl]], writes=[B_hst[l][c]])
            wgr, Bwgr, _ = wnext(l, "gr%d" % hbi)
            for cl in range(4):
                c = hbi * 4 + cl
                bgr = BK.get()
                mm_group(ps[:, bgr, :], [(wgr[:, kc, cl * 128:(cl + 1) * 128], hb[:, kc, :]) for kc in range(KC)],
                         list(B_hb) + [Bwgr], [B_ps[bgr]])
                S.add("act", lambda e, bgr=bgr: e.activation(out=ps[:, bgr, :], in_=ps[:, bgr, :],
                                                            func=AF.Gelu_apprx_tanh),
                      reads=[B_ps[bgr]], writes=[B_ps[bgr]])
                S.add("dve", lambda e, bgr=bgr, cl=cl, c=c: e.tensor_tensor(out=hg[:, c, :], in0=ps[:, bgr, :],
                                                                          in1=rbuf[:, cl, :], op=ALU.mult),
                      reads=[B_ps[bgr], B_rbuf[cl]], writes=[B_hg[c]])
        if STOP <= 6:
            return
        S.cur_phase = "merge"
        for gi in range(2):
            wol, Bwol, _ = wnext(l, "ol%d" % gi)
            wgb, Bwgb, _ = wnext(l, "gb%d" % gi)
            for mi in range(4):
                m = gi * 4 + mi
                si = m % 2
                bgb = BK.get()
                mm_group(ps[:, bgb, :], [(wgb[:, kc, mi * 128:(mi + 1) * 128], hb[:, kc, :]) for kc in range(KC)],
                         list(B_hb) + [Bwgb], [B_ps[bgb]])
                S.add("act", lambda e, bgb=bgb, si=si: e.activation(out=sgb[:, si, :], in_=ps[:, bgb, :],
                                                                  func=AF.Sigmoid),
                      reads=[B_ps[bgb]], writes=[B_sgb[si]])
                byb = BK.get()
                mm_group(ps[:, byb, :], [(wol[:, kc, mi * 128:(mi + 1) * 128], hg[:, kc, :]) for kc in range(KC)],
                         list(B_hg) + [Bwol], [B_ps[byb]])
                S.add("dve", lambda e, byb=byb, si=si: e.tensor_tensor(out=m2b[:, si, :], in0=ps[:, byb, :],
                                                                     in1=sgb[:, si, :], op=ALU.mult),
                      reads=[B_ps[byb], B_sgb[si]], writes=[B_m2[si]])
                S.add("pool", lambda e, si=si, m=m: e.tensor_tensor(out=mrg[:, m, :], in0=u1[:, m, :], in1=m2b[:, si, :],
                                                                  op=ALU.add),
                      reads=[B_u1[m], B_m2[si]], writes=[B_mrg[m]])
        if STOP <= 7:
            return
        S.cur_phase = "wo"
        for gi in range(2):
            wwo, Bwwo, _ = wnext(l, "wo%d" % gi)
            for mi in range(4):
                m = gi * 4 + mi
                bo = BK.get()
                if KSPLIT and gi == 0:
                    mm_group_ks(ps[:, bo, :], [(wwo[:, kc, mi * 128:(mi + 1) * 128], mrg[:, kc, :]) for kc in range(KC)],
                                list(B_mrg), [Bwwo], [B_ps[bo]])
                else:
                    mm_group(ps[:, bo, :], [(wwo[:, kc, mi * 128:(mi + 1) * 128], mrg[:, kc, :]) for kc in range(KC)],
                             list(B_mrg) + [Bwwo], [B_ps[bo]])
                S.add("dve", lambda e, bo=bo, m=m: e.tensor_tensor(out=xs[:, m, :], in0=ps[:, bo, :], in1=xs[:, m, :],
                                                                 op=ALU.add),
                      reads=[B_ps[bo], B_xs[m]], writes=[B_xs[m]])
        if STOP <= 8:
            return
        S.cur_phase = "norm2"
        bk = rmsnorm_to_hb("g2", l, xs, B_xs)
        for kc in range(KC):
            S.add("dve", lambda e, kc=kc, bk=bk: e.scalar_tensor_tensor(
                out=hb[:, kc, :], in0=xs[:, kc, :], scalar=vcol("g2", l * 8 + kc), in1=ps[:, bk, :],
                op0=ALU.mult, op1=ALU.mult), reads=[B_xs[kc], B_ps[bk]], writes=[B_hb[kc]])
        if STOP <= 9:
            return
        S.cur_phase = "up"
        BK.set(range(8))
        pair_i = 0
        for j in range(12):
            wup, Bwup, _ = wnext(l, "up%d" % j)
            for pi in range(2):
                i = 2 * j + pi
                cv, cg = 2 * i, 2 * i + 1
                bv = (pair_i % 4) * 2
                bg = bv + 1
                fa = pair_i % 3
                pair_i += 1
                for which, bb_ in ((0, bv), (1, bg)):
                    col = (2 * pi + which) * 128
                    if KSPLIT and j == 0:
                        mm_group_ks(ps[:, bb_, :], [(wup[:, kc, col:col + 128], hb[:, kc, :]) for kc in range(KC)],
                                    list(B_hb), [Bwup], [B_ps[bb_]])
                    else:
                        mm_group(ps[:, bb_, :], [(wup[:, kc, col:col + 128], hb[:, kc, :]) for kc in range(KC)],
                                 list(B_hb) + [Bwup], [B_ps[bb_]])
                wf = lambda tap, cc_: vcol("fcw", (l * 3 + tap) * 48 + cc_)
                for which, bb_, cc_ in ((0, bv, cv), (1, bg, cg)):
                    S.add("act", lambda e, which=which, bb_=bb_, cc_=cc_, fa=fa: e.activation(
                        out=facc[:, fa, which, 2:512], in_=ps[:, bb_, 0:510], func=AF.Identity, scale=wf(0, cc_),
                        bias=vcol("fcb", l * 48 + cc_)), reads=[B_ps[bb_]], writes=[B_facc[fa]])
                S.add("act", lambda e, fa=fa, cv=cv: e.copy(out=facc[:, fa, :, 0:2], in_=fixf[:, l, cv:cv + 2, :]),
                      reads=[B_fixf[l], B_facc[fa]], writes=[B_facc[fa]])
                S.add("act", lambda e, bv=bv, cv=cv: e.copy(out=Tf[:, l, cv:cv + 2, :], in_=ps[:, bv:bv + 2, 510:512]),
                      reads=[B_ps[bv], B_ps[bg]], writes=[B_Tf[l]])
                for which, bb_, cc_ in ((0, bv, cv), (1, bg, cg)):
                    S.add("dve", lambda e, which=which, bb_=bb_, cc_=cc_, fa=fa: e.scalar_tensor_tensor(
                        out=facc[:, fa, which, 1:512], in0=ps[:, bb_, 0:511], scalar=wf(1, cc_),
                        in1=facc[:, fa, which, 1:512], op0=ALU.mult, op1=ALU.add),
                        reads=[B_ps[bb_], B_facc[fa]], writes=[B_facc[fa]])
                S.add("dve", lambda e, bv=bv, cv=cv, fa=fa: e.scalar_tensor_tensor(
                    out=facc[:, fa, 0, :], in0=ps[:, bv, :], scalar=wf(2, cv), in1=facc[:, fa, 0, :],
                    op0=ALU.mult, op1=ALU.add), reads=[B_ps[bv], B_facc[fa]], writes=[B_facc[fa]])
                S.add("dve", lambda e, bg=bg, cg=cg, fa=fa: e.scalar_tensor_tensor(
                    out=facc[:, fa, 1, :], in0=ps[:, bg, :], scalar=wf(2, cg), in1=facc[:, fa, 1, :],
                    op0=ALU.mult, op1=ALU.add), reads=[B_ps[bg], B_facc[fa]], writes=[B_facc[fa]])
                S.add("act", lambda e, fa=fa: e.activation(out=facc[:, fa, 1, :], in_=facc[:, fa, 1, :],
                                                          func=AF.Gelu_apprx_tanh),
                      reads=[B_facc[fa]], writes=[B_facc[fa]])
                S.add(GVENG, lambda e, fa=fa, i=i: e.tensor_tensor(out=gvs(i), in0=facc[:, fa, 0, :],
                                                                  in1=facc[:, fa, 1, :], op=ALU.mult),
                      reads=[B_facc[fa]], writes=[B_gv[i]])
        compute_fix_f(l)
        if STOP <= 10:
            return
        S.cur_phase = "down"
        for cb in range(2):
            for t in range(3):
                wdn, Bwdn, _ = wnext(l, "dn%d" % (cb * 3 + t))

                def fn(e, wdn=wdn, t=t, cb=cb):
                    ins = first = None
                    for mi in range(4):
                        for kc in range(KC):
                            ins = e.matmul(ps[:, cb * 4 + mi, :], wdn[:, kc, mi * 128:(mi + 1) * 128], gvs(t * 8 + kc),
                                           start=(t == 0 and kc == 0), stop=(t == 2 and kc == KC - 1))
                            if first is None:
                                first = ins
                    return (first, ins)
                S.add("pe", fn, reads=[B_gv[t * 8 + kc] for kc in range(KC)] + [Bwdn],
                      writes=[B_ps[cb * 4 + mi] for mi in range(4)], cost=7.9)
            for mi in range(4):
                m = cb * 4 + mi
                S.add("dve", lambda e, mi=mi, m=m, cb=cb: e.tensor_tensor(out=xs[:, m, :], in0=ps[:, cb * 4 + mi, :],
                                                                        in1=xs[:, m, :], op=ALU.add),
                      reads=[B_ps[cb * 4 + mi], B_xs[m]], writes=[B_xs[m]])

    B_out = Buf("out_dram")
    def load_x(i, xs, B_xs):
        t0 = i * TT
        S.add("sp", lambda e: e.dma_start(out=xs[:, :, :], in_=xTv[:, :, t0:t0 + TT]), writes=list(B_xs), dma=True,
              xfer=2 * XF)

    def finalize(i, xs, B_xs):
        t0 = i * TT
        S.cur_phase = "final"
        BK.set(range(8))
        bk = rmsnorm_to_hb("gf", 0, xs, B_xs)
        for kc in range(KC):
            S.add("dve", lambda e, kc=kc: e.scalar_tensor_tensor(
                out=u1[:, kc, :], in0=xs[:, kc, :], scalar=vcol("gf", kc), in1=ps[:, bk, :],
                op0=ALU.mult, op1=ALU.mult), reads=[B_xs[kc], B_ps[bk]], writes=[B_u1[kc]])
        S.add("sp", lambda e: e.dma_start(out=yTv[:, :, t0:t0 + TT], in_=u1[:, :, :]), reads=list(B_u1),
              writes=[B_out], dma=True, xfer=2 * XF)

    PAIR = int(os.environ.get("MK_PAIR", "2"))
    for ti in range(min(PAIR, NT)):
        load_x(ti, XS[ti], BXS[ti])
    for j0 in range(0, NT, PAIR):
        tiles = list(range(j0, min(NT, j0 + PAIR)))
        for l in range(L):
            for ti, i in enumerate(tiles):
                cur_tile[0] = i
                layer(l, XS[ti], BXS[ti], HBS[ti], BHB[ti])
                if l == L - 1:
                    finalize(i, XS[ti], BXS[ti])
                    if i + PAIR < NT:
                        load_x(i + PAIR, XS[ti], BXS[ti])
    S.add("sp", lambda e: e.nop(), reads=[B_out])
    S.emit(window=int(os.environ.get("MK_WINDOW", "96")))
    print("[mk] simulated schedule time (us):", getattr(S, "sim_time", None), "ops:", len(S.ops),
          "busy:", {k: round(v) for k, v in getattr(S, "sim_busy", {}).items()})
    if hasattr(S, "sim_gaps"):
        ntl = max(1, NT * L)
        print("[mk] PE idle-by-phase us/tile-layer:", {k: round(v / ntl, 1) for k, v in S.sim_gaps.items()})
        print("[mk] PE busy-by-phase us/tile-layer:", {k: round(v / ntl, 1) for k, v in S.sim_pebusy.items()})
    return nc


def prep_weights(inp, L):
    f = lambda a: np.asarray(a, dtype=np.float32)
    VO, NV = vec_layout(L)
    wbig = np.zeros((L, NG, 128, 4096), np.float32)
    wa = np.zeros((L, 128, 128), np.float32)
    wal = np.zeros((L, 17, 512), np.float32)
    vec = np.zeros((128, NV), np.float32)

    def grp(W, c0, ncols=512):
        blk = W[:, c0:c0 + ncols].reshape(8, 128, ncols).transpose(1, 0, 2)
        return blk.reshape(128, 8 * ncols)

    w_in = f(inp["w_in"])
    for l in range(L):
        Wl = w_in[l]
        oq, ok, ov, og, oa, oxr, ogr, oga, ogb = 0, 512, 1024, 2048, 3072, 3088, 4112, 5136, 6160
        G = {}
        G["k"] = grp(Wl, ok)
        G["v0"] = grp(Wl, ov); G["v1"] = grp(Wl, ov + 512)
        G["q"] = grp(Wl, oq)
        G["g0"] = grp(Wl, og); G["g1"] = grp(Wl, og + 512)
        G["ga0"] = grp(Wl, oga); G["ga1"] = grp(Wl, oga + 512)
        G["xr0"] = grp(Wl, oxr); G["xr1"] = grp(Wl, oxr + 512)
        G["gr0"] = grp(Wl, ogr); G["gr1"] = grp(Wl, ogr + 512)
        G["gb0"] = grp(Wl, ogb); G["gb1"] = grp(Wl, ogb + 512)
        for nm, key in (("og", "w_out_gla"), ("ol", "w_out_lru"), ("wo", "w_o")):
            W = f(inp[key])[l]
            G[nm + "0"] = grp(W, 0); G[nm + "1"] = grp(W, 512)
        blk = np.zeros((128, 16, 128), np.float32)
        for ti, key in enumerate(("lru_w_a", "lru_w_x")):
            W = f(inp[key])[l]
            for c in range(8):
                for s in range(2):
                    blk[s * 64:(s + 1) * 64, ti * 8 + c, s * 64:(s + 1) * 64] = W[2 * c + s]
        gb = np.zeros((128, 4096), np.float32)
        gb[:, 0:2048] = blk.reshape(128, 2048)
        G["blk"] = gb
        Wup = f(inp["w_up"])[l]
        for j in range(12):
            cols = np.concatenate([np.arange(c0, c0 + 128) for c0 in
                                   (2 * j * 128, 3072 + 2 * j * 128, (2 * j + 1) * 128, 3072 + (2 * j + 1) * 128)])
            G["up%d" % j] = grp(Wup[:, cols], 0)
        Wdn = f(inp["w_down"])[l]
        for cb in range(2):
            for t in range(3):
                G["dn%d" % (cb * 3 + t)] = grp(Wdn[t * 1024:(t + 1) * 1024, :], cb * 512)
        for gi, nm in enumerate(GROUPS):
            wbig[l, gi] = G[nm]
        wa[l] = Wl[:, oa:oa + 16].reshape(8, 128, 16).transpose(1, 0, 2).reshape(128, 128)
        wal[l, 0:16] = f(inp["w_alpha"])[l]
        wal[l, 16] = f(inp["b_alpha"])[l]

        def fm(v, n):
            return v.reshape(n, 128).T
        vec[:, VO["g1"] + l * 8: VO["g1"] + l * 8 + 8] = fm(f(inp["norm_mix"])[l], 8)
        vec[:, VO["g2"] + l * 8: VO["g2"] + l * 8 + 8] = fm(f(inp["norm_ffn"])[l], 8)
        vec[:, VO["gn"] + l * 2: VO["gn"] + l * 2 + 2] = fm(f(inp["gla_norm"])[l], 2)
        for j in range(4):
            o = VO["lcw"] + (l * 4 + j) * 8
            vec[:, o:o + 8] = fm(f(inp["lru_conv_w"])[l, j], 8)
        vec[:, VO["lcb"] + l * 8: VO["lcb"] + l * 8 + 8] = fm(f(inp["lru_conv_b"])[l], 8)
        vec[:, VO["ba"] + l * 8: VO["ba"] + l * 8 + 8] = fm(f(inp["lru_b_a"])[l], 8)
        vec[:, VO["bx"] + l * 8: VO["bx"] + l * 8 + 8] = fm(f(inp["lru_b_x"])[l], 8)
        vec[:, VO["lam"] + l * 8: VO["lam"] + l * 8 + 8] = fm(f(inp["lru_lambda"])[l], 8)
        perm = np.zeros(48, np.int64)
        for i in range(24):
            perm[2 * i] = i
            perm[2 * i + 1] = 24 + i
        for j in range(3):
            o = VO["fcw"] + (l * 3 + j) * 48
            vec[:, o:o + 48] = fm(f(inp["ffn_conv_w"])[l, j], 48)[:, perm]
        vec[:, VO["fcb"] + l * 48: VO["fcb"] + l * 48 + 48] = fm(f(inp["ffn_conv_b"])[l], 48)[:, perm]
    vec[:, VO["gf"]:VO["gf"] + 8] = f(inp["norm_final"]).reshape(8, 128).T
    cst = np.zeros((128, 130), np.float32)
    s = np.arange(128)
    cst[:, 0:128] = ((s[:, None] > s[None, :]) & (s[:, None] // 64 == s[None, :] // 64)).astype(np.float32)
    cst[:, 128] = (s // 64 == 0)
    cst[:, 129] = (s // 64 == 1)
    return dict(wbig=wbig, wa=wa, wal=wal, vec=vec, cst=cst)


_CACHE = {}


def run(inp, T, L, ncores, trace=False):
    key = (T, L)
    if key not in _CACHE:
        _CACHE[key] = build(T, L)
    nc = _CACHE[key]
    shared = prep_weights(inp, L)
    x = np.asarray(inp["x"], dtype=np.float32)
    in_maps = []
    for c in range(ncores):
        m = dict(shared)
        m["xT"] = np.ascontiguousarray(x[c, :T, :].T)
        in_maps.append(m)
    res = run_bass_kernel_spmd(nc, in_maps, core_ids=list(range(ncores)), trace=trace)
    out = np.stack([np.ascontiguousarray(r["yT"].T) for r in res.results], axis=0)
    return out, res


def kernel(**inputs):
    out, _ = run(inputs, 4096, 4, 8)
    return out.astype(np.float32)
```
